# Optimizing a Trainium2 kernel written in Bass

```python
import math
import jax, jax.numpy as jnp
from jax import lax
import numpy as np

D_MODEL = 4096
BATCH = 4
SEQ = 4096
DEPTH = 2

CTX_LEN = 256
GRID_W = 64
ROPE_BASE = 10000.0
NORM_EPS = 1e-6
F32 = jnp.float32

M_HEADS = 8
M_DQK = 64
M_DV = 128
M_CHUNK = 64
W_HEADS = 8
W_KV_HEADS = 2
W_DH = 128
WINDOW = 128
DF_HEADS = 8
DF_DQK = 64
DF_DV = 128
Q_BLOCK = 128

BRANCH_W = 1024
N_BRANCH = 3
D_FF = 11008
FFN_CONV_W = 3

IN_SIZES = (
    M_HEADS * M_DQK, M_HEADS * M_DQK, M_HEADS * M_DV, M_HEADS * M_DV, 4 * M_HEADS,
    W_HEADS * W_DH, W_KV_HEADS * W_DH, W_KV_HEADS * W_DH,
    DF_HEADS * 2 * DF_DQK, DF_HEADS * 2 * DF_DQK, DF_HEADS * DF_DV,
    D_MODEL, D_MODEL, D_MODEL,
)
IN_COLS = sum(IN_SIZES)

kernel_name = 'hybrid_mlstm_swa_diffattn_adaln_prefix'


def rms_norm(x, g):
    x32 = x.astype(F32)
    y = x32 * lax.rsqrt(jnp.mean(x32 * x32, axis=-1, keepdims=True) + NORM_EPS)
    return (y * g.astype(F32)).astype(x.dtype)


def modulate(h, shift, scale):
    return h * (1 + scale) + shift


def split_projection(p):
    out = []
    start = 0
    for size in IN_SIZES:
        out.append(p[..., start:start + size])
        start += size
    return out


def axial_rope_tables(n_tokens, head_dim, dtype):
    n_rows = n_tokens // GRID_W
    n_freq = head_dim // 4
    inv = ROPE_BASE ** (-jnp.arange(n_freq, dtype=F32) / n_freq)
    rows = jnp.repeat(jnp.arange(n_rows, dtype=F32), GRID_W)
    cols = jnp.tile(jnp.arange(GRID_W, dtype=F32), n_rows)
    ang = jnp.stack([rows[:, None] * inv, cols[:, None] * inv], axis=1)
    return jnp.cos(ang).astype(dtype), jnp.sin(ang).astype(dtype)


def apply_rope(x, cos, sin):
    shp = x.shape
    xr = x.reshape(shp[:-1] + (2, 2, shp[-1] // 4))
    a, b = xr[..., 0, :], xr[..., 1, :]
    extra = len(shp) - 3
    cos = cos.reshape(cos.shape[:1] + (1,) * extra + cos.shape[1:])
    sin = sin.reshape(sin.shape[:1] + (1,) * extra + sin.shape[1:])
    out = jnp.stack([a * cos - b * sin, b * cos + a * sin], axis=-2)
    return out.reshape(shp)


def mlstm_prep(q, k, v, gates, i_bias, f_bias):
    B, n = q.shape[:2]
    q = q.reshape(B, n, M_HEADS, M_DQK).astype(F32)
    k = k.reshape(B, n, M_HEADS, M_DQK).astype(F32) * (M_DQK ** -0.5)
    v = v.reshape(B, n, M_HEADS, M_DV).astype(F32)
    gates = gates.reshape(B, n, 4, M_HEADS).astype(F32)
    log_i = gates[:, :, 0:2] + i_bias.astype(F32)
    log_f = jax.nn.log_sigmoid(gates[:, :, 2:4] + f_bias.astype(F32))
    return q, k, v, log_i, log_f


def mlstm_chunk_states(k, v, log_i, log_f, state0):
    B, n, H, dk = k.shape
    dv = v.shape[-1]
    nc = n // M_CHUNK
    kc = k.reshape(B, nc, M_CHUNK, H, dk)
    vc = v.reshape(B, nc, M_CHUNK, H, dv)
    bcum = jnp.cumsum(log_f.reshape(B, nc, M_CHUNK, H), axis=2)
    g = bcum[:, :, -1]
    log_a = g[:, :, None] - bcum + log_i.reshape(B, nc, M_CHUNK, H)
    a_max = log_a.max(axis=2)
    w = jnp.exp(log_a - a_max[:, :, None])
    c_loc = jnp.einsum('bclh,bclhv,bclhd->bchvd', w, vc, kc)
    n_loc = jnp.einsum('bclh,bclhd->bchd', w, kc)

    def step(state, inp):
        C, nv, m = state
        g_c, a_c, cl, nl = inp
        m_new = jnp.maximum(g_c + m, a_c)
        dec = jnp.exp(g_c + m - m_new)
        add = jnp.exp(a_c - m_new)
        C_new = dec[..., None, None] * C + add[..., None, None] * cl
        n_new = dec[..., None] * nv + add[..., None] * nl
        return (C_new, n_new, m_new), state

    xs = (jnp.moveaxis(g, 1, 0), jnp.moveaxis(a_max, 1, 0), jnp.moveaxis(c_loc, 1, 0), jnp.moveaxis(n_loc, 1, 0))
    final, prev = lax.scan(step, state0, xs)
    prev = tuple(jnp.moveaxis(p, 0, 1) for p in prev)
    return prev, final


def mlstm_chunk_outputs(q, k, v, log_i, log_f, prev):
    C_prev, n_prev, m_prev = prev
    B, n, H, dk = q.shape
    dv = v.shape[-1]
    L = M_CHUNK
    nc = n // L
    qc = q.reshape(B, nc, L, H, dk)
    kc = k.reshape(B, nc, L, H, dk)
    vc = v.reshape(B, nc, L, H, dv)
    bcum = jnp.cumsum(log_f.reshape(B, nc, L, H), axis=2).transpose(0, 1, 3, 2)
    li = log_i.reshape(B, nc, L, H).transpose(0, 1, 3, 2)
    earlier = jnp.tril(jnp.ones((L, L), dtype=bool))
    log_d = jnp.where(earlier, bcum[..., :, None] - bcum[..., None, :] + li[..., None, :], -jnp.inf)
    log_inter = bcum + m_prev[..., None]
    m = jnp.maximum(log_inter, log_d.max(axis=-1))
    scores = jnp.einsum('bclhd,bcshd->bchls', qc, kc)
    w = jnp.exp(log_d - m[..., None]) * scores
    inter = jnp.exp(log_inter - m)
    num = (jnp.einsum('bchls,bcshv->bclhv', w, vc)
           + jnp.einsum('bchl,bclhd,bchvd->bclhv', inter, qc, C_prev))
    den = w.sum(axis=-1) + inter * jnp.einsum('bclhd,bchd->bchl', qc, n_prev)
    denom = jnp.maximum(jnp.abs(den), jnp.exp(-m))
    h = num / denom.transpose(0, 1, 3, 2)[..., None]
    return h.reshape(B, n, H, dv)


def mlstm_direction(q, k, v, log_i, log_f, state0, with_output):
    prev, final = mlstm_chunk_states(k, v, log_i, log_f, state0)
    h = mlstm_chunk_outputs(q, k, v, log_i, log_f, prev) if with_output else None
    return h, final


def mlstm_bidirectional(lat, ctx, with_ctx_output):
    q, k, v, log_i, log_f = lat
    cq, ck, cv, c_log_i, c_log_f = ctx
    B = q.shape[0]
    zero = (jnp.zeros((B, M_HEADS, M_DV, M_DQK), F32), jnp.zeros((B, M_HEADS, M_DQK), F32),
            jnp.zeros((B, M_HEADS), F32))
    h_lat, h_ctx = [], []
    for d in range(2):
        orient = (lambda a: a) if d == 0 else (lambda a: jnp.flip(a, axis=1))
        hc, state = mlstm_direction(orient(cq), orient(ck), orient(cv), orient(c_log_i[:, :, d]),
                                    orient(c_log_f[:, :, d]), zero, with_ctx_output)
        hl, _ = mlstm_direction(orient(q), orient(k), orient(v), orient(log_i[:, :, d]),
                                orient(log_f[:, :, d]), state, True)
        h_lat.append(orient(hl))
        if with_ctx_output:
            h_ctx.append(orient(hc))
    return h_lat[0] + h_lat[1], (h_ctx[0] + h_ctx[1] if with_ctx_output else None)


def mlstm_finish(h, o_pre, g):
    B, n = h.shape[:2]
    y = h * lax.rsqrt(jnp.mean(h * h, axis=-1, keepdims=True) + NORM_EPS)
    y = y.reshape(B, n, M_HEADS * M_DV) * g.astype(F32)
    return (y * jax.nn.sigmoid(o_pre.astype(F32))).astype(o_pre.dtype)


def window_attention(q, k, v, kc, vc, sink):
    B, S = q.shape[:2]
    nb = S // WINDOW
    G = W_HEADS // W_KV_HEADS
    n_ctx = kc.shape[1]
    qb = q.reshape(B, nb, WINDOW, W_KV_HEADS, G, W_DH)
    pad = ((0, 0), (WINDOW, WINDOW), (0, 0), (0, 0))
    kp = jnp.pad(k, pad).reshape(B, nb + 2, WINDOW, W_KV_HEADS, W_DH)
    vp = jnp.pad(v, pad).reshape(B, nb + 2, WINDOW, W_KV_HEADS, W_DH)
    kband = jnp.concatenate([kp[:, :-2], kp[:, 1:-1], kp[:, 2:]], axis=2)
    vband = jnp.concatenate([vp[:, :-2], vp[:, 1:-1], vp[:, 2:]], axis=2)
    scale = W_DH ** -0.5
    s_loc = jnp.einsum('bnqhgd,bnkhd->bnhgqk', qb, kband).astype(F32) * scale
    qi = jnp.arange(WINDOW)[:, None]
    kj = jnp.arange(3 * WINDOW)[None, :]
    kpos = jnp.arange(nb)[:, None, None] * WINDOW - WINDOW + kj
    valid = (jnp.abs(kj - WINDOW - qi) <= WINDOW)[None] & (kpos >= 0) & (kpos < S)
    s_loc = jnp.where(valid[None, :, None, None], s_loc, -jnp.inf)
    s_ctx = jnp.einsum('bnqhgd,bkhd->bnhgqk', qb, kc).astype(F32) * scale
    sink_col = jnp.broadcast_to(sink.reshape(W_KV_HEADS, G)[:, :, None, None].astype(F32),
                                s_loc.shape[:-1] + (1,))
    p = jax.nn.softmax(jnp.concatenate([s_ctx, s_loc, sink_col], axis=-1), axis=-1).astype(v.dtype)
    out = (jnp.einsum('bnhgqk,bkhd->bnqhgd', p[..., :n_ctx], vc)
           + jnp.einsum('bnhgqk,bnkhd->bnqhgd', p[..., n_ctx:n_ctx + 3 * WINDOW], vband))
    return out.reshape(B, S, W_HEADS * W_DH)


def ctx_sink_attention(q, kc, vc, sink):
    B, n = q.shape[:2]
    G = W_HEADS // W_KV_HEADS
    qg = q.reshape(B, n, W_KV_HEADS, G, W_DH)
    s = jnp.einsum('bqhgd,bkhd->bhgqk', qg, kc).astype(F32) * (W_DH ** -0.5)
    sink_col = jnp.broadcast_to(sink.reshape(W_KV_HEADS, G)[:, :, None, None].astype(F32), s.shape[:-1] + (1,))
    p = jax.nn.softmax(jnp.concatenate([s, sink_col], axis=-1), axis=-1)[..., :-1].astype(vc.dtype)
    return jnp.einsum('bhgqk,bkhd->bqhgd', p, vc).reshape(B, n, W_HEADS * W_DH)


def diff_lambda_value(lp, lambda_init):
    lp = lp.astype(F32)
    return jnp.exp(jnp.sum(lp[0] * lp[1], axis=-1)) - jnp.exp(jnp.sum(lp[2] * lp[3], axis=-1)) + lambda_init


def diff_attend(q, k, v, lam):
    s = jnp.einsum('bqhjd,bkhjd->bhjqk', q, k).astype(F32) * (DF_DQK ** -0.5)
    p = jax.nn.softmax(s, axis=-1)
    a = p[:, :, 0] - lam[None, :, None, None] * p[:, :, 1]
    return jnp.einsum('bhqk,bkhv->bqhv', a.astype(v.dtype), v)


def diff_attention_latent(q, k_all, v_all, lam):
    B, n = q.shape[:2]
    nb = n // Q_BLOCK
    qb = jnp.moveaxis(q.reshape((B, nb, Q_BLOCK) + q.shape[2:]), 1, 0)
    out = lax.map(lambda blk: diff_attend(blk, k_all, v_all, lam), qb)
    return jnp.moveaxis(out, 0, 1).reshape(B, n, DF_HEADS, DF_DV)


def diff_finish(o, g, lambda_init):
    B, n = o.shape[:2]
    o32 = o.astype(F32)
    y = o32 * lax.rsqrt(jnp.mean(o32 * o32, axis=-1, keepdims=True) + NORM_EPS) * g.astype(F32) * (1.0 - lambda_init)
    return y.reshape(B, n, DF_HEADS * DF_DV).astype(o.dtype)


def merge_branches(ys, gates, w_branch, w_out):
    z = sum(jax.nn.sigmoid(g) * (y @ w_branch[i]) for i, (y, g) in enumerate(zip(ys, gates)))
    return z @ w_out


def conv_glu(h, w_up, conv_w, conv_b, w_down):
    u, g = jnp.split(h @ w_up, 2, axis=-1)
    half = FFN_CONV_W // 2
    g = lax.conv_general_dilated(g, conv_w[:, None, :], (1,), ((half, half),),
                                 dimension_numbers=('NWC', 'WIO', 'NWC'),
                                 feature_group_count=g.shape[-1]) + conv_b
    return (jax.nn.silu(g) * u) @ w_down


def trunk_layer(x, xc, s_c, s_ctx, rope_w, rope_d, lambda_init, ctx_out,
                ada_w, ada_b, norm1_g, norm2_g, w_in, i_bias, f_bias, m_norm_g, sink, lam_p, d_norm_g,
                w_branch, w_out, w_up, conv_w, conv_b, w_down):
    B, n = x.shape[:2]
    nc = xc.shape[1]
    sh1, sc1, g1, sh2, sc2, g2 = [m[:, None, :] for m in jnp.split(s_c @ ada_w + ada_b, 6, axis=-1)]
    csh1, csc1, cg1, csh2, csc2, cg2 = jnp.split(s_ctx @ ada_w + ada_b, 6, axis=-1)

    h = modulate(rms_norm(x, norm1_g), sh1, sc1)
    hc = modulate(rms_norm(xc, norm1_g), csh1, csc1)
    mq, mk, mv, mo, mg, wq, wk, wv, fq, fk, fv, gm, gw, gf = split_projection(h @ w_in)
    cmq, cmk, cmv, cmo, cmg, cwq, cwk, cwv, cfq, cfk, cfv, cgm, cgw, cgf = split_projection(hc @ w_in)

    hm, hm_c = mlstm_bidirectional(mlstm_prep(mq, mk, mv, mg, i_bias, f_bias),
                                   mlstm_prep(cmq, cmk, cmv, cmg, i_bias, f_bias), ctx_out)
    y_m = mlstm_finish(hm, mo, m_norm_g)

    q_w = apply_rope(wq.reshape(B, n, W_HEADS, W_DH), *rope_w)
    k_w = apply_rope(wk.reshape(B, n, W_KV_HEADS, W_DH), *rope_w)
    v_w = wv.reshape(B, n, W_KV_HEADS, W_DH)
    kc_w = cwk.reshape(B, nc, W_KV_HEADS, W_DH)
    vc_w = cwv.reshape(B, nc, W_KV_HEADS, W_DH)
    y_w = window_attention(q_w, k_w, v_w, kc_w, vc_w, sink)

    q_f = apply_rope(fq.reshape(B, n, DF_HEADS, 2, DF_DQK), *rope_d)
    k_f = apply_rope(fk.reshape(B, n, DF_HEADS, 2, DF_DQK), *rope_d)
    v_f = fv.reshape(B, n, DF_HEADS, DF_DV)
    kc_f = cfk.reshape(B, nc, DF_HEADS, 2, DF_DQK)
    vc_f = cfv.reshape(B, nc, DF_HEADS, DF_DV)
    lam = diff_lambda_value(lam_p, lambda_init)
    y_f = diff_finish(diff_attention_latent(q_f, jnp.concatenate([kc_f, k_f], axis=1),
                                            jnp.concatenate([vc_f, v_f], axis=1), lam), d_norm_g, lambda_init)

    x = x + g1 * merge_branches((y_m, y_w, y_f), (gm, gw, gf), w_branch, w_out)
    x = x + g2 * conv_glu(modulate(rms_norm(x, norm2_g), sh2, sc2), w_up, conv_w, conv_b, w_down)
    if not ctx_out:
        return x, None

    yc_m = mlstm_finish(hm_c, cmo, m_norm_g)
    yc_w = ctx_sink_attention(cwq.reshape(B, nc, W_HEADS, W_DH), kc_w, vc_w, sink)
    yc_f = diff_finish(diff_attend(cfq.reshape(B, nc, DF_HEADS, 2, DF_DQK), kc_f, vc_f, lam), d_norm_g, lambda_init)
    xc = xc + cg1 * merge_branches((yc_m, yc_w, yc_f), (cgm, cgw, cgf), w_branch, w_out)
    xc = xc + cg2 * conv_glu(modulate(rms_norm(xc, norm2_g), csh2, csc2), w_up, conv_w, conv_b, w_down)
    return x, xc


def setup_inputs(seed: int = 0) -> dict:
    key = jax.random.key(seed)
    ks = jax.random.split(key, 24)
    nrm = jax.random.normal
    D = D_MODEL
    return {
        'x': nrm(ks[0], (BATCH, SEQ, D), F32),
        'c': nrm(ks[1], (BATCH, D), F32),
        'ctx': nrm(ks[2], (BATCH, CTX_LEN, D), F32),
        'c_ctx': nrm(ks[3], (D,), F32),
        'ada_w': nrm(ks[4], (DEPTH, D, 6 * D), F32) * (0.5 * D ** -0.5),
        'ada_b': nrm(ks[5], (DEPTH, 6 * D), F32) * 0.02,
        'norm1_g': 1.0 + 0.05 * nrm(ks[6], (DEPTH, D), F32),
        'norm2_g': 1.0 + 0.05 * nrm(ks[7], (DEPTH, D), F32),
        'w_in': nrm(ks[8], (DEPTH, D, IN_COLS), F32) * D ** -0.5,
        'mlstm_i_bias': 0.1 * nrm(ks[9], (DEPTH, 2, M_HEADS), F32),
        'mlstm_f_bias': jnp.linspace(3.0, 6.0, M_HEADS, dtype=F32)[None, None, :] + 0.1 * nrm(ks[10], (DEPTH, 2, M_HEADS), F32),
        'mlstm_norm_g': 1.0 + 0.05 * nrm(ks[11], (DEPTH, M_HEADS * M_DV), F32),
        'swa_sink': 0.5 * nrm(ks[12], (DEPTH, W_HEADS), F32),
        'diff_lambda': 0.1 * nrm(ks[13], (DEPTH, 4, DF_HEADS, DF_DQK), F32),
        'diff_norm_g': 1.0 + 0.05 * nrm(ks[14], (DEPTH, DF_DV), F32),
        'w_branch': nrm(ks[15], (DEPTH, N_BRANCH, BRANCH_W, D), F32) * BRANCH_W ** -0.5,
        'w_out': nrm(ks[16], (DEPTH, D, D), F32) * D ** -0.5,
        'ffn_w_up': nrm(ks[17], (DEPTH, D, 2 * D_FF), F32) * D ** -0.5,
        'ffn_conv_w': nrm(ks[18], (DEPTH, FFN_CONV_W, D_FF), F32) * FFN_CONV_W ** -0.5,
        'ffn_conv_b': 0.02 * nrm(ks[19], (DEPTH, D_FF), F32),
        'ffn_w_down': nrm(ks[20], (DEPTH, D_FF, D), F32) * D_FF ** -0.5,
        'final_norm_g': 1.0 + 0.05 * nrm(ks[21], (D,), F32),
    }


def reference(x, c, ctx, c_ctx, ada_w, ada_b, norm1_g, norm2_g, w_in, mlstm_i_bias, mlstm_f_bias,
              mlstm_norm_g, swa_sink, diff_lambda, diff_norm_g, w_branch, w_out, ffn_w_up, ffn_conv_w,
              ffn_conv_b, ffn_w_down, final_norm_g):
    n_lat = x.shape[1]
    rope_w = axial_rope_tables(n_lat, W_DH, x.dtype)
    rope_d = axial_rope_tables(n_lat, DF_DQK, x.dtype)
    s_c = jax.nn.silu(c)
    s_ctx = jax.nn.silu(c_ctx)
    xc = ctx
    for l in range(DEPTH):
        lambda_init = 0.8 - 0.6 * math.exp(-0.3 * l)
        x, xc = trunk_layer(x, xc, s_c, s_ctx, rope_w, rope_d, lambda_init, l < DEPTH - 1,
                            ada_w[l], ada_b[l], norm1_g[l], norm2_g[l], w_in[l], mlstm_i_bias[l],
                            mlstm_f_bias[l], mlstm_norm_g[l], swa_sink[l], diff_lambda[l], diff_norm_g[l],
                            w_branch[l], w_out[l], ffn_w_up[l], ffn_conv_w[l], ffn_conv_b[l], ffn_w_down[l])
    return rms_norm(x, final_norm_g)
```

```python
import math
from contextlib import ExitStack
import numpy as np
import concourse.bass as bass
import concourse.mybir as mybir
from concourse.bass_utils import run_bass_kernel_spmd

F32, BF16 = mybir.dt.float32, mybir.dt.bfloat16
ALU = mybir.AluOpType
AF = mybir.ActivationFunctionType
AX = mybir.AxisListType
ENGS = ("tensor", "vector", "scalar", "gpsimd", "sync")
EPS = 1e-6
QUADS = [[0, 1, 2, 3], [4, 5, 6, 7]]
PAIRS = [[0, 4], [1, 5], [2, 6], [3, 7]]


def piece_rows(rows_loc, L):
    pr = 1
    while pr * 2 * L <= 262144 and rows_loc % (pr * 2) == 0:
        pr *= 2
    return pr


class Cfg:
    def __init__(self, D=4096, SEQ=4096, CTX=256, DFF=11008, NL=2):
        self.D, self.SEQ, self.CTX, self.DFF, self.NL = D, SEQ, CTX, DFF, NL
        self.KC, self.FC, self.T = D // 128, DFF // 128, CTX + SEQ
        self.KG = min(8, self.KC)
        self.nF = -(-(35 + 3 * self.KC) // 8) * 8
        self.nB = -(-(3 * self.KC) // 8) * 8
        self.nO = -(-self.KC // 8) * 8
        self.nU = -(-(2 * self.FC) // 8) * 8
        self.nA = -(-(6 * self.KC) // 8) * 8


class Buf:
    __slots__ = ("w", "r", "dsem", "dgen")

    def __init__(self):
        self.w = None
        self.r = {}
        self.dsem = None
        self.dgen = -1


class Prog:
    def __init__(self, nc, stack, n_dma_sems=90):
        self.nc = nc
        self.ops = {e: [] for e in ENGS}
        self.esem = {e: stack.enter_context(nc.semaphore("pe_" + e)) for e in ENGS}
        self.ecnt = {e: 0 for e in ENGS}
        self.waited = {e: {} for e in ENGS}
        self.dma_sems = [stack.enter_context(nc.semaphore(f"dq{i}")) for i in range(n_dma_sems)]
        self.dma_cnt = [0] * n_dma_sems
        self.dma_free = list(range(n_dma_sems))
        self.bgen = 0

    def _need(self, eng, ev, waits):
        sem, val = ev
        k = id(sem)
        if self.waited[eng].get(k, 0) >= val:
            return
        self.waited[eng][k] = val
        waits.append((sem, val))

    def _deps(self, eng, reads, writes):
        skip = self.esem[eng] if eng == "tensor" else None
        need = {}

        def add(ev):
            if ev is None or ev[0] is skip:
                return
            k = id(ev[0])
            if k not in need or need[k][1] < ev[1]:
                need[k] = ev
        for b in reads:
            add(b.w)
        for b in writes:
            add(b.w)
            for ev in b.r.values():
                add(ev)
        waits = []
        for ev in need.values():
            self._need(eng, ev, waits)
        return waits

    def _commit(self, ev, reads, writes):
        k = id(ev[0])
        for b in reads:
            b.r[k] = ev
        for b in writes:
            b.w = ev
            b.r = {}

    def op(self, eng, fn, reads=(), writes=()):
        waits = self._deps(eng, reads, writes)
        self.ecnt[eng] += 1
        ev = (self.esem[eng], self.ecnt[eng])
        self.ops[eng].append((waits, fn, (self.esem[eng], 1)))
        self._commit(ev, reads, writes)

    def dma(self, eng, fn, reads=(), writes=(), slot=None):
        waits = self._deps(eng, reads, writes)
        if slot is None:
            slot = (list(writes) + list(reads))[0]
        if slot.dsem is None or slot.dgen != self.bgen:
            slot.dsem = self.dma_free.pop(0)
            slot.dgen = self.bgen
        i = slot.dsem
        self.dma_cnt[i] += 16
        ev = (self.dma_sems[i], self.dma_cnt[i])
        self.ops[eng].append((waits, fn, (self.dma_sems[i], 16)))
        self._commit(ev, reads, writes)

    def barrier(self):
        evs = [(self.esem[e], self.ecnt[e]) for e in ENGS if self.ecnt[e] > 0]
        evs += [(self.dma_sems[i], c) for i, c in enumerate(self.dma_cnt) if c > 0]
        for e in ENGS:
            waits = []
            for ev in evs:
                self._need(e, ev, waits)
            if waits:
                self.ops[e].append((waits, None, None))
        self.dma_free = list(range(len(self.dma_sems)))
        self.bgen += 1

    def emit(self):
        self.barrier()
        with self.nc.Block() as block:
            for e in ENGS:
                ops = self.ops[e]

                def body(engobj, ops=ops):
                    for waits, fn, inc in ops:
                        for sem, val in waits:
                            engobj.wait_ge(sem, val)
                        if fn is not None:
                            fn(engobj).then_inc(inc[0], inc[1])
                getattr(block, e)(body)
        self.ops = {e: [] for e in ENGS}


class Rot:
    def __init__(self, tiles):
        self.t = tiles
        self.b = [Buf() for _ in tiles]
        self.i = 0

    def next(self):
        k = self.i % len(self.t)
        self.i += 1
        return self.t[k], self.b[k]


def build_program(C):
    nc = bass.Bass("TRN2", target_bir_lowering=False)
    D, KC, FC, T, CTX, SEQ, KG = C.D, C.KC, C.FC, C.T, C.CTX, C.SEQ, C.KG
    SQD = math.sqrt(D)
    uid = [0]

    def din(name, shape, dt=F32):
        return nc.dram_tensor(name, list(shape), dt, kind="ExternalInput").ap()

    def dint(name, shape, dt):
        return nc.dram_tensor(name, list(shape), dt, kind="Internal").ap()

    xin = din("xin", [KC, 128, T])
    cT = din("cT", [128, KC, 5])
    consts = din("consts", [128, 8 * 128])
    ropes = din("ropes", [4, 128, T])
    onehot = din("onehot", [128, 4])
    fng = din("fng", [128, KC])
    W = []
    for l in range(C.NL):
        W.append(dict(
            winF=din(f"winF{l}", [C.nF // 8, 128, KC * 128]),
            winT=din(f"winT{l}", [1, 128, KC * 512]),
            wbr=din(f"wbr{l}", [C.nB // 8, 128, 8 * 128]),
            wout=din(f"wout{l}", [C.nO // 8, 128, KC * 128]),
            wup=din(f"wup{l}", [C.nU // 8, 128, KC * 128]),
            wdn=din(f"wdn{l}", [C.nO // 8, 128, FC * 128]),
            ada=din(f"ada{l}", [C.nA // 8, 128, KC * 128]),
            adab=din(f"adab{l}", [128, C.nA // 8]),
            n1g=din(f"n1g{l}", [128, KC]), n2g=din(f"n2g{l}", [128, KC]),
            convw=din(f"convw{l}", [128, 3, FC]), convb=din(f"convb{l}", [128, FC]),
            gbias=din(f"gbias{l}", [128, 32]), mng=din(f"mng{l}", [128, 1024]),
            sink=din(f"sink{l}", [128, 8]), lam=din(f"lam{l}", [64, 4, 8]),
            dng=din(f"dng{l}", [128, 128]),
        ))
    out = nc.dram_tensor("out", [KC, 128, SEQ], F32, kind="ExternalOutput").ap()

    G = []
    for l in range(C.NL):
        g = {}
        for nm, n, L in (("winF", C.nF, KC * 128), ("winT", 8, KC * 512), ("wbr", C.nB, 1024),
                         ("wout", C.nO, KC * 128), ("wup", C.nU, KC * 128), ("wdn", C.nO, FC * 128)):
            g[nm] = (dint(f"{nm}s{l}", [n // 8 * 128, L], BF16), dint(f"{nm}q{l}", [n // 2 * 128, L], BF16),
                     dint(f"{nm}g{l}", [n * 128, L], BF16), n, L)
        g["mods"] = dint(f"mods{l}", [128, C.nA // 8 * 5], F32)
        g["modq"] = dint(f"modq{l}", [4 * 128, C.nA // 8 * 5], F32)
        g["modg"] = dint(f"modg{l}", [8 * 128, C.nA // 8 * 5], F32)
        G.append(g)
    xT = [xin, dint("xT1", [KC, 128, T], F32), dint("xT2", [KC, 128, T], F32)]
    mqT = dint("mqT", [512, T], BF16)
    mkT = dint("mkT", [512, T], BF16)
    wqT = dint("wqT", [1024, T], BF16)
    wkT = dint("wkT", [256, T], BF16)
    fqT = dint("fqT", [1024, T], BF16)
    fkT = dint("fkT", [1024, T], BF16)
    gatesT = dint("gatesT", [3 * D, T], BF16)
    TM = dint("TM", [T, 4096], BF16)
    MG = dint("MG", [T, 32], F32)
    NCH = T // 64
    CST = dint("CST", [2, NCH, 64, 8 * 129], BF16)
    Y = dint("Y", [T, 3072], BF16)
    actT = dint("actT", [FC * 128, T], BF16)

    with ExitStack() as top:
        P = Prog(nc, top)

        def sb(st, shape, dt):
            uid[0] += 1
            return st.enter_context(nc.sbuf_tensor(f"s{uid[0]}", list(shape), dt))

        def pst(st, shape, dt=F32):
            uid[0] += 1
            return st.enter_context(nc.psum_tensor(f"p{uid[0]}", list(shape), dt))

        cst = sb(top, [128, 8 * 128], F32)
        cbf = sb(top, [128, 8 * 128], BF16)
        Bc = Buf()
        oh = sb(top, [128, 4], F32)
        fngt = sb(top, [128, KC], F32)
        modL = [sb(top, [128, 6, KC], F32) for _ in range(C.NL)]
        modC = [sb(top, [128, 6, KC], F32) for _ in range(C.NL)]
        gef = [sb(top, [128, 4, KC], F32) for _ in range(C.NL)]
        lamt = [sb(top, [128, 16], F32) for _ in range(C.NL)]
        esink = [sb(top, [128, 8], F32) for _ in range(C.NL)]
        Bmod = Buf()
        IDF, TRF, TRB, WML, WMR, ONE, ROW, ROD = [slice(i * 128, (i + 1) * 128) for i in range(8)]

        import os as _os0
        _tog = _os0.environ.get("K_TOG", "")
        epsD = sb(top, [128, 4], F32)
        if "nomemset" not in _tog:
            P.op("vector", lambda e: e.memset(epsD[:, 0:1], EPS * D), writes=[Bc])
            P.op("vector", lambda e: e.memset(epsD[:, 1:2], EPS), writes=[Bc])
            P.op("vector", lambda e: e.memset(epsD[:, 2:3], 1.0), writes=[Bc])
        touch = sb(top, [1, 64], F32)
        Bt_ = Buf()
        _all_in = [cT, ropes, onehot, fng, xin, consts]
        for _l in range(C.NL):
            _all_in += [W[_l][k] for k in W[_l]]
        for _ap in ([] if "notouch" in _tog else _all_in):
            _v = _ap
            while len(_v.shape) > 2:
                _v = _v[0]
            P.dma("sync", lambda e, _v=_v: e.dma_start(out=touch[0:1, 0:2], in_=_v[0:1, 0:2]), writes=[Bt_])
        if "noconst" not in _tog:
            P.dma("sync", lambda e: e.dma_start(out=cst[:], in_=consts[:, :]), writes=[Bc])
            P.dma("sync", lambda e: e.dma_start(out=oh[:], in_=onehot[:, :]), writes=[Bc])
            P.dma("sync", lambda e: e.dma_start(out=fngt[:], in_=fng[:, :]), writes=[Bc])
        if "nocbf" not in _tog:
            P.op("vector", lambda e: e.tensor_copy(out=cbf[:], in_=cst[:]), reads=[Bc], writes=[Bc])

        def phase_weights():
            for l in range(C.NL):
                for nm in ("winF", "winT", "wbr", "wout", "wup", "wdn"):
                    sh, q, g, n, L = G[l][nm]
                    src = W[l][nm]
                    Bs = Buf()
                    for j in range(n // 8):
                        for c0 in range(0, L, 4096):
                            c1 = min(L, c0 + 4096)
                            P.dma("gpsimd", lambda e, j=j, sh=sh, src=src, c0=c0, c1=c1: e.dma_start(
                                out=sh[j * 128:(j + 1) * 128, c0:c1], in_=src[j][:, c0:c1]), writes=[Bs])
                    rows_loc = n // 8 * 128
                    PR = piece_rows(rows_loc, L)
                    for p_ in range(rows_loc // PR):
                        Bq = Buf()
                        P.op("gpsimd", lambda e, sh=sh, q=q, p_=p_, PR=PR: e.collective_compute(
                            "AllGather", ALU.bypass, replica_groups=QUADS, ins=[sh[p_ * PR:(p_ + 1) * PR, :]], outs=[q[p_ * 4 * PR:(p_ + 1) * 4 * PR, :]]),
                            reads=[Bs], writes=[Bq])
                        P.op("gpsimd", lambda e, q=q, g=g, p_=p_, PR=PR: e.collective_compute(
                            "AllGather", ALU.bypass, replica_groups=PAIRS, ins=[q[p_ * 4 * PR:(p_ + 1) * 4 * PR, :]], outs=[g[p_ * 8 * PR:(p_ + 1) * 8 * PR, :]]),
                            reads=[Bq], writes=[Buf()])
            P.emit()

        def phase_ada(l):
            nloc = C.nA // 8
            with ExitStack() as st:
                sT = sb(st, [128, KC, 5], F32)
                e1 = sb(st, [128, KC, 5], F32)
                wt = Rot([sb(st, [128, KC * 128], F32) for _ in range(2)])
                bia = sb(st, [128, nloc], F32)
                msh = sb(st, [128, nloc, 5], F32)
                mall = sb(st, [128, 8, nloc * 5], F32)
                acc = sb(st, [128, 8 * nloc], F32)
                smalls = sb(st, [128, 64], F32)
                lp = sb(st, [64, 4, 8], F32)
                pr = sb(st, [64, 16], F32)
                pp = pst(st, [128, nloc, 8])
                pl = pst(st, [128, 16])
                Bs_, Bb, Bm, Bp, Ba = Buf(), Buf(), Buf(), Buf(), Buf()
                P.dma("sync", lambda e: e.dma_start(out=sT[:], in_=cT[:, :, :]), writes=[Bs_])
                P.dma("sync", lambda e: e.dma_start(out=bia[:], in_=W[l]["adab"][:, :]), writes=[Bb])
                P.op("scalar", lambda e: e.activation(out=e1[:], in_=sT[:], func=AF.Exp, scale=-1.0), reads=[Bs_], writes=[Ba])
                P.op("vector", lambda e: e.tensor_scalar_add(out=e1[:], in0=e1[:], scalar1=1.0), reads=[Ba], writes=[Ba])
                P.op("vector", lambda e: e.reciprocal(out=e1[:], in_=e1[:]), reads=[Ba], writes=[Ba])
                P.op("vector", lambda e: e.tensor_tensor(out=sT[:], in0=sT[:], in1=e1[:], op=ALU.mult), reads=[Ba, Bs_], writes=[Bs_])
                for cb in range(nloc):
                    w, Bw = wt.next()
                    P.dma("sync", lambda e, w=w, cb=cb: e.dma_start(out=w[:], in_=W[l]["ada"][cb]), writes=[Bw])
                    for kc in range(KC):
                        P.op("tensor", lambda e, w=w, cb=cb, kc=kc: e.matmul(
                            pp[:, cb, 0:5], lhsT=w[:, kc * 128:(kc + 1) * 128], rhs=sT[:, kc, :],
                            start=(kc == 0), stop=(kc == KC - 1)), reads=[Bw, Bs_], writes=[Bp])
                P.op("vector", lambda e: e.tensor_tensor(
                    out=msh[:], in0=pp[:, :, 0:5], in1=bia[:].unsqueeze(2).to_broadcast([128, nloc, 5]), op=ALU.add),
                    reads=[Bp, Bb], writes=[Bm])
                gm = G[l]
                Bd1, Bd2, Bd3 = Buf(), Buf(), Buf()
                P.dma("gpsimd", lambda e: e.dma_start(out=gm["mods"][:, :], in_=msh[:].rearrange("p a b -> p (a b)")),
                      reads=[Bm], writes=[Bd1])
                P.op("gpsimd", lambda e: e.collective_compute("AllGather", ALU.bypass, replica_groups=QUADS,
                                                              ins=[gm["mods"][:, :]], outs=[gm["modq"][:, :]]),
                     reads=[Bd1], writes=[Bd2])
                P.op("gpsimd", lambda e: e.collective_compute("AllGather", ALU.bypass, replica_groups=PAIRS,
                                                              ins=[gm["modq"][:, :]], outs=[gm["modg"][:, :]]),
                     reads=[Bd2], writes=[Bd3])
                Bma = Buf()
                P.dma("gpsimd", lambda e: e.dma_start(out=mall[:], in_=gm["modg"].rearrange("(r p) f -> p r f", p=128)),
                      reads=[Bd3], writes=[Bma])
                mv = mall[:].rearrange("p r (c f) -> p r c f", f=5)
                accv = acc[:].rearrange("p (r c) -> p r c", r=8)
                P.op("vector", lambda e: e.tensor_scalar(out=accv, in0=mv[:, :, :, 0], scalar1=oh[:, 0:1], scalar2=None, op0=ALU.mult),
                     reads=[Bma, Bc], writes=[Bmod])
                for r in range(1, 4):
                    P.op("vector", lambda e, r=r: e.scalar_tensor_tensor(out=accv, in0=mv[:, :, :, r], scalar=oh[:, r:r + 1], in1=accv,
                                                                         op0=ALU.mult, op1=ALU.add), reads=[Bma, Bc, Bmod], writes=[Bmod])
                P.op("vector", lambda e: e.tensor_copy(out=modL[l][:].rearrange("p a b -> p (a b)"), in_=acc[:, 0:6 * KC]),
                     reads=[Bmod], writes=[Bmod])
                P.op("vector", lambda e: e.tensor_copy(out=accv, in_=mv[:, :, :, 4]), reads=[Bma, Bmod], writes=[Bmod])
                P.op("vector", lambda e: e.tensor_copy(out=modC[l][:].rearrange("p a b -> p (a b)"), in_=acc[:, 0:6 * KC]),
                     reads=[Bmod], writes=[Bmod])
                ng = sb(st, [128, 2, KC], F32)
                P.dma("sync", lambda e: e.dma_start(out=ng[:, 0, :], in_=W[l]["n1g"][:, :]), writes=[Bb])
                P.dma("sync", lambda e: e.dma_start(out=ng[:, 1, :], in_=W[l]["n2g"][:, :]), writes=[Bb])
                for k, (md, which) in enumerate(((modL[l], 0), (modC[l], 0), (modL[l], 1), (modC[l], 1))):
                    sc_idx = 1 if which == 0 else 4
                    P.op("vector", lambda e, md=md, k=k, sc_idx=sc_idx, which=which: e.scalar_tensor_tensor(
                        out=gef[l][:, k, :], in0=md[:, sc_idx, :], scalar=1.0, in1=ng[:, which, :], op0=ALU.add, op1=ALU.mult),
                        reads=[Bmod, Bb], writes=[Bmod])
                P.op("vector", lambda e: e.tensor_scalar_mul(out=gef[l][:], in0=gef[l][:], scalar1=SQD), reads=[Bmod], writes=[Bmod])
                lam_init = 0.8 - 0.6 * math.exp(-0.3 * l)
                Bl = Buf()
                P.dma("sync", lambda e: e.dma_start(out=lp[:], in_=W[l]["lam"][:, :, :]), writes=[Bl])
                P.op("vector", lambda e: e.tensor_tensor(out=pr[:, 0:8], in0=lp[:, 0, :], in1=lp[:, 1, :], op=ALU.mult), reads=[Bl], writes=[Ba])
                P.op("vector", lambda e: e.tensor_tensor(out=pr[:, 8:16], in0=lp[:, 2, :], in1=lp[:, 3, :], op=ALU.mult), reads=[Bl], writes=[Ba])
                Bpl = Buf()
                P.op("tensor", lambda e: e.matmul(pl[:, :], lhsT=cst[0:64, ONE], rhs=pr[:, :], start=True, stop=True), reads=[Ba, Bc], writes=[Bpl])
                P.op("scalar", lambda e: e.activation(out=smalls[:, 0:16], in_=pl[:, :], func=AF.Exp), reads=[Bpl], writes=[Ba])
                P.op("vector", lambda e: e.tensor_tensor(out=lamt[l][:, 0:8], in0=smalls[:, 0:8], in1=smalls[:, 8:16], op=ALU.subtract), reads=[Ba], writes=[Bmod])
                P.op("vector", lambda e: e.tensor_scalar_add(out=lamt[l][:, 0:8], in0=lamt[l][:, 0:8], scalar1=lam_init), reads=[Bmod], writes=[Bmod])
                P.op("vector", lambda e: e.tensor_scalar_mul(out=lamt[l][:, 8:16], in0=lamt[l][:, 0:8], scalar1=-1.0), reads=[Bmod], writes=[Bmod])
                P.dma("sync", lambda e: e.dma_start(out=smalls[:, 32:40], in_=W[l]["sink"][:, :]), writes=[Bl])
                P.op("scalar", lambda e: e.activation(out=esink[l][:], in_=smalls[:, 32:40], func=AF.Exp), reads=[Bl], writes=[Bmod])
                P.emit()

        def token_groups(include_ctx=True):
            gs = []
            if include_ctx:
                gs.append((0, CTX, True))
            for t0 in range(CTX, T, 512):
                gs.append((t0, min(512, T - t0), False))
            return gs

        def norm_prologue(st, src, cols, hT, Bh, geff, shift, pn, Bpn, xch, sq, rstd, tmp):
            Brs = Buf()
            for (c0, t0, n) in cols:
                for kc in range(KC):
                    x, Bx = xch.next()
                    s, Bs_ = sq.next()
                    P.dma("sync", lambda e, x=x, kc=kc, t0=t0, n=n: e.dma_start(out=x[:, :n], in_=src[kc][:, t0:t0 + n]), writes=[Bx])
                    P.op("scalar", lambda e, x=x, s=s, n=n: e.activation(out=s[:, :n], in_=x[:, :n], func=AF.Square), reads=[Bx], writes=[Bs_])
                    P.op("tensor", lambda e, s=s, n=n, kc=kc, c0=c0: e.matmul(pn[:, c0:c0 + n], lhsT=cbf[:, ONE], rhs=s[:, :n],
                                                                            start=(kc == 0), stop=(kc == KC - 1)), reads=[Bs_, Bc], writes=[Bpn])
                P.op("scalar", lambda e, c0=c0, n=n: e.activation(out=rstd[:, c0:c0 + n], in_=pn[:, c0:c0 + n], func=AF.Ln, bias=epsD[:, 0:1]), reads=[Bpn, Bc], writes=[Brs])
                P.op("scalar", lambda e, c0=c0, n=n: e.activation(out=rstd[:, c0:c0 + n], in_=rstd[:, c0:c0 + n], func=AF.Exp, scale=-0.5), reads=[Brs], writes=[Brs])
                for kc in range(KC):
                    x, Bx = xch.next()
                    tm, Bt = tmp.next()
                    P.dma("sync", lambda e, x=x, kc=kc, t0=t0, n=n: e.dma_start(out=x[:, :n], in_=src[kc][:, t0:t0 + n]), writes=[Bx])
                    P.op("vector", lambda e, x=x, tm=tm, c0=c0, n=n: e.tensor_tensor(out=tm[:, :n], in0=x[:, :n], in1=rstd[:, c0:c0 + n], op=ALU.mult),
                         reads=[Bx, Brs], writes=[Bt])
                    if shift is None:
                        P.op("scalar", lambda e, tm=tm, kc=kc, c0=c0, n=n: e.activation(out=hT[:, kc, c0:c0 + n], in_=tm[:, :n], func=AF.Identity,
                                                                                      scale=geff[:, kc:kc + 1]), reads=[Bt, Bmod], writes=[Bh])
                    else:
                        P.op("scalar", lambda e, tm=tm, kc=kc, c0=c0, n=n: e.activation(out=hT[:, kc, c0:c0 + n], in_=tm[:, :n], func=AF.Identity,
                                                                                      scale=geff[:, kc:kc + 1], bias=shift[:, kc:kc + 1]),
                             reads=[Bt, Bmod], writes=[Bh])

        F_BLOCKS = ([("mq", i) for i in range(4)] + [("mk", i) for i in range(4)] + [("wq", i) for i in range(8)] +
                    [("wk", i) for i in range(2)] + [("fq", i) for i in range(8)] + [("fk", i) for i in range(8)] + [("mg", 0)] +
                    [("g", i) for i in range(3 * KC)])

        def phase_p1(l, src):
            wF = G[l]["winF"][2].rearrange("(n p) f -> n p f", p=128)
            wT = G[l]["winT"][2].rearrange("(n p) f -> n p f", p=128)
            with ExitStack() as st:
                hT = sb(st, [128, KC, 512], BF16)
                Bh = Buf()
                xch = Rot([sb(st, [128, 512], F32) for _ in range(3)])
                sq = Rot([sb(st, [128, 512], BF16) for _ in range(2)])
                tmp = Rot([sb(st, [128, 512], F32) for _ in range(2)])
                rstd = sb(st, [128, 512], F32)
                WF = Rot([sb(st, [128, KC * 128], BF16) for _ in range(3)])
                WTt = Rot([sb(st, [128, KG * 512], BF16) for _ in range(2)])
                ob = Rot([sb(st, [128, 512], BF16) for _ in range(4)])
                of = Rot([sb(st, [128, 512], F32) for _ in range(3)])
                rp = sb(st, [128, 4, 512], F32)
                Brp = Buf()
                pm = Rot([pst(st, [128, 512]) for _ in range(4)])
                prr = Rot([pst(st, [128, 512]) for _ in range(2)])
                pn = pst(st, [128, 512])
                Bpn = Buf()
                for (t0, n, is_ctx) in token_groups():
                    ci = 1 if is_ctx else 0
                    md = modC[l] if is_ctx else modL[l]
                    norm_prologue(st, src, [(0, t0, n)], hT, Bh, gef[l][:, ci, :], md[:, 0, :], pn, Bpn, xch, sq, rstd, tmp)
                    P.dma("sync", lambda e, t0=t0, n=n: e.dma_start(out=rp[:, :, :n], in_=ropes[:, :, t0:t0 + n].rearrange("a p t -> p a t")), writes=[Brp])
                    _p1 = _os0.environ.get("K_P1", "")
                    for bi, (kind, idx) in enumerate(F_BLOCKS):
                        if kind == "mg" or "noF" in _p1:
                            continue
                        if "norope" in _p1 and kind in ("wq", "wk", "fq", "fk"):
                            continue
                        if "nog" in _p1 and kind == "g":
                            continue
                        if "nomq" in _p1 and kind in ("mq", "mk"):
                            continue
                        w, Bw = WF.next()
                        P.dma("sync", lambda e, w=w, bi=bi: e.dma_start(out=w[:], in_=wF[bi]), writes=[Bw])
                        p, Bp = pm.next()
                        for kc in range(KC):
                            P.op("tensor", lambda e, p=p, w=w, kc=kc, n=n: e.matmul(p[:, :n], lhsT=w[:, kc * 128:(kc + 1) * 128], rhs=hT[:, kc, :n],
                                                                                 start=(kc == 0), stop=(kc == KC - 1)), reads=[Bw, Bh], writes=[Bp])
                        o, Bo = ob.next()
                        if kind in ("mq", "mk"):
                            dst = (mqT if kind == "mq" else mkT)[idx * 128:(idx + 1) * 128, t0:t0 + n]
                            scl = 1.0 if kind == "mq" else 0.125
                            P.op("scalar", lambda e, o=o, p=p, n=n, scl=scl: e.activation(out=o[:, :n], in_=p[:, :n], func=AF.Identity, scale=scl), reads=[Bp], writes=[Bo])
                        elif kind == "g":
                            dst = gatesT[idx * 128:(idx + 1) * 128, t0:t0 + n]
                            P.op("scalar", lambda e, o=o, p=p, n=n: e.activation(out=o[:, :n], in_=p[:, :n], func=AF.Sigmoid), reads=[Bp], writes=[Bo])
                        else:
                            dst = {"wq": wqT, "wk": wkT, "fq": fqT, "fk": fkT}[kind][idx * 128:(idx + 1) * 128, t0:t0 + n]
                            ti = 0 if kind in ("wq", "wk") else 2
                            rot = ROW if kind in ("wq", "wk") else ROD
                            xb, Bxb = ob.next()
                            P.op("scalar", lambda e, xb=xb, p=p, n=n: e.activation(out=xb[:, :n], in_=p[:, :n], func=AF.Identity), reads=[Bp], writes=[Bxb])
                            _lvl = int(_os0.environ.get("K_ROPE", "3"))
                            if _lvl == 1:
                                P.dma("sync", lambda e, xb=xb, dst=dst, n=n: e.dma_start(out=dst, in_=xb[:, :n]), reads=[Bxb])
                                continue
                            p2, Bp2 = prr.next()
                            P.op("tensor", lambda e, p2=p2, xb=xb, n=n, rot=rot: e.matmul(p2[:, :n], lhsT=cbf[:, rot], rhs=xb[:, :n], start=True, stop=True),
                                 reads=[Bxb, Bc], writes=[Bp2])
                            if _lvl == 2:
                                P.op("scalar", lambda e, o=o, p2=p2, n=n: e.activation(out=o[:, :n], in_=p2[:, :n], func=AF.Identity), reads=[Bp2], writes=[Bo])
                                P.dma("sync", lambda e, o=o, dst=dst, n=n: e.dma_start(out=dst, in_=o[:, :n]), reads=[Bo])
                                continue
                            f1, Bf1 = of.next()
                            f2, Bf2 = of.next()
                            _sw = _os0.environ.get("K_SWAP", "")
                            if _sw == "swap":
                                P.op("vector", lambda e, f1=f1, p=p, n=n, ti=ti: e.tensor_tensor(out=f1[:, :n], in0=rp[:, ti, :n], in1=p[:, :n], op=ALU.mult),
                                     reads=[Bp, Brp], writes=[Bf1])
                            elif _sw == "sb":
                                P.op("vector", lambda e, f1=f1, p=p, n=n, ti=ti: e.tensor_tensor(out=f1[:, :n], in0=rstd[:, :n], in1=rp[:, ti, :n], op=ALU.mult),
                                     reads=[Bp, Brp], writes=[Bf1])
                            else:
                                P.op("vector", lambda e, f1=f1, p=p, n=n, ti=ti: e.tensor_tensor(out=f1[:, :n], in0=p[:, :n], in1=rp[:, ti, :n], op=ALU.mult),
                                     reads=[Bp, Brp, Bxb], writes=[Bf1])
                            if _lvl == 4:
                                P.op("scalar", lambda e, o=o, f1=f1, n=n: e.activation(out=o[:, :n], in_=f1[:, :n], func=AF.Identity), reads=[Bf1], writes=[Bo])
                                P.dma("sync", lambda e, o=o, dst=dst, n=n: e.dma_start(out=dst, in_=o[:, :n]), reads=[Bo])
                                continue
                            P.op("vector", lambda e, f2=f2, p2=p2, n=n, ti=ti: e.tensor_tensor(out=f2[:, :n], in0=p2[:, :n], in1=rp[:, ti + 1, :n], op=ALU.mult),
                                 reads=[Bp2, Brp], writes=[Bf2])
                            P.op("vector" if "ropevec" in _p1 else "gpsimd", lambda e, o=o, f1=f1, f2=f2, n=n: e.tensor_tensor(out=o[:, :n], in0=f1[:, :n], in1=f2[:, :n], op=ALU.add),
                                 reads=[Bf1, Bf2], writes=[Bo])
                        P.dma("sync", lambda e, o=o, dst=dst, n=n: e.dma_start(out=dst, in_=o[:, :n]), reads=[Bo])
                    nsub = n // 128
                    for blk in range(0 if "noT" not in _p1 else 8, 8):
                        ps_ = [pm.next() for _ in range(nsub)]
                        for kg in range(KC // KG):
                            w, Bw = WTt.next()
                            P.dma("sync", lambda e, w=w, blk=blk, kg=kg: e.dma_start(out=w[:], in_=wT[blk][:, kg * KG * 512:(kg + 1) * KG * 512]), writes=[Bw])
                            for s in range(nsub):
                                p, Bp = ps_[s]
                                for j in range(KG):
                                    kc = kg * KG + j
                                    P.op("tensor", lambda e, p=p, w=w, s=s, j=j, kc=kc: e.matmul(
                                        p[:, :], lhsT=hT[:, kc, s * 128:(s + 1) * 128], rhs=w[:, j * 512:(j + 1) * 512],
                                        start=(kc == 0), stop=(kc == KC - 1)), reads=[Bw, Bh], writes=[Bp])
                        for s in range(nsub):
                            p, Bp = ps_[s]
                            o, Bo = ob.next()
                            tt = t0 + s * 128
                            eng = "scalar" if s % 2 == 0 else "vector"
                            if blk == 5:
                                P.op("vector", lambda e, o=o, p=p: e.tensor_copy(out=o[:, 0:256], in_=p[:, 0:256]), reads=[Bp], writes=[Bo])
                                f1, Bf1 = of.next()
                                P.op("scalar", lambda e, f1=f1, p=p: e.activation(out=f1[:, 0:32], in_=p[:, 256:288], func=AF.Identity), reads=[Bp, Bo], writes=[Bf1])
                                P.dma("sync", lambda e, o=o, tt=tt: e.dma_start(out=TM[tt:tt + 128, 2560:2816], in_=o[:, 0:256]), reads=[Bo])
                                P.dma("sync", lambda e, f1=f1, tt=tt: e.dma_start(out=MG[tt:tt + 128, :], in_=f1[:, 0:32]), reads=[Bf1])
                                continue
                            scl = 0.125 if blk == 0 else 1.0
                            if eng == "scalar":
                                P.op("scalar", lambda e, o=o, p=p, scl=scl: e.activation(out=o[:, :], in_=p[:, :], func=AF.Identity, scale=scl), reads=[Bp], writes=[Bo])
                            else:
                                P.op("vector", lambda e, o=o, p=p, scl=scl: e.tensor_scalar_mul(out=o[:, :], in0=p[:, :], scalar1=scl), reads=[Bp], writes=[Bo])
                            c0 = blk * 512 if blk < 5 else 3072 + (blk - 6) * 512
                            P.dma("sync", lambda e, o=o, tt=tt, c0=c0: e.dma_start(out=TM[tt:tt + 128, c0:c0 + 512], in_=o[:, :]), reads=[Bo])
                P.emit()

        def gate_math(st, mg, Bmg, gb, dirs, pg, Bpg, sm, Bsm):
            P.op("vector", lambda e: e.tensor_tensor(out=sm[:, 0, :], in0=mg[:, 0:16], in1=gb[0:64, 0:16], op=ALU.add), reads=[Bmg, Bc], writes=[Bsm])
            P.op("vector", lambda e: e.tensor_tensor(out=sm[:, 1, :], in0=mg[:, 16:32], in1=gb[0:64, 16:32], op=ALU.add), reads=[Bmg, Bc, Bsm], writes=[Bsm])
            P.op("scalar", lambda e: e.activation(out=sm[:, 1, :], in_=sm[:, 1, :], func=AF.Exp, scale=-1.0), reads=[Bsm], writes=[Bsm])
            P.op("scalar", lambda e: e.activation(out=sm[:, 1, :], in_=sm[:, 1, :], func=AF.Ln, bias=epsD[0:64, 2:3]), reads=[Bsm], writes=[Bsm])
            for d in dirs:
                tri = TRF if d == 0 else TRB
                P.op("tensor", lambda e, d=d, tri=tri: e.matmul(pg[:, d * 8:(d + 1) * 8], lhsT=cst[0:64, tri][:, 0:64], rhs=sm[:, 1, d * 8:(d + 1) * 8],
                                                              start=True, stop=True), reads=[Bsm, Bc], writes=[Bpg])
                P.op("tensor", lambda e, d=d: e.matmul(pg[:, 16 + d * 8:16 + (d + 1) * 8], lhsT=cst[0:64, ONE][:, 0:64], rhs=sm[:, 1, d * 8:(d + 1) * 8],
                                                     start=True, stop=True), reads=[Bsm, Bc], writes=[Bpg])
            lo, hi = min(dirs) * 8, (max(dirs) + 1) * 8
            P.op("vector", lambda e: e.tensor_tensor(out=sm[:, 2, lo:hi], in0=sm[:, 0, lo:hi], in1=pg[:, lo:hi], op=ALU.add), reads=[Bpg, Bsm], writes=[Bsm])
            P.op("vector", lambda e: e.tensor_tensor(out=sm[:, 4, lo:hi], in0=sm[:, 2, lo:hi], in1=pg[:, 16 + lo:16 + hi], op=ALU.subtract), reads=[Bpg, Bsm], writes=[Bsm])
            P.op("scalar", lambda e: e.activation(out=sm[:, 2, lo:hi], in_=sm[:, 2, lo:hi], func=AF.Exp), reads=[Bsm], writes=[Bsm])
            P.op("scalar", lambda e: e.activation(out=sm[:, 4, lo:hi], in_=sm[:, 4, lo:hi], func=AF.Exp), reads=[Bsm], writes=[Bsm])
            P.op("scalar", lambda e: e.activation(out=sm[:, 3, lo:hi], in_=pg[:, lo:hi], func=AF.Exp, scale=-1.0), reads=[Bpg, Bsm], writes=[Bsm])
            P.op("scalar", lambda e: e.activation(out=sm[:, 5, lo:hi], in_=pg[:, 16 + lo:16 + hi], func=AF.Exp, scale=-1.0), reads=[Bpg, Bsm], writes=[Bsm])

        def phase_mstate(l, d):
            ncc = CTX // 64
            order = list(range(NCH)) if d == 0 else (list(range(ncc - 1, -1, -1)) + list(range(NCH - 1, ncc - 1, -1)))
            with ExitStack() as st:
                gb = sb(st, [128, 32], F32)
                P.dma("sync", lambda e: e.dma_start(out=gb[:], in_=W[l]["gbias"][:, :]), writes=[Bc])
                Cs = sb(st, [64, 8, 129], F32)
                BCs = Buf()
                P.op("vector", lambda e: e.memset(Cs[:], 0.0), writes=[BCs])
                mgs = Rot([sb(st, [64, 32], F32) for _ in range(3)])
                ks = Rot([sb(st, [64, 8, 64], BF16) for _ in range(3)])
                v1 = Rot([sb(st, [64, 8, 129], BF16) for _ in range(3)])
                for t_, b_ in zip(v1.t, v1.b):
                    P.op("gpsimd", lambda e, t_=t_: e.memset(t_[:, :, 128:129], 1.0), writes=[b_])
                sms = Rot([sb(st, [64, 6, 16], F32) for _ in range(2)])
                kws = Rot([sb(st, [64, 8, 64], BF16) for _ in range(2)])
                cbs = Rot([sb(st, [64, 8, 129], BF16) for _ in range(2)])
                pgs = Rot([pst(st, [64, 32]) for _ in range(2)])
                pc = pst(st, [64, 4, 512])
                Bpc = Buf()
                for c in order:
                    t0 = c * 64
                    mg, Bmg = mgs.next()
                    k, Bk = ks.next()
                    v, Bv = v1.next()
                    P.dma("sync", lambda e, mg=mg, t0=t0: e.dma_start(out=mg[:], in_=MG[t0:t0 + 64, :]), writes=[Bmg])
                    P.dma("sync", lambda e, k=k, t0=t0: e.dma_start(out=k[:].rearrange("p a b -> p (a b)"), in_=TM[t0:t0 + 64, 0:512]), writes=[Bk])
                    P.dma("sync", lambda e, v=v, t0=t0: e.dma_start(out=v[:, :, 0:128], in_=TM[t0:t0 + 64, 512:1536].rearrange("p (a b) -> p a b", b=128)), writes=[Bv])
                    sm, Bsm = sms.next()
                    pg, Bpg = pgs.next()
                    gate_math(st, mg, Bmg, gb, [d], pg, Bpg, sm, Bsm)
                    kw, Bkw = kws.next()
                    P.op("vector", lambda e, kw=kw, k=k, sm=sm: e.tensor_tensor(out=kw[:], in0=k[:], in1=sm[:, 4, d * 8:(d + 1) * 8].unsqueeze(2).to_broadcast([64, 8, 64]),
                                                                              op=ALU.mult), reads=[Bk, Bsm], writes=[Bkw])
                    cb_, Bcb = cbs.next()
                    P.op("scalar", lambda e, cb_=cb_: e.activation(out=cb_[:], in_=Cs[:], func=AF.Identity), reads=[BCs], writes=[Bcb])
                    P.dma("sync", lambda e, cb_=cb_, c=c: e.dma_start(out=CST[d, c], in_=cb_[:].rearrange("p a b -> p (a b)")), reads=[Bcb])
                    P.op("gpsimd", lambda e, sm=sm: e.tensor_tensor(out=Cs[:], in0=Cs[:], in1=sm[:, 5, d * 8:(d + 1) * 8].unsqueeze(2).to_broadcast([64, 8, 129]),
                                                                   op=ALU.mult), reads=[Bsm, BCs], writes=[BCs])
                    for r in range(2):
                        for hh in range(4):
                            h = r * 4 + hh
                            P.op("tensor", lambda e, kw=kw, v=v, h=h, hh=hh: e.matmul(pc[:, hh, 0:129], lhsT=kw[:, h, :], rhs=v[:, h, :], start=True, stop=True),
                                 reads=[Bkw, Bv], writes=[Bpc])
                        P.op("vector", lambda e, r=r: e.tensor_tensor(out=Cs[:, r * 4:(r + 1) * 4, :], in0=Cs[:, r * 4:(r + 1) * 4, :], in1=pc[:, :, 0:129], op=ALU.add),
                             reads=[Bpc, BCs], writes=[BCs, Bpc])
                P.emit()

        def phase_mout(l, with_ctx):
            ncc = CTX // 64
            chunks = list(range(0 if with_ctx else ncc, NCH))
            mq3 = mqT.rearrange("(h d) t -> d h t", d=64)
            mk3 = mkT.rearrange("(h d) t -> d h t", d=64)
            with ExitStack() as st:
                gb = sb(st, [128, 32], F32)
                mng = sb(st, [64, 1024], F32)
                P.dma("sync", lambda e: e.dma_start(out=gb[:], in_=W[l]["gbias"][:, :]), writes=[Bc])
                P.dma("sync", lambda e: e.dma_start(out=mng[:], in_=W[l]["mng"][0:64, :]), writes=[Bc])
                mgs = Rot([sb(st, [64, 32], F32) for _ in range(2)])
                qs = Rot([sb(st, [64, 8, 64], BF16) for _ in range(2)])
                ks = Rot([sb(st, [64, 8, 64], BF16) for _ in range(2)])
                v1 = Rot([sb(st, [64, 8, 129], BF16) for _ in range(2)])
                for t_, b_ in zip(v1.t, v1.b):
                    P.op("gpsimd", lambda e, t_=t_: e.memset(t_[:, :, 128:129], 1.0), writes=[b_])
                cf = Rot([sb(st, [64, 2, 8 * 129], BF16) for _ in range(2)])
                mos = Rot([sb(st, [64, 1024], BF16) for _ in range(2)])
                sms = Rot([sb(st, [64, 6, 16], F32) for _ in range(2)])
                ats = Rot([sb(st, [64, 2, 8, 64], BF16) for _ in range(2)])
                hacc = sb(st, [64, 8, 128], F32)
                htmp = sb(st, [64, 8, 128], F32)
                sig = sb(st, [64, 1024], F32)
                sml = sb(st, [64, 64], F32)
                yo = Rot([sb(st, [64, 1024], BF16) for _ in range(2)])
                Bh_, Bsl = Buf(), Buf()
                pgs = Rot([pst(st, [64, 32]) for _ in range(2)])
                pss = Rot([pst(st, [64, 8, 64]) for _ in range(1)])
                po = pst(st, [64, 4, 512])
                Bpo = Buf()
                for c in chunks:
                    t0 = c * 64
                    mg, Bmg = mgs.next()
                    q, Bq = qs.next()
                    k, Bk = ks.next()
                    v, Bv = v1.next()
                    cc_, Bcc = cf.next()
                    mo, Bmo = mos.next()
                    P.dma("sync", lambda e, mg=mg, t0=t0: e.dma_start(out=mg[:], in_=MG[t0:t0 + 64, :]), writes=[Bmg])
                    P.dma("sync", lambda e, q=q, t0=t0: e.dma_start(out=q[:], in_=mq3[:, :, t0:t0 + 64]), writes=[Bq])
                    P.dma("sync", lambda e, k=k, t0=t0: e.dma_start(out=k[:], in_=mk3[:, :, t0:t0 + 64]), writes=[Bk])
                    P.dma("sync", lambda e, v=v, t0=t0: e.dma_start(out=v[:, :, 0:128], in_=TM[t0:t0 + 64, 512:1536].rearrange("p (a b) -> p a b", b=128)), writes=[Bv])
                    P.dma("sync", lambda e, cc_=cc_, c=c: e.dma_start(out=cc_[:], in_=CST[:, c].rearrange("d p f -> p d f")), writes=[Bcc])
                    P.dma("sync", lambda e, mo=mo, t0=t0: e.dma_start(out=mo[:], in_=TM[t0:t0 + 64, 1536:2560]), writes=[Bmo])
                    sm, Bsm = sms.next()
                    pg, Bpg = pgs.next()
                    gate_math(st, mg, Bmg, gb, [0, 1], pg, Bpg, sm, Bsm)
                    pS, BpS = pss.next()
                    for h in range(8):
                        P.op("tensor", lambda e, pS=pS, k=k, q=q, h=h: e.matmul(pS[:, h, :], lhsT=k[:, h, :], rhs=q[:, h, :], start=True, stop=True),
                             reads=[Bk, Bq], writes=[BpS])
                    at, Bat = ats.next()
                    for d in range(2):
                        msk = TRF if d == 0 else TRB
                        P.op("vector", lambda e, at=at, pS=pS, sm=sm, d=d: e.tensor_tensor(
                            out=at[:, d], in0=pS[:], in1=sm[:, 2, d * 8:(d + 1) * 8].unsqueeze(2).to_broadcast([64, 8, 64]), op=ALU.mult),
                            reads=[BpS, Bsm], writes=[Bat])
                        P.op("gpsimd", lambda e, at=at, d=d, msk=msk: e.tensor_tensor(
                            out=at[:, d], in0=at[:, d], in1=cbf[0:64, msk][:, 0:64].unsqueeze(1).to_broadcast([64, 8, 64]), op=ALU.mult),
                            reads=[Bat, Bc], writes=[Bat])
                    for d in range(2):
                        for r in range(2):
                            for hh in range(4):
                                h = r * 4 + hh
                                P.op("tensor", lambda e, at=at, v=v, d=d, h=h, hh=hh: e.matmul(po[:, hh, 0:129], lhsT=at[:, d, h, :], rhs=v[:, h, :], start=True, stop=False),
                                     reads=[Bat, Bv], writes=[Bpo])
                                P.op("tensor", lambda e, q=q, cc_=cc_, d=d, h=h, hh=hh: e.matmul(po[:, hh, 0:129], lhsT=q[:, h, :], rhs=cc_[:, d, h * 129:(h + 1) * 129],
                                                                                               start=False, stop=True), reads=[Bq, Bcc], writes=[Bpo])
                            ebs = sm[:, 3, d * 8 + r * 4:d * 8 + r * 4 + 4]
                            P.op("vector", lambda e, ebs=ebs: e.tensor_tensor(out=sml[:, 0:4], in0=po[:, :, 128], in1=ebs, op=ALU.mult), reads=[Bpo, Bsm], writes=[Bsl])
                            P.op("vector", lambda e: e.tensor_scalar_mul(out=sml[:, 16:20], in0=sml[:, 0:4], scalar1=-1.0), reads=[Bsl], writes=[Bsl])
                            P.op("vector", lambda e: e.tensor_tensor(out=sml[:, 0:4], in0=sml[:, 0:4], in1=sml[:, 16:20], op=ALU.max), reads=[Bsl], writes=[Bsl])
                            P.op("vector", lambda e: e.tensor_scalar_max(out=sml[:, 0:4], in0=sml[:, 0:4], scalar1=1.0), reads=[Bsl], writes=[Bsl])
                            P.op("vector", lambda e: e.reciprocal(out=sml[:, 0:4], in_=sml[:, 0:4]), reads=[Bsl], writes=[Bsl])
                            P.op("vector", lambda e, ebs=ebs: e.tensor_tensor(out=sml[:, 4:8], in0=ebs, in1=sml[:, 0:4], op=ALU.mult), reads=[Bsl, Bsm], writes=[Bsl])
                            dst = hacc if d == 0 else htmp
                            P.op("vector", lambda e, dst=dst, r=r: e.tensor_tensor(out=dst[:, r * 4:(r + 1) * 4, :], in0=po[:, :, 0:128],
                                                                                  in1=sml[:, 4:8].unsqueeze(2).to_broadcast([64, 4, 128]), op=ALU.mult),
                                 reads=[Bpo, Bsl, Bh_], writes=[Bh_, Bpo])
                    P.op("gpsimd", lambda e: e.tensor_tensor(out=hacc[:], in0=hacc[:], in1=htmp[:], op=ALU.add), reads=[Bh_], writes=[Bh_])
                    P.op("gpsimd", lambda e: e.tensor_tensor(out=htmp[:], in0=hacc[:], in1=hacc[:], op=ALU.mult), reads=[Bh_], writes=[Bh_])
                    P.op("vector", lambda e: e.tensor_reduce(out=sml[:, 8:16], in_=htmp[:], axis=AX.X, op=ALU.add), reads=[Bh_, Bsl], writes=[Bsl])
                    P.op("scalar", lambda e: e.activation(out=sml[:, 8:16], in_=sml[:, 8:16], func=AF.Ln, scale=1.0 / 128, bias=epsD[0:64, 1:2]), reads=[Bsl, Bc], writes=[Bsl])
                    P.op("scalar", lambda e: e.activation(out=sml[:, 8:16], in_=sml[:, 8:16], func=AF.Exp, scale=-0.5), reads=[Bsl], writes=[Bsl])
                    P.op("vector", lambda e: e.tensor_tensor(out=hacc[:], in0=hacc[:], in1=sml[:, 8:16].unsqueeze(2).to_broadcast([64, 8, 128]), op=ALU.mult),
                         reads=[Bsl, Bh_], writes=[Bh_])
                    P.op("gpsimd", lambda e: e.tensor_tensor(out=hacc[:].rearrange("p a b -> p (a b)"), in0=hacc[:].rearrange("p a b -> p (a b)"), in1=mng[:], op=ALU.mult),
                         reads=[Bh_, Bc], writes=[Bh_])
                    P.op("scalar", lambda e, mo=mo: e.activation(out=sig[:], in_=mo[:], func=AF.Sigmoid), reads=[Bmo, Bh_], writes=[Bh_])
                    y, By = yo.next()
                    P.op("vector", lambda e, y=y: e.tensor_tensor(out=y[:], in0=hacc[:].rearrange("p a b -> p (a b)"), in1=sig[:], op=ALU.mult), reads=[Bh_], writes=[By, Bh_])
                    P.dma("sync", lambda e, y=y, t0=t0: e.dma_start(out=Y[t0:t0 + 64, 0:1024], in_=y[:]), reads=[By])
                P.emit()

        def phase_win(l, with_ctx):
            nb = SEQ // 128
            ncb = CTX // 128
            scale = 128 ** -0.5
            wq4 = wqT.rearrange("(h d) t -> d h t", d=128)
            with ExitStack() as st:
                qs = Rot([sb(st, [128, 4, 128], BF16) for _ in range(2)])
                kts = Rot([sb(st, [128, 128], BF16) for _ in range(4)])
                vts = Rot([sb(st, [128, 129], BF16) for _ in range(4)])
                for t_, b_ in zip(vts.t, vts.b):
                    P.op("gpsimd", lambda e, t_=t_: e.memset(t_[:, 128:129], 1.0), writes=[b_])
                pts = Rot([sb(st, [128, 4, 128], BF16) for _ in range(3)])
                sml = sb(st, [128, 16], F32)
                Bsl = Buf()
                ys = Rot([sb(st, [128, 4, 128], BF16) for _ in range(2)])
                pS = Rot([pst(st, [128, 512]) for _ in range(2)])
                po = pst(st, [128, 4, 512])
                Bpo = Buf()
                jobs = []
                if with_ctx:
                    for qb in range(ncb):
                        jobs.append((qb * 128, [(kb * 128, None) for kb in range(ncb)]))
                for n_ in range(nb):
                    keys = [(kb * 128, None) for kb in range(ncb)]
                    for off, m in ((-1, WML), (0, None), (1, WMR)):
                        if 0 <= n_ + off < nb:
                            keys.append((CTX + (n_ + off) * 128, m))
                    jobs.append((CTX + n_ * 128, keys))
                for (tq, keys) in jobs:
                    for kh in range(2):
                        q, Bq = qs.next()
                        P.dma("sync", lambda e, q=q, tq=tq, kh=kh: e.dma_start(out=q[:], in_=wq4[:, kh * 4:(kh + 1) * 4, tq:tq + 128]), writes=[Bq])
                        for ki, (tk, m) in enumerate(keys):
                            kt, Bk = kts.next()
                            vt, Bv = vts.next()
                            P.dma("sync", lambda e, kt=kt, tk=tk, kh=kh: e.dma_start(out=kt[:], in_=wkT[kh * 128:(kh + 1) * 128, tk:tk + 128]), writes=[Bk])
                            P.dma("sync", lambda e, vt=vt, tk=tk, kh=kh: e.dma_start(out=vt[:, 0:128], in_=TM[tk:tk + 128, 2560 + kh * 128:2560 + (kh + 1) * 128]), writes=[Bv])
                            p, Bp = pS.next()
                            P.op("tensor", lambda e, p=p, kt=kt, q=q: e.matmul(p[:, :], lhsT=kt[:], rhs=q[:].rearrange("p a b -> p (a b)"), start=True, stop=True),
                                 reads=[Bk, Bq], writes=[Bp])
                            pt, Bpt = pts.next()
                            P.op("scalar", lambda e, pt=pt, p=p: e.activation(out=pt[:].rearrange("p a b -> p (a b)"), in_=p[:, :], func=AF.Exp, scale=scale), reads=[Bp], writes=[Bpt])
                            if m is not None:
                                P.op("vector", lambda e, pt=pt, m=m: e.tensor_tensor(out=pt[:], in0=pt[:], in1=cbf[:, m].unsqueeze(1).to_broadcast([128, 4, 128]), op=ALU.mult),
                                     reads=[Bpt, Bc], writes=[Bpt])
                            for g_ in range(4):
                                P.op("tensor", lambda e, pt=pt, vt=vt, g_=g_, ki=ki, nk=len(keys): e.matmul(po[:, g_, 0:129], lhsT=pt[:, g_, :], rhs=vt[:], start=(ki == 0), stop=(ki == nk - 1)),
                                     reads=[Bpt, Bv], writes=[Bpo])
                        P.op("vector", lambda e, kh=kh: e.tensor_tensor(out=sml[:, 0:4], in0=po[:, :, 128], in1=esink[l][:, kh * 4:(kh + 1) * 4], op=ALU.add), reads=[Bpo, Bmod], writes=[Bsl])
                        P.op("vector", lambda e: e.reciprocal(out=sml[:, 4:8], in_=sml[:, 0:4]), reads=[Bsl], writes=[Bsl])
                        y, By = ys.next()
                        P.op("vector", lambda e, y=y: e.tensor_tensor(out=y[:], in0=po[:, :, 0:128], in1=sml[:, 4:8].unsqueeze(2).to_broadcast([128, 4, 128]), op=ALU.mult),
                             reads=[Bpo, Bsl], writes=[By, Bpo])
                        P.dma("sync", lambda e, y=y, tq=tq, kh=kh: e.dma_start(out=Y[tq:tq + 128, 1024 + kh * 512:1024 + (kh + 1) * 512], in_=y[:].rearrange("p a b -> p (a b)")), reads=[By])
                P.emit()

        def phase_diff(l, with_ctx):
            lam_init = 0.8 - 0.6 * math.exp(-0.3 * l)
            nkb_all = T // 128
            ncb = CTX // 128
            with ExitStack() as st:
                dng = sb(st, [128, 128], F32)
                P.dma("sync", lambda e: e.dma_start(out=dng[:], in_=W[l]["dng"][:, :]), writes=[Bc])
                P.op("vector", lambda e: e.tensor_scalar_mul(out=dng[:], in0=dng[:], scalar1=(1.0 - lam_init)), reads=[Bc], writes=[Bc])
                Kh = Rot([sb(st, [64, 2, T], BF16) for _ in range(2)])
                Vh = Rot([sb(st, [128, nkb_all, 129], BF16) for _ in range(2)])
                for t_, b_ in zip(Vh.t, Vh.b):
                    P.op("gpsimd", lambda e, t_=t_: e.memset(t_[:, :, 128:129], 1.0), writes=[b_])
                qs = Rot([sb(st, [64, 2, 256], BF16) for _ in range(2)])
                pts = Rot([sb(st, [128, 512], BF16) for _ in range(3)])
                sml = sb(st, [128, 16], F32)
                o1 = sb(st, [128, 128], F32)
                o2 = sb(st, [128, 128], F32)
                Bsl, Bo1 = Buf(), Buf()
                ys = Rot([sb(st, [128, 128], BF16) for _ in range(2)])
                pS = Rot([pst(st, [128, 512]) for _ in range(2)])
                po = pst(st, [128, 4, 512])
                Bpo = Buf()
                qtiles = []
                if with_ctx:
                    for t0 in range(0, CTX, 256):
                        qtiles.append((t0, min(256, CTX - t0), ncb))
                for t0 in range(CTX, T, 256):
                    qtiles.append((t0, 256, nkb_all))
                for h in range(8):
                    K_, BK = Kh.next()
                    V_, BV = Vh.next()
                    P.dma("sync", lambda e, K_=K_, h=h: e.dma_start(out=K_[:], in_=fkT[h * 128:(h + 1) * 128, :].rearrange("(j d) t -> d j t", d=64)), writes=[BK])
                    for kb in range(nkb_all):
                        P.dma("sync", lambda e, V_=V_, h=h, kb=kb: e.dma_start(out=V_[:, kb, 0:128], in_=TM[kb * 128:(kb + 1) * 128, 3072 + h * 128:3072 + (h + 1) * 128]), writes=[BV])
                    for (tq, nq, nkb) in qtiles:
                        q, Bq = qs.next()
                        nsub = nq // 128
                        P.dma("sync", lambda e, q=q, tq=tq, nq=nq, h=h: e.dma_start(out=q[:, :, :nq], in_=fqT[h * 128:(h + 1) * 128, tq:tq + nq].rearrange("(j d) t -> d j t", d=64)), writes=[Bq])
                        for kb in range(nkb):
                            p, Bp = pS.next()
                            for j in range(2):
                                P.op("tensor", lambda e, p=p, K_=K_, q=q, kb=kb, j=j, nq=nq: e.matmul(
                                    p[:, j * 256:j * 256 + nq], lhsT=K_[:, j, kb * 128:(kb + 1) * 128], rhs=q[:, j, :nq], start=True, stop=True),
                                    reads=[BK, Bq], writes=[Bp])
                            pt, Bpt = pts.next()
                            if nq == 256:
                                P.op("scalar", lambda e, pt=pt, p=p: e.activation(out=pt[:], in_=p[:, :], func=AF.Exp, scale=0.125), reads=[Bp], writes=[Bpt])
                            else:
                                for j in range(2):
                                    P.op("scalar", lambda e, pt=pt, p=p, j=j, nq=nq: e.activation(out=pt[:, j * 256:j * 256 + nq], in_=p[:, j * 256:j * 256 + nq], func=AF.Exp, scale=0.125),
                                         reads=[Bp], writes=[Bpt])
                            for j in range(2):
                                for s in range(nsub):
                                    P.op("tensor", lambda e, pt=pt, V_=V_, j=j, s=s, kb=kb, nkb=nkb: e.matmul(
                                        po[:, j * 2 + s, 0:129], lhsT=pt[:, j * 256 + s * 128:j * 256 + (s + 1) * 128], rhs=V_[:, kb, :], start=(kb == 0), stop=(kb == nkb - 1)),
                                        reads=[Bpt, BV], writes=[Bpo])
                        for s in range(nsub):
                            P.op("vector", lambda e, s=s: e.reciprocal(out=sml[:, 0:1], in_=po[:, s, 128:129]), reads=[Bpo], writes=[Bsl])
                            P.op("vector", lambda e, s=s: e.reciprocal(out=sml[:, 1:2], in_=po[:, 2 + s, 128:129]), reads=[Bpo, Bsl], writes=[Bsl])
                            P.op("vector", lambda e, h=h: e.tensor_tensor(out=sml[:, 1:2], in0=sml[:, 1:2], in1=lamt[l][:, 8 + h:9 + h], op=ALU.mult), reads=[Bsl, Bmod], writes=[Bsl])
                            P.op("vector", lambda e, s=s: e.tensor_scalar(out=o1[:], in0=po[:, s, 0:128], scalar1=sml[:, 0:1], scalar2=None, op0=ALU.mult), reads=[Bpo, Bsl], writes=[Bo1])
                            P.op("vector", lambda e, s=s: e.scalar_tensor_tensor(out=o1[:], in0=po[:, 2 + s, 0:128], scalar=sml[:, 1:2], in1=o1[:], op0=ALU.mult, op1=ALU.add),
                                 reads=[Bpo, Bsl, Bo1], writes=[Bo1, Bpo])
                            P.op("gpsimd", lambda e: e.tensor_tensor(out=o2[:], in0=o1[:], in1=o1[:], op=ALU.mult), reads=[Bo1], writes=[Bo1])
                            P.op("vector", lambda e: e.tensor_reduce(out=sml[:, 2:3], in_=o2[:], axis=AX.X, op=ALU.add), reads=[Bo1, Bsl], writes=[Bsl])
                            P.op("scalar", lambda e: e.activation(out=sml[:, 2:3], in_=sml[:, 2:3], func=AF.Ln, scale=1.0 / 128, bias=epsD[:, 1:2]), reads=[Bsl, Bc], writes=[Bsl])
                            P.op("scalar", lambda e: e.activation(out=sml[:, 2:3], in_=sml[:, 2:3], func=AF.Exp, scale=-0.5), reads=[Bsl], writes=[Bsl])
                            y, By = ys.next()
                            P.op("vector", lambda e, y=y: e.scalar_tensor_tensor(out=y[:], in0=o1[:], scalar=sml[:, 2:3], in1=dng[:], op0=ALU.mult, op1=ALU.mult),
                                 reads=[Bo1, Bsl, Bc], writes=[By, Bo1])
                            P.dma("sync", lambda e, y=y, tq=tq, s=s, h=h: e.dma_start(out=Y[tq + s * 128:tq + (s + 1) * 128, 2048 + h * 128:2048 + (h + 1) * 128], in_=y[:]), reads=[By])
                P.emit()

        def phase_merge(l, with_ctx, src, dst):
            wB = G[l]["wbr"][2].rearrange("(n p) f -> n p f", p=128)
            wO = G[l]["wout"][2].rearrange("(n p) f -> n p f", p=128)
            with ExitStack() as st:
                yT = sb(st, [128, 24, 512], BF16)
                zT = sb(st, [128, KC, 512], BF16)
                ByT, BzT = Buf(), Buf()
                yrows = Rot([sb(st, [128, 3072], BF16) for _ in range(2)])
                WB = Rot([sb(st, [128, 1024], BF16) for _ in range(4)])
                WO = Rot([sb(st, [128, KC * 128], BF16) for _ in range(2)])
                gts = Rot([sb(st, [128, 512], BF16) for _ in range(4)])
                zf = Rot([sb(st, [128, 512], F32) for _ in range(3)])
                xs = Rot([sb(st, [128, 512], F32) for _ in range(2)])
                xo = Rot([sb(st, [128, 512], F32) for _ in range(2)])
                ptr = Rot([pst(st, [128, 4, 128], BF16) for _ in range(2)])
                pm = Rot([pst(st, [128, 512]) for _ in range(4)])
                for (t0, n, is_ctx) in token_groups(with_ctx):
                    md = modC[l] if is_ctx else modL[l]
                    for s in range(n // 128):
                        yr, Byr = yrows.next()
                        P.dma("sync", lambda e, yr=yr, t0=t0, s=s: e.dma_start(out=yr[:], in_=Y[t0 + s * 128:t0 + (s + 1) * 128, :]), writes=[Byr])
                        for c4 in range(6):
                            pt_, Bpt_ = ptr.next()
                            for j in range(4):
                                cbk = c4 * 4 + j
                                P.op("tensor", lambda e, pt_=pt_, yr=yr, cbk=cbk, j=j: e.transpose(pt_[:, j, :], yr[:, cbk * 128:(cbk + 1) * 128], cbf[:, IDF]),
                                     reads=[Byr, Bc], writes=[Bpt_])
                            eng = "vector" if c4 % 2 == 0 else "scalar"
                            if eng == "vector":
                                P.op("vector", lambda e, pt_=pt_, c4=c4, s=s: e.tensor_copy(out=yT[:, c4 * 4:(c4 + 1) * 4, s * 128:(s + 1) * 128], in_=pt_[:]), reads=[Bpt_], writes=[ByT])
                            else:
                                P.op("scalar", lambda e, pt_=pt_, c4=c4, s=s: e.activation(out=yT[:, c4 * 4:(c4 + 1) * 4, s * 128:(s + 1) * 128], in_=pt_[:], func=AF.Identity), reads=[Bpt_], writes=[ByT])
                    for cb in range(KC):
                        zs = []
                        for i in range(3):
                            w, Bw = WB.next()
                            gt, Bgt = gts.next()
                            P.dma("sync", lambda e, w=w, i=i, cb=cb: e.dma_start(out=w[:], in_=wB[i * KC + cb]), writes=[Bw])
                            P.dma("sync", lambda e, gt=gt, i=i, cb=cb, t0=t0, n=n: e.dma_start(out=gt[:, :n], in_=gatesT[(i * KC + cb) * 128:(i * KC + cb + 1) * 128, t0:t0 + n]), writes=[Bgt])
                            p, Bp = pm.next()
                            for kc in range(8):
                                P.op("tensor", lambda e, p=p, w=w, i=i, kc=kc, n=n: e.matmul(p[:, :n], lhsT=w[:, kc * 128:(kc + 1) * 128], rhs=yT[:, i * 8 + kc, :n], start=(kc == 0), stop=(kc == 7)),
                                     reads=[Bw, ByT], writes=[Bp])
                            z, Bz = zf.next()
                            P.op("vector", lambda e, z=z, p=p, gt=gt, n=n: e.tensor_tensor(out=z[:, :n], in0=p[:, :n], in1=gt[:, :n], op=ALU.mult), reads=[Bp, Bgt], writes=[Bz])
                            zs.append((z, Bz))
                        P.op("gpsimd", lambda e, zs=zs, n=n: e.tensor_tensor(out=zs[0][0][:, :n], in0=zs[0][0][:, :n], in1=zs[1][0][:, :n], op=ALU.add), reads=[zs[0][1], zs[1][1]], writes=[zs[0][1]])
                        P.op("gpsimd", lambda e, zs=zs, n=n, cb=cb: e.tensor_tensor(out=zT[:, cb, :n], in0=zs[0][0][:, :n], in1=zs[2][0][:, :n], op=ALU.add), reads=[zs[0][1], zs[2][1]], writes=[BzT])
                    for cb in range(KC):
                        w, Bw = WO.next()
                        x, Bx = xs.next()
                        P.dma("sync", lambda e, w=w, cb=cb: e.dma_start(out=w[:], in_=wO[cb]), writes=[Bw])
                        P.dma("sync", lambda e, x=x, cb=cb, t0=t0, n=n: e.dma_start(out=x[:, :n], in_=src[cb][:, t0:t0 + n]), writes=[Bx])
                        p, Bp = pm.next()
                        for kc in range(KC):
                            P.op("tensor", lambda e, p=p, w=w, kc=kc, n=n: e.matmul(p[:, :n], lhsT=w[:, kc * 128:(kc + 1) * 128], rhs=zT[:, kc, :n], start=(kc == 0), stop=(kc == KC - 1)),
                                 reads=[Bw, BzT], writes=[Bp])
                        o, Bo = xo.next()
                        P.op("vector", lambda e, o=o, p=p, x=x, cb=cb, n=n, md=md: e.scalar_tensor_tensor(out=o[:, :n], in0=p[:, :n], scalar=md[:, 2, cb:cb + 1], in1=x[:, :n], op0=ALU.mult, op1=ALU.add),
                             reads=[Bp, Bx, Bmod], writes=[Bo])
                        P.dma("sync", lambda e, o=o, cb=cb, t0=t0, n=n: e.dma_start(out=dst[cb][:, t0:t0 + n], in_=o[:, :n]), reads=[Bo])
                P.emit()

        def phase_ffn_up(l, with_ctx, src):
            wU = G[l]["wup"][2].rearrange("(n p) f -> n p f", p=128)
            with ExitStack() as st:
                hT = sb(st, [128, KC, 512], BF16)
                Bh = Buf()
                xch = Rot([sb(st, [128, 512], F32) for _ in range(3)])
                sq = Rot([sb(st, [128, 512], BF16) for _ in range(2)])
                tmp = Rot([sb(st, [128, 512], F32) for _ in range(2)])
                rstd = sb(st, [128, 512], F32)
                cw = sb(st, [128, 3, FC], F32)
                cb_ = sb(st, [128, FC], F32)
                P.dma("sync", lambda e: e.dma_start(out=cw[:], in_=W[l]["convw"][:, :, :]), writes=[Bc])
                P.dma("sync", lambda e: e.dma_start(out=cb_[:], in_=W[l]["convb"][:, :]), writes=[Bc])
                WU = Rot([sb(st, [128, KC * 128], BF16) for _ in range(4)])
                gs = Rot([sb(st, [128, 512], F32) for _ in range(2)])
                sg = Rot([sb(st, [128, 512], F32) for _ in range(2)])
                ao = Rot([sb(st, [128, 512], BF16) for _ in range(3)])
                pm = Rot([pst(st, [128, 512]) for _ in range(6)])
                pn = pst(st, [128, 512])
                Bpn = Buf()
                seqs = ([(0, CTX, True)] if with_ctx else []) + [(CTX, SEQ, False)]
                for (s0, slen, is_ctx) in seqs:
                    ci = 3 if is_ctx else 2
                    md = modC[l] if is_ctx else modL[l]
                    for o0 in range(0, slen, 510):
                        no = min(510, slen - o0)
                        lo = max(o0 - 1, 0)
                        hi = min(o0 + no + 1, slen)
                        c0 = lo - (o0 - 1)
                        nin = no + 2
                        norm_prologue(st, src, [(c0, s0 + lo, hi - lo)], hT, Bh, gef[l][:, ci, :], md[:, 3, :], pn, Bpn, xch, sq, rstd, tmp)
                        if c0 == 1:
                            P.op("gpsimd", lambda e: e.memset(hT[:, :, 0:1], 0.0), writes=[Bh])
                        if hi - lo + c0 < nin:
                            P.op("gpsimd", lambda e, nin=nin: e.memset(hT[:, :, nin - 1:nin], 0.0), writes=[Bh])
                        for fc in range(FC):
                            wu, Bwu = WU.next()
                            wg, Bwg = WU.next()
                            P.dma("sync", lambda e, wu=wu, fc=fc: e.dma_start(out=wu[:], in_=wU[fc]), writes=[Bwu])
                            P.dma("sync", lambda e, wg=wg, fc=fc: e.dma_start(out=wg[:], in_=wU[FC + fc]), writes=[Bwg])
                            pu, Bpu = pm.next()
                            pg_, Bpg_ = pm.next()
                            for (p, Bp, w, Bw) in ((pu, Bpu, wu, Bwu), (pg_, Bpg_, wg, Bwg)):
                                for kc in range(KC):
                                    P.op("tensor", lambda e, p=p, w=w, kc=kc, nin=nin: e.matmul(p[:, :nin], lhsT=w[:, kc * 128:(kc + 1) * 128], rhs=hT[:, kc, :nin], start=(kc == 0), stop=(kc == KC - 1)),
                                         reads=[Bw, Bh], writes=[Bp])
                            g1_, Bg1 = gs.next()
                            P.op("scalar", lambda e, g1_=g1_, pg_=pg_, fc=fc, no=no: e.activation(out=g1_[:, :no], in_=pg_[:, 1:no + 1], func=AF.Identity, scale=cw[:, 1, fc:fc + 1], bias=cb_[:, fc:fc + 1]),
                                 reads=[Bpg_, Bc], writes=[Bg1])
                            P.op("vector", lambda e, g1_=g1_, pg_=pg_, fc=fc, no=no: e.scalar_tensor_tensor(out=g1_[:, :no], in0=pg_[:, 0:no], scalar=cw[:, 0, fc:fc + 1], in1=g1_[:, :no], op0=ALU.mult, op1=ALU.add),
                                 reads=[Bpg_, Bg1, Bc], writes=[Bg1])
                            P.op("vector", lambda e, g1_=g1_, pg_=pg_, fc=fc, no=no: e.scalar_tensor_tensor(out=g1_[:, :no], in0=pg_[:, 2:no + 2], scalar=cw[:, 2, fc:fc + 1], in1=g1_[:, :no], op0=ALU.mult, op1=ALU.add),
                                 reads=[Bpg_, Bg1, Bc], writes=[Bg1, Bpg_])
                            s_, Bs_ = sg.next()
                            P.op("scalar", lambda e, s_=s_, g1_=g1_, no=no: e.activation(out=s_[:, :no], in_=g1_[:, :no], func=AF.Silu), reads=[Bg1], writes=[Bs_])
                            a, Ba = ao.next()
                            P.op("vector", lambda e, a=a, s_=s_, pu=pu, no=no: e.tensor_tensor(out=a[:, :no], in0=pu[:, 1:no + 1], in1=s_[:, :no], op=ALU.mult), reads=[Bs_, Bpu], writes=[Ba, Bpu])
                            P.dma("sync", lambda e, a=a, fc=fc, s0=s0, o0=o0, no=no: e.dma_start(out=actT[fc * 128:(fc + 1) * 128, s0 + o0:s0 + o0 + no], in_=a[:, :no]), reads=[Ba])
                P.emit()

        def phase_ffn_dn(l, with_ctx, src, dst):
            wD = G[l]["wdn"][2].rearrange("(n p) f -> n p f", p=128)
            a3 = actT.rearrange("(c p) t -> p c t", p=128)
            with ExitStack() as st:
                aT = sb(st, [128, FC, 512], BF16)
                Ba = Buf()
                WD = Rot([sb(st, [128, FC * 128], BF16) for _ in range(2)])
                xs = Rot([sb(st, [128, 512], F32) for _ in range(2)])
                xo = Rot([sb(st, [128, 512], F32) for _ in range(2)])
                pm = Rot([pst(st, [128, 512]) for _ in range(4)])
                for (t0, n, is_ctx) in token_groups(with_ctx):
                    md = modC[l] if is_ctx else modL[l]
                    step = 16
                    for f0 in range(0, FC, step):
                        f1 = min(FC, f0 + step)
                        P.dma("sync", lambda e, f0=f0, f1=f1, t0=t0, n=n: e.dma_start(out=aT[:, f0:f1, :n], in_=a3[:, f0:f1, t0:t0 + n]), writes=[Ba])
                    for cb in range(KC):
                        w, Bw = WD.next()
                        x, Bx = xs.next()
                        P.dma("sync", lambda e, w=w, cb=cb: e.dma_start(out=w[:], in_=wD[cb]), writes=[Bw])
                        P.dma("sync", lambda e, x=x, cb=cb, t0=t0, n=n: e.dma_start(out=x[:, :n], in_=src[cb][:, t0:t0 + n]), writes=[Bx])
                        p, Bp = pm.next()
                        for kc in range(FC):
                            P.op("tensor", lambda e, p=p, w=w, kc=kc, n=n: e.matmul(p[:, :n], lhsT=w[:, kc * 128:(kc + 1) * 128], rhs=aT[:, kc, :n], start=(kc == 0), stop=(kc == FC - 1)),
                                 reads=[Bw, Ba], writes=[Bp])
                        o, Bo = xo.next()
                        P.op("vector", lambda e, o=o, p=p, x=x, cb=cb, n=n, md=md: e.scalar_tensor_tensor(out=o[:, :n], in0=p[:, :n], scalar=md[:, 5, cb:cb + 1], in1=x[:, :n], op0=ALU.mult, op1=ALU.add),
                             reads=[Bp, Bx, Bmod], writes=[Bo])
                        P.dma("sync", lambda e, o=o, cb=cb, t0=t0, n=n: e.dma_start(out=dst[cb][:, t0:t0 + n], in_=o[:, :n]), reads=[Bo])
                P.emit()

        def phase_final(src):
            with ExitStack() as st:
                xch = Rot([sb(st, [128, 512], F32) for _ in range(3)])
                sq = Rot([sb(st, [128, 512], BF16) for _ in range(2)])
                rstd = sb(st, [128, 512], F32)
                fg = sb(st, [128, KC], F32)
                os_ = Rot([sb(st, [128, 512], F32) for _ in range(3)])
                pn = pst(st, [128, 512])
                Bpn, Brs = Buf(), Buf()
                P.op("vector", lambda e: e.tensor_scalar_mul(out=fg[:], in0=fngt[:], scalar1=SQD), reads=[Bc], writes=[Bc])
                import os as _o3
                _skip = _o3.environ.get("K_SKIP", "") != ""
                for t0 in range(CTX, T, 512):
                    n = min(512, T - t0)
                    for kc in range(0 if not _skip else KC, KC):
                        x, Bx = xch.next()
                        s, Bs_ = sq.next()
                        P.dma("sync", lambda e, x=x, kc=kc, t0=t0, n=n: e.dma_start(out=x[:, :n], in_=src[kc][:, t0:t0 + n]), writes=[Bx])
                        P.op("scalar", lambda e, x=x, s=s, n=n: e.activation(out=s[:, :n], in_=x[:, :n], func=AF.Square), reads=[Bx], writes=[Bs_])
                        P.op("tensor", lambda e, s=s, n=n, kc=kc: e.matmul(pn[:, :n], lhsT=cbf[:, ONE], rhs=s[:, :n], start=(kc == 0), stop=(kc == KC - 1)), reads=[Bs_, Bc], writes=[Bpn])
                    if not _skip:
                        P.op("scalar", lambda e, n=n: e.activation(out=rstd[:, :n], in_=pn[:, :n], func=AF.Ln, bias=epsD[:, 0:1]), reads=[Bpn, Bc], writes=[Brs])
                        P.op("scalar", lambda e, n=n: e.activation(out=rstd[:, :n], in_=rstd[:, :n], func=AF.Exp, scale=-0.5), reads=[Brs], writes=[Brs])
                    for kc in range(KC):
                        x, Bx = xch.next()
                        o, Bo = os_.next()
                        P.dma("sync", lambda e, x=x, kc=kc, t0=t0, n=n: e.dma_start(out=x[:, :n], in_=src[kc][:, t0:t0 + n]), writes=[Bx])
                        import os as _o2
                        _dbg = _o2.environ.get("K_DBG", "")
                        if _dbg == "rstd":
                            P.op("vector", lambda e, o=o, n=n: e.tensor_copy(out=o[:, :n], in_=rstd[:, :n]), reads=[Bx, Brs, Bc], writes=[Bo])
                        elif _dbg == "pn":
                            P.op("vector", lambda e, o=o, n=n: e.tensor_copy(out=o[:, :n], in_=pn[:, :n]), reads=[Bx, Brs, Bc, Bpn], writes=[Bo])
                        elif _dbg == "x":
                            P.op("vector", lambda e, o=o, x=x, n=n: e.tensor_copy(out=o[:, :n], in_=x[:, :n]), reads=[Bx, Brs, Bc], writes=[Bo])
                        elif _dbg == "fg":
                            P.op("vector", lambda e, o=o, x=x, n=n, kc=kc: e.tensor_scalar(out=o[:, :n], in0=x[:, :n], scalar1=fg[:, kc:kc + 1], scalar2=None, op0=ALU.mult), reads=[Bx, Brs, Bc], writes=[Bo])
                        else:
                            P.op("vector", lambda e, x=x, o=o, kc=kc, n=n: e.scalar_tensor_tensor(out=o[:, :n], in0=x[:, :n], scalar=fg[:, kc:kc + 1], in1=rstd[:, :n], op0=ALU.mult, op1=ALU.mult),
                                 reads=[Bx, Brs, Bc], writes=[Bo])
                        P.dma("sync", lambda e, o=o, kc=kc, t0=t0, n=n: e.dma_start(out=out[kc][:, t0 - CTX:t0 - CTX + n], in_=o[:, :n]), reads=[Bo])
                P.emit()

        import os as _os
        _stop = int(_os.environ.get("K_STOP", "999"))
        _cnt = [0]

        def _go(fn, *a):
            _cnt[0] += 1
            if _cnt[0] <= _stop:
                fn(*a)
        if _os.environ.get("K_ONEBLK", "") == "":
            P.emit()
        _go(phase_weights)
        cur = 0
        for l in range(C.NL):
            last = (l == C.NL - 1)
            wc = not last
            _go(phase_ada, l)
            _go(phase_p1, l, xT[cur])
            _go(phase_mstate, l, 0)
            _go(phase_mstate, l, 1)
            _go(phase_mout, l, wc)
            _go(phase_win, l, wc)
            _go(phase_diff, l, wc)
            mid = 1 if cur != 1 else 2
            _go(phase_merge, l, wc, xT[cur], xT[mid])
            _go(phase_ffn_up, l, wc, xT[mid])
            nxt = 2 if mid == 1 else 1
            _go(phase_ffn_dn, l, wc, xT[mid], xT[nxt])
            if _cnt[0] <= _stop:
                cur = nxt
        phase_final(xT[cur])
    return nc


IN_OFF = dict(mq=0, mk=512, mv=1024, mo=2048, mg=3072, wq=3104, wk=4128, wv=4384, fq=4640, fk=5664, fv=6688, g=7712)


def _tile_F(Wm, col_blocks, n_pad):
    K = Wm.shape[0]
    Kc = K // 128
    outt = np.zeros((n_pad, 128, Kc * 128), np.float32)
    for i, cols in enumerate(col_blocks):
        cols = np.asarray(cols)
        blk = np.zeros((K, 128), np.float32)
        ok = cols >= 0
        blk[:, ok] = Wm[:, cols[ok]]
        outt[i] = blk.reshape(Kc, 128, 128).transpose(1, 0, 2).reshape(128, Kc * 128)
    return outt


def _tile_T(Wm, col_blocks):
    K = Wm.shape[0]
    Kc = K // 128
    outt = np.zeros((len(col_blocks), 128, Kc * 512), np.float32)
    for i, cols in enumerate(col_blocks):
        cols = np.asarray(cols)
        blk = np.zeros((K, 512), np.float32)
        ok = cols >= 0
        blk[:, ok] = Wm[:, cols[ok]]
        outt[i] = blk.reshape(Kc, 128, 512).transpose(1, 0, 2).reshape(128, Kc * 512)
    return outt


def _fm(v):
    return np.ascontiguousarray(v.reshape(-1, 128).T.astype(np.float32))


def _rope_tables(C, hd, units):
    nf = hd // 4
    inv = (10000.0 ** (-np.arange(nf, dtype=np.float32) / nf)).astype(np.float32)
    pos = np.arange(C.SEQ)
    rows = (pos // 64).astype(np.float32)
    cols = (pos % 64).astype(np.float32)
    ang = np.stack([rows[:, None] * inv, cols[:, None] * inv], axis=1)
    cos = np.cos(ang).astype(np.float32)
    sin = np.sin(ang).astype(np.float32)
    ct = np.ones((128, C.T), np.float32)
    stt = np.zeros((128, C.T), np.float32)
    for u in range(units):
        for i in range(2):
            for j in range(2):
                r0 = u * hd + i * (hd // 2) + j * nf
                ct[r0:r0 + nf, C.CTX:] = cos[:, i, :].T
                stt[r0:r0 + nf, C.CTX:] = sin[:, i, :].T
    return ct, stt


def _rot_matrix(hd, units):
    nf = hd // 4
    R = np.zeros((128, 128), np.float32)
    for u in range(units):
        for i in range(2):
            for f in range(nf):
                a = u * hd + i * (hd // 2) + f
                b = a + nf
                R[b, a] = -1.0
                R[a, b] = 1.0
    return R


def prepare_inputs(C, inp):
    D, KC, FC = C.D, C.KC, C.FC
    f32 = np.float32
    consts = np.zeros((128, 8 * 128), f32)
    consts[:, 0:128] = np.eye(128, dtype=f32)
    j = np.arange(64)[:, None]
    l_ = np.arange(64)[None, :]
    consts[0:64, 128:192] = (j <= l_)
    consts[0:64, 256:320] = (j >= l_)
    kj = np.arange(128)[:, None]
    qi = np.arange(128)[None, :]
    consts[:, 384:512] = (kj >= qi)
    consts[:, 512:640] = (kj <= qi)
    consts[:, 640:768] = 1.0
    consts[:, 768:896] = _rot_matrix(128, 1)
    consts[:, 896:1024] = _rot_matrix(64, 2)
    cw, sw = _rope_tables(C, 128, 1)
    cd, sd = _rope_tables(C, 64, 2)
    ropes = np.stack([cw, sw, cd, sd], 0)
    cT = np.concatenate([np.asarray(inp["c"], f32), np.asarray(inp["c_ctx"], f32)[None]], 0)
    cT = np.ascontiguousarray(cT.reshape(5, KC, 128).transpose(2, 1, 0))
    common = dict(consts=consts, ropes=ropes, cT=cT, fng=_fm(np.asarray(inp["final_norm_g"])))
    per_layer = []
    for l in range(C.NL):
        w_in = np.asarray(inp["w_in"][l], f32)
        ar = np.arange(128)
        Fb = []
        for nm, cnt in (("mq", 4), ("mk", 4), ("wq", 8), ("wk", 2), ("fq", 8), ("fk", 8)):
            for i in range(cnt):
                Fb.append(IN_OFF[nm] + i * 128 + ar)
        Fb.append(np.where(ar < 32, IN_OFF["mg"] + ar, -1))
        for i in range(3 * KC):
            Fb.append(IN_OFF["g"] + i * 128 + ar)
        a5 = np.arange(512)
        Tb = [IN_OFF["mk"] + a5, IN_OFF["mv"] + a5, IN_OFF["mv"] + 512 + a5, IN_OFF["mo"] + a5, IN_OFF["mo"] + 512 + a5,
              np.where(a5 < 256, IN_OFF["wv"] + a5, np.where(a5 < 288, IN_OFF["mg"] + a5 - 256, -1)),
              IN_OFF["fv"] + a5, IN_OFF["fv"] + 512 + a5]
        wbr = np.asarray(inp["w_branch"][l], f32)
        d = dict(
            winF=_tile_F(w_in, Fb, C.nF), winT=_tile_T(w_in, Tb),
            wbr=_tile_F(wbr.reshape(3 * 1024, D)[:1024] * 0, [], C.nB),
            wout=_tile_F(np.asarray(inp["w_out"][l], f32), [i * 128 + ar for i in range(KC)], C.nO),
            wup=_tile_F(np.asarray(inp["ffn_w_up"][l], f32), [i * 128 + ar for i in range(2 * FC)], C.nU),
            wdn=_tile_F(np.asarray(inp["ffn_w_down"][l], f32), [i * 128 + ar for i in range(KC)], C.nO),
            ada=_tile_F(np.asarray(inp["ada_w"][l], f32), [i * 128 + ar for i in range(6 * KC)], C.nA),
        )
        wb = np.zeros((C.nB, 128, 1024), f32)
        for i in range(3):
            wb[i * KC:(i + 1) * KC] = _tile_F(wbr[i], [c * 128 + ar for c in range(KC)], KC)
        d["wbr"] = wb
        adab = np.zeros((C.nA * 128,), f32)
        adab[:6 * D] = np.asarray(inp["ada_b"][l], f32)
        d["adab_full"] = adab.reshape(C.nA, 128)
        gb = np.concatenate([np.asarray(inp["mlstm_i_bias"][l], f32).reshape(16), np.asarray(inp["mlstm_f_bias"][l], f32).reshape(16)])
        d["small"] = dict(
            n1g=_fm(np.asarray(inp["norm1_g"][l])), n2g=_fm(np.asarray(inp["norm2_g"][l])),
            convw=np.ascontiguousarray(np.asarray(inp["ffn_conv_w"][l], f32).reshape(3, FC, 128).transpose(2, 0, 1)),
            convb=_fm(np.asarray(inp["ffn_conv_b"][l])),
            gbias=np.ascontiguousarray(np.broadcast_to(gb[None], (128, 32))),
            mng=np.ascontiguousarray(np.broadcast_to(np.asarray(inp["mlstm_norm_g"][l], f32)[None], (128, 1024))),
            sink=np.ascontiguousarray(np.broadcast_to(np.asarray(inp["swa_sink"][l], f32)[None], (128, 8))),
            lam=np.ascontiguousarray(np.asarray(inp["diff_lambda"][l], f32).transpose(2, 0, 1)),
            dng=np.ascontiguousarray(np.broadcast_to(np.asarray(inp["diff_norm_g"][l], f32)[None], (128, 128))),
        )
        per_layer.append(d)
    in_maps = []
    for core in range(8):
        b = core // 2
        xi = np.concatenate([np.asarray(inp["ctx"][b], f32), np.asarray(inp["x"][b], f32)], 0)
        m = dict(common)
        m["xin"] = np.ascontiguousarray(xi.T.reshape(KC, 128, C.T))
        ohm = np.zeros((128, 4), f32)
        ohm[:, b] = 1.0
        m["onehot"] = ohm
        for l in range(C.NL):
            d = per_layer[l]
            for nm in ("winF", "winT", "wbr", "wout", "wup", "wdn"):
                full = d[nm]
                n = full.shape[0] // 8
                L = full.shape[2]
                PR = piece_rows(n * 128, L)
                m[f"{nm}{l}"] = np.ascontiguousarray(full.reshape(-1, 8, PR, L)[:, core]).reshape(n, 128, L)
            n = d["ada"].shape[0] // 8
            m[f"ada{l}"] = np.ascontiguousarray(d["ada"][core * n:(core + 1) * n])
            n = C.nA // 8
            m[f"adab{l}"] = np.ascontiguousarray(d["adab_full"][core * n:(core + 1) * n].T)
            for k, v in d["small"].items():
                m[f"{k}{l}"] = v
        in_maps.append(m)
    return in_maps


_CACHE = {}


def run(C, inp):
    key = (C.D, C.SEQ, C.CTX, C.DFF, C.NL)
    if key not in _CACHE:
        _CACHE[key] = build_program(C)
    nc = _CACHE[key]
    in_maps = prepare_inputs(C, inp)
    res = run_bass_kernel_spmd(nc, in_maps, core_ids=list(range(8)))
    outs = []
    for b in range(4):
        o = res.results[2 * b]["out"]
        outs.append(np.ascontiguousarray(o.reshape(C.D, C.SEQ).T))
    return np.stack(outs, 0).astype(np.float32)


def kernel(**inputs):
    return run(Cfg(), inputs)
```

```python
import math
from contextlib import ExitStack
import numpy as np
import concourse.bass as bass
import concourse.mybir as mybir
from concourse.bass_utils import run_bass_kernel_spmd

F32, BF16 = mybir.dt.float32, mybir.dt.bfloat16
ALU = mybir.AluOpType
AF = mybir.ActivationFunctionType
AX = mybir.AxisListType
ENGS = ("tensor", "vector", "scalar", "gpsimd", "sync")
EPS = 1e-6
QUADS = [[0, 1, 2, 3], [4, 5, 6, 7]]
PAIRS = [[0, 4], [1, 5], [2, 6], [3, 7]]


def piece_rows(rows_loc, L):
    pr = 1
    while pr * 2 * L <= 262144 and rows_loc % (pr * 2) == 0:
        pr *= 2
    return pr


class Cfg:
    def __init__(self, D=4096, SEQ=4096, CTX=256, DFF=11008, NL=2):
        self.D, self.SEQ, self.CTX, self.DFF, self.NL = D, SEQ, CTX, DFF, NL
        self.KC, self.FC, self.T = D // 128, DFF // 128, CTX + SEQ
        self.KG = min(8, self.KC)
        self.nF = -(-(35 + 3 * self.KC) // 8) * 8
        self.nB = -(-(3 * self.KC) // 8) * 8
        self.nO = -(-self.KC // 8) * 8
        self.nU = -(-(2 * self.FC) // 8) * 8
        self.nA = -(-(6 * self.KC) // 8) * 8


class Buf:
    __slots__ = ("w", "r", "dsem", "dgen")

    def __init__(self):
        self.w = None
        self.r = {}
        self.dsem = None
        self.dgen = -1


class Prog:
    def __init__(self, nc, stack, n_dma_sems=90):
        self.nc = nc
        self.ops = {e: [] for e in ENGS}
        self.esem = {e: stack.enter_context(nc.semaphore("pe_" + e)) for e in ENGS}
        self.ecnt = {e: 0 for e in ENGS}
        self.waited = {e: {} for e in ENGS}
        self.dma_sems = [stack.enter_context(nc.semaphore(f"dq{i}")) for i in range(n_dma_sems)]
        self.dma_cnt = [0] * n_dma_sems
        self.dma_free = list(range(n_dma_sems))
        self.bgen = 0

    def _need(self, eng, ev, waits):
        sem, val = ev
        k = id(sem)
        if self.waited[eng].get(k, 0) >= val:
            return
        self.waited[eng][k] = val
        waits.append((sem, val))

    def _deps(self, eng, reads, writes):
        skip = self.esem[eng] if eng == "tensor" else None
        need = {}

        def add(ev):
            if ev is None or ev[0] is skip:
                return
            k = id(ev[0])
            if k not in need or need[k][1] < ev[1]:
                need[k] = ev
        for b in reads:
            add(b.w)
        for b in writes:
            add(b.w)
            for ev in b.r.values():
                add(ev)
        waits = []
        for ev in need.values():
            self._need(eng, ev, waits)
        return waits

    def _commit(self, ev, reads, writes):
        k = id(ev[0])
        for b in reads:
            b.r[k] = ev
        for b in writes:
            b.w = ev
            b.r = {}

    def op(self, eng, fn, reads=(), writes=()):
        waits = self._deps(eng, reads, writes)
        self.ecnt[eng] += 1
        ev = (self.esem[eng], self.ecnt[eng])
        self.ops[eng].append((waits, fn, (self.esem[eng], 1)))
        self._commit(ev, reads, writes)

    def dma(self, eng, fn, reads=(), writes=(), slot=None):
        waits = self._deps(eng, reads, writes)
        if slot is None:
            slot = (list(writes) + list(reads))[0]
        if slot.dsem is None or slot.dgen != self.bgen:
            slot.dsem = self.dma_free.pop(0)
            slot.dgen = self.bgen
        i = slot.dsem
        self.dma_cnt[i] += 16
        ev = (self.dma_sems[i], self.dma_cnt[i])
        self.ops[eng].append((waits, fn, (self.dma_sems[i], 16)))
        self._commit(ev, reads, writes)

    def barrier(self):
        evs = [(self.esem[e], self.ecnt[e]) for e in ENGS if self.ecnt[e] > 0]
        evs += [(self.dma_sems[i], c) for i, c in enumerate(self.dma_cnt) if c > 0]
        for e in ENGS:
            waits = []
            for ev in evs:
                self._need(e, ev, waits)
            if waits:
                self.ops[e].append((waits, None, None))
        self.dma_free = list(range(len(self.dma_sems)))
        self.bgen += 1

    def emit(self):
        self.barrier()
        with self.nc.Block() as block:
            for e in ENGS:
                ops = self.ops[e]

                def body(engobj, ops=ops):
                    for waits, fn, inc in ops:
                        for sem, val in waits:
                            engobj.wait_ge(sem, val)
                        if fn is not None:
                            fn(engobj).then_inc(inc[0], inc[1])
                getattr(block, e)(body)
        self.ops = {e: [] for e in ENGS}


class Rot:
    def __init__(self, tiles):
        self.t = tiles
        self.b = [Buf() for _ in tiles]
        self.i = 0

    def next(self):
        k = self.i % len(self.t)
        self.i += 1
        return self.t[k], self.b[k]


import os as _osq
STQ = _osq.environ.get("K_STQ", "gpsimd")
OVL = _osq.environ.get("K_OVL", "0") == "1"


def build_program(C):
    nc = bass.Bass("TRN2", target_bir_lowering=False)
    D, KC, FC, T, CTX, SEQ, KG = C.D, C.KC, C.FC, C.T, C.CTX, C.SEQ, C.KG
    SQD = math.sqrt(D)
    uid = [0]

    def din(name, shape, dt=F32):
        return nc.dram_tensor(name, list(shape), dt, kind="ExternalInput").ap()

    def dint(name, shape, dt):
        return nc.dram_tensor(name, list(shape), dt, kind="Internal").ap()

    xin = din("xin", [KC, 128, T])
    cT = din("cT", [128, KC, 5])
    consts = din("consts", [128, 8 * 128])
    ropes = din("ropes", [4, 128, T])
    onehot = din("onehot", [128, 4])
    fng = din("fng", [128, KC])
    W = []
    for l in range(C.NL):
        W.append(dict(
            winF=din(f"winF{l}", [C.nF // 8, 128, KC * 128]),
            winT=din(f"winT{l}", [1, 128, KC * 512]),
            wbr=din(f"wbr{l}", [C.nB // 8, 128, 8 * 128]),
            wout=din(f"wout{l}", [C.nO // 8, 128, KC * 128]),
            wup=din(f"wup{l}", [C.nU // 8, 128, KC * 128]),
            wdn=din(f"wdn{l}", [C.nO // 8, 128, FC * 128]),
            ada=din(f"ada{l}", [C.nA // 8, 128, KC * 128]),
            adab=din(f"adab{l}", [128, C.nA // 8]),
            n1g=din(f"n1g{l}", [128, KC]), n2g=din(f"n2g{l}", [128, KC]),
            convw=din(f"convw{l}", [128, 3, FC]), convb=din(f"convb{l}", [128, FC]),
            gbias=din(f"gbias{l}", [128, 32]), mng=din(f"mng{l}", [128, 1024]),
            sink=din(f"sink{l}", [128, 8]), lam=din(f"lam{l}", [64, 4, 8]),
            dng=din(f"dng{l}", [128, 128]),
        ))
    out = nc.dram_tensor("out", [KC, 128, SEQ], F32, kind="ExternalOutput").ap()

    G = []
    for l in range(C.NL):
        g = {}
        for nm, n, L in (("winF", C.nF, KC * 128), ("winT", 8, KC * 512), ("wbr", C.nB, 1024),
                         ("wout", C.nO, KC * 128), ("wup", C.nU, KC * 128), ("wdn", C.nO, FC * 128)):
            g[nm] = (dint(f"{nm}s{l}", [n // 8 * 128, L], BF16), dint(f"{nm}q{l}", [n // 2 * 128, L], BF16),
                     dint(f"{nm}g{l}", [n * 128, L], BF16), n, L)
        g["mods"] = dint(f"mods{l}", [128, C.nA // 8 * 5], F32)
        g["modq"] = dint(f"modq{l}", [4 * 128, C.nA // 8 * 5], F32)
        g["modg"] = dint(f"modg{l}", [8 * 128, C.nA // 8 * 5], F32)
        G.append(g)
    xT = [xin, dint("xT1", [KC, 128, T], F32), dint("xT2", [KC, 128, T], F32)]
    mqT = dint("mqT", [512, T], BF16)
    mkT = dint("mkT", [512, T], BF16)
    wqT = dint("wqT", [1024, T], BF16)
    wkT = dint("wkT", [256, T], BF16)
    fqT = dint("fqT", [1024, T], BF16)
    fkT = dint("fkT", [1024, T], BF16)
    gatesT = dint("gatesT", [3 * D, T], BF16)
    TM = dint("TM", [T, 4096], BF16)
    MG = dint("MG", [T, 32], F32)
    NCH = T // 64
    CST = dint("CST", [2, NCH, 64, 8 * 129], BF16)
    Y = dint("Y", [T, 3072], BF16)
    actT = dint("actT", [FC * 128, T], BF16)

    with ExitStack() as top:
        P = Prog(nc, top)

        def sb(st, shape, dt):
            uid[0] += 1
            return st.enter_context(nc.sbuf_tensor(f"s{uid[0]}", list(shape), dt))

        def pst(st, shape, dt=F32):
            uid[0] += 1
            return st.enter_context(nc.psum_tensor(f"p{uid[0]}", list(shape), dt))

        cst = sb(top, [128, 8 * 128], F32)
        cbf = sb(top, [128, 8 * 128], BF16)
        Bc = Buf()
        oh = sb(top, [128, 4], F32)
        fngt = sb(top, [128, KC], F32)
        modL = [sb(top, [128, 6, KC], F32) for _ in range(C.NL)]
        modC = [sb(top, [128, 6, KC], F32) for _ in range(C.NL)]
        gef = [sb(top, [128, 4, KC], F32) for _ in range(C.NL)]
        lamt = [sb(top, [128, 16], F32) for _ in range(C.NL)]
        esink = [sb(top, [128, 8], F32) for _ in range(C.NL)]
        Bmod = Buf()
        IDF, TRF, TRB, WML, WMR, ONE, ROW, ROD = [slice(i * 128, (i + 1) * 128) for i in range(8)]

        import os as _os0
        _tog = _os0.environ.get("K_TOG", "")
        epsD = sb(top, [128, 4], F32)
        if "nomemset" not in _tog:
            P.op("vector", lambda e: e.memset(epsD[:, 0:1], EPS * D), writes=[Bc])
            P.op("vector", lambda e: e.memset(epsD[:, 1:2], EPS), writes=[Bc])
            P.op("vector", lambda e: e.memset(epsD[:, 2:3], 1.0), writes=[Bc])
        touch = sb(top, [1, 64], F32)
        Bt_ = Buf()
        _all_in = [cT, ropes, onehot, fng, xin, consts]
        for _l in range(C.NL):
            _all_in += [W[_l][k] for k in W[_l]]
        for _ap in ([] if "notouch" in _tog else _all_in):
            _v = _ap
            while len(_v.shape) > 2:
                _v = _v[0]
            P.dma("sync", lambda e, _v=_v: e.dma_start(out=touch[0:1, 0:2], in_=_v[0:1, 0:2]), writes=[Bt_])
        if "noconst" not in _tog:
            P.dma("sync", lambda e: e.dma_start(out=cst[:], in_=consts[:, :]), writes=[Bc])
            P.dma("sync", lambda e: e.dma_start(out=oh[:], in_=onehot[:, :]), writes=[Bc])
            P.dma("sync", lambda e: e.dma_start(out=fngt[:], in_=fng[:, :]), writes=[Bc])
        if "nocbf" not in _tog:
            P.op("vector", lambda e: e.tensor_copy(out=cbf[:], in_=cst[:]), reads=[Bc], writes=[Bc])

        def weights_ops(l):
            if True:
                for nm in ("winF", "winT", "wbr", "wout", "wup", "wdn"):
                    sh, q, g, n, L = G[l][nm]
                    src = W[l][nm]
                    Bs = Buf()
                    for j in range(n // 8):
                        for c0 in range(0, L, 4096):
                            c1 = min(L, c0 + 4096)
                            P.dma("gpsimd", lambda e, j=j, sh=sh, src=src, c0=c0, c1=c1: e.dma_start(
                                out=sh[j * 128:(j + 1) * 128, c0:c1], in_=src[j][:, c0:c1]), writes=[Bs])
                    rows_loc = n // 8 * 128
                    PR = piece_rows(rows_loc, L)
                    for p_ in range(rows_loc // PR):
                        Bq = Buf()
                        P.op("gpsimd", lambda e, sh=sh, q=q, p_=p_, PR=PR: e.collective_compute(
                            "AllGather", ALU.bypass, replica_groups=QUADS, ins=[sh[p_ * PR:(p_ + 1) * PR, :]], outs=[q[p_ * 4 * PR:(p_ + 1) * 4 * PR, :]]),
                            reads=[Bs], writes=[Bq])
                        P.op("gpsimd", lambda e, q=q, g=g, p_=p_, PR=PR: e.collective_compute(
                            "AllGather", ALU.bypass, replica_groups=PAIRS, ins=[q[p_ * 4 * PR:(p_ + 1) * 4 * PR, :]], outs=[g[p_ * 8 * PR:(p_ + 1) * 8 * PR, :]]),
                            reads=[Bq], writes=[Buf()])

        def phase_weights():
            weights_ops(0)
            if not OVL:
                for l_ in range(1, C.NL):
                    weights_ops(l_)
            P.emit()

        def phase_ada(l):
            nloc = C.nA // 8
            with ExitStack() as st:
                sT = sb(st, [128, KC, 5], F32)
                e1 = sb(st, [128, KC, 5], F32)
                wt = Rot([sb(st, [128, KC * 128], F32) for _ in range(2)])
                bia = sb(st, [128, nloc], F32)
                msh = sb(st, [128, nloc, 5], F32)
                mall = sb(st, [128, 8, nloc * 5], F32)
                acc = sb(st, [128, 8 * nloc], F32)
                smalls = sb(st, [128, 64], F32)
                lp = sb(st, [64, 4, 8], F32)
                pr = sb(st, [64, 16], F32)
                pp = pst(st, [128, nloc, 8])
                pl = pst(st, [128, 16])
                Bs_, Bb, Bm, Bp, Ba = Buf(), Buf(), Buf(), Buf(), Buf()
                P.dma("sync", lambda e: e.dma_start(out=sT[:], in_=cT[:, :, :]), writes=[Bs_])
                P.dma("sync", lambda e: e.dma_start(out=bia[:], in_=W[l]["adab"][:, :]), writes=[Bb])
                P.op("scalar", lambda e: e.activation(out=e1[:], in_=sT[:], func=AF.Exp, scale=-1.0), reads=[Bs_], writes=[Ba])
                P.op("vector", lambda e: e.tensor_scalar_add(out=e1[:], in0=e1[:], scalar1=1.0), reads=[Ba], writes=[Ba])
                P.op("vector", lambda e: e.reciprocal(out=e1[:], in_=e1[:]), reads=[Ba], writes=[Ba])
                P.op("vector", lambda e: e.tensor_tensor(out=sT[:], in0=sT[:], in1=e1[:], op=ALU.mult), reads=[Ba, Bs_], writes=[Bs_])
                for cb in range(nloc):
                    w, Bw = wt.next()
                    P.dma("sync", lambda e, w=w, cb=cb: e.dma_start(out=w[:], in_=W[l]["ada"][cb]), writes=[Bw])
                    for kc in range(KC):
                        P.op("tensor", lambda e, w=w, cb=cb, kc=kc: e.matmul(
                            pp[:, cb, 0:5], lhsT=w[:, kc * 128:(kc + 1) * 128], rhs=sT[:, kc, :],
                            start=(kc == 0), stop=(kc == KC - 1)), reads=[Bw, Bs_], writes=[Bp])
                P.op("vector", lambda e: e.tensor_tensor(
                    out=msh[:], in0=pp[:, :, 0:5], in1=bia[:].unsqueeze(2).to_broadcast([128, nloc, 5]), op=ALU.add),
                    reads=[Bp, Bb], writes=[Bm])
                gm = G[l]
                Bd1, Bd2, Bd3 = Buf(), Buf(), Buf()
                P.dma("gpsimd", lambda e: e.dma_start(out=gm["mods"][:, :], in_=msh[:].rearrange("p a b -> p (a b)")),
                      reads=[Bm], writes=[Bd1])
                P.op("gpsimd", lambda e: e.collective_compute("AllGather", ALU.bypass, replica_groups=QUADS,
                                                              ins=[gm["mods"][:, :]], outs=[gm["modq"][:, :]]),
                     reads=[Bd1], writes=[Bd2])
                P.op("gpsimd", lambda e: e.collective_compute("AllGather", ALU.bypass, replica_groups=PAIRS,
                                                              ins=[gm["modq"][:, :]], outs=[gm["modg"][:, :]]),
                     reads=[Bd2], writes=[Bd3])
                Bma = Buf()
                P.dma("gpsimd", lambda e: e.dma_start(out=mall[:], in_=gm["modg"].rearrange("(r p) f -> p r f", p=128)),
                      reads=[Bd3], writes=[Bma])
                mv = mall[:].rearrange("p r (c f) -> p r c f", f=5)
                accv = acc[:].rearrange("p (r c) -> p r c", r=8)
                P.op("vector", lambda e: e.tensor_scalar(out=accv, in0=mv[:, :, :, 0], scalar1=oh[:, 0:1], scalar2=None, op0=ALU.mult),
                     reads=[Bma, Bc], writes=[Bmod])
                for r in range(1, 4):
                    P.op("vector", lambda e, r=r: e.scalar_tensor_tensor(out=accv, in0=mv[:, :, :, r], scalar=oh[:, r:r + 1], in1=accv,
                                                                         op0=ALU.mult, op1=ALU.add), reads=[Bma, Bc, Bmod], writes=[Bmod])
                P.op("vector", lambda e: e.tensor_copy(out=modL[l][:].rearrange("p a b -> p (a b)"), in_=acc[:, 0:6 * KC]),
                     reads=[Bmod], writes=[Bmod])
                P.op("vector", lambda e: e.tensor_copy(out=accv, in_=mv[:, :, :, 4]), reads=[Bma, Bmod], writes=[Bmod])
                P.op("vector", lambda e: e.tensor_copy(out=modC[l][:].rearrange("p a b -> p (a b)"), in_=acc[:, 0:6 * KC]),
                     reads=[Bmod], writes=[Bmod])
                ng = sb(st, [128, 2, KC], F32)
                P.dma("sync", lambda e: e.dma_start(out=ng[:, 0, :], in_=W[l]["n1g"][:, :]), writes=[Bb])
                P.dma("sync", lambda e: e.dma_start(out=ng[:, 1, :], in_=W[l]["n2g"][:, :]), writes=[Bb])
                for k, (md, which) in enumerate(((modL[l], 0), (modC[l], 0), (modL[l], 1), (modC[l], 1))):
                    sc_idx = 1 if which == 0 else 4
                    P.op("vector", lambda e, md=md, k=k, sc_idx=sc_idx, which=which: e.scalar_tensor_tensor(
                        out=gef[l][:, k, :], in0=md[:, sc_idx, :], scalar=1.0, in1=ng[:, which, :], op0=ALU.add, op1=ALU.mult),
                        reads=[Bmod, Bb], writes=[Bmod])
                P.op("vector", lambda e: e.tensor_scalar_mul(out=gef[l][:], in0=gef[l][:], scalar1=SQD), reads=[Bmod], writes=[Bmod])
                lam_init = 0.8 - 0.6 * math.exp(-0.3 * l)
                Bl = Buf()
                P.dma("sync", lambda e: e.dma_start(out=lp[:], in_=W[l]["lam"][:, :, :]), writes=[Bl])
                P.op("vector", lambda e: e.tensor_tensor(out=pr[:, 0:8], in0=lp[:, 0, :], in1=lp[:, 1, :], op=ALU.mult), reads=[Bl], writes=[Ba])
                P.op("vector", lambda e: e.tensor_tensor(out=pr[:, 8:16], in0=lp[:, 2, :], in1=lp[:, 3, :], op=ALU.mult), reads=[Bl], writes=[Ba])
                Bpl = Buf()
                P.op("tensor", lambda e: e.matmul(pl[:, :], lhsT=cst[0:64, ONE], rhs=pr[:, :], start=True, stop=True), reads=[Ba, Bc], writes=[Bpl])
                P.op("scalar", lambda e: e.activation(out=smalls[:, 0:16], in_=pl[:, :], func=AF.Exp), reads=[Bpl], writes=[Ba])
                P.op("vector", lambda e: e.tensor_tensor(out=lamt[l][:, 0:8], in0=smalls[:, 0:8], in1=smalls[:, 8:16], op=ALU.subtract), reads=[Ba], writes=[Bmod])
                P.op("vector", lambda e: e.tensor_scalar_add(out=lamt[l][:, 0:8], in0=lamt[l][:, 0:8], scalar1=lam_init), reads=[Bmod], writes=[Bmod])
                P.op("vector", lambda e: e.tensor_scalar_mul(out=lamt[l][:, 8:16], in0=lamt[l][:, 0:8], scalar1=-1.0), reads=[Bmod], writes=[Bmod])
                P.dma("sync", lambda e: e.dma_start(out=smalls[:, 32:40], in_=W[l]["sink"][:, :]), writes=[Bl])
                P.op("scalar", lambda e: e.activation(out=esink[l][:], in_=smalls[:, 32:40], func=AF.Exp), reads=[Bl], writes=[Bmod])
                P.emit()

        def token_groups(include_ctx=True):
            gs = []
            if include_ctx:
                gs.append((0, CTX, True))
            for t0 in range(CTX, T, 512):
                gs.append((t0, min(512, T - t0), False))
            return gs

        def norm_prologue(st, src, cols, hT, Bh, geff, shift, pn, Bpn, xch, sq, rstd, tmp):
            Brs = Buf()
            for (c0, t0, n) in cols:
                for kc in range(KC):
                    x, Bx = xch.next()
                    s, Bs_ = sq.next()
                    P.dma("sync", lambda e, x=x, kc=kc, t0=t0, n=n: e.dma_start(out=x[:, :n], in_=src[kc][:, t0:t0 + n]), writes=[Bx])
                    P.op("scalar", lambda e, x=x, s=s, n=n: e.activation(out=s[:, :n], in_=x[:, :n], func=AF.Square), reads=[Bx], writes=[Bs_])
                    P.op("tensor", lambda e, s=s, n=n, kc=kc, c0=c0: e.matmul(pn[:, c0:c0 + n], lhsT=cbf[:, ONE], rhs=s[:, :n],
                                                                            start=(kc == 0), stop=(kc == KC - 1)), reads=[Bs_, Bc], writes=[Bpn])
                P.op("scalar", lambda e, c0=c0, n=n: e.activation(out=rstd[:, c0:c0 + n], in_=pn[:, c0:c0 + n], func=AF.Ln, bias=epsD[:, 0:1]), reads=[Bpn, Bc], writes=[Brs])
                P.op("scalar", lambda e, c0=c0, n=n: e.activation(out=rstd[:, c0:c0 + n], in_=rstd[:, c0:c0 + n], func=AF.Exp, scale=-0.5), reads=[Brs], writes=[Brs])
                for kc in range(KC):
                    x, Bx = xch.next()
                    tm, Bt = tmp.next()
                    P.dma("sync", lambda e, x=x, kc=kc, t0=t0, n=n: e.dma_start(out=x[:, :n], in_=src[kc][:, t0:t0 + n]), writes=[Bx])
                    P.op("vector", lambda e, x=x, tm=tm, c0=c0, n=n: e.tensor_tensor(out=tm[:, :n], in0=x[:, :n], in1=rstd[:, c0:c0 + n], op=ALU.mult),
                         reads=[Bx, Brs], writes=[Bt])
                    if shift is None:
                        P.op("scalar", lambda e, tm=tm, kc=kc, c0=c0, n=n: e.activation(out=hT[:, kc, c0:c0 + n], in_=tm[:, :n], func=AF.Identity,
                                                                                      scale=geff[:, kc:kc + 1]), reads=[Bt, Bmod], writes=[Bh])
                    else:
                        P.op("scalar", lambda e, tm=tm, kc=kc, c0=c0, n=n: e.activation(out=hT[:, kc, c0:c0 + n], in_=tm[:, :n], func=AF.Identity,
                                                                                      scale=geff[:, kc:kc + 1], bias=shift[:, kc:kc + 1]),
                             reads=[Bt, Bmod], writes=[Bh])

        F_BLOCKS = ([("mq", i) for i in range(4)] + [("mk", i) for i in range(4)] + [("wq", i) for i in range(8)] +
                    [("wk", i) for i in range(2)] + [("fq", i) for i in range(8)] + [("fk", i) for i in range(8)] + [("mg", 0)] +
                    [("g", i) for i in range(3 * KC)])

        def phase_p1(l, src):
            wF = G[l]["winF"][2].rearrange("(n p) f -> n p f", p=128)
            wT = G[l]["winT"][2].rearrange("(n p) f -> n p f", p=128)
            if OVL and l + 1 < C.NL:
                weights_ops(l + 1)
            with ExitStack() as st:
                hT = sb(st, [128, KC, 512], BF16)
                Bh = Buf()
                xch = Rot([sb(st, [128, 512], F32) for _ in range(3)])
                sq = Rot([sb(st, [128, 512], BF16) for _ in range(2)])
                tmp = Rot([sb(st, [128, 512], F32) for _ in range(2)])
                rstd = sb(st, [128, 512], F32)
                WF = Rot([sb(st, [128, KC * 128], BF16) for _ in range(3)])
                WTt = Rot([sb(st, [128, KG * 512], BF16) for _ in range(2)])
                ob = Rot([sb(st, [128, 512], BF16) for _ in range(4)])
                of = Rot([sb(st, [128, 512], F32) for _ in range(3)])
                rp = sb(st, [128, 4, 512], F32)
                Brp = Buf()
                pm = Rot([pst(st, [128, 512]) for _ in range(4)])
                prr = Rot([pst(st, [128, 512]) for _ in range(2)])
                pn = pst(st, [128, 512])
                Bpn = Buf()
                for (t0, n, is_ctx) in token_groups():
                    ci = 1 if is_ctx else 0
                    md = modC[l] if is_ctx else modL[l]
                    norm_prologue(st, src, [(0, t0, n)], hT, Bh, gef[l][:, ci, :], md[:, 0, :], pn, Bpn, xch, sq, rstd, tmp)
                    P.dma("sync", lambda e, t0=t0, n=n: e.dma_start(out=rp[:, :, :n], in_=ropes[:, :, t0:t0 + n].rearrange("a p t -> p a t")), writes=[Brp])
                    _p1 = _os0.environ.get("K_P1", "")
                    for bi, (kind, idx) in enumerate(F_BLOCKS):
                        if kind == "mg" or "noF" in _p1:
                            continue
                        if "norope" in _p1 and kind in ("wq", "wk", "fq", "fk"):
                            continue
                        if "nog" in _p1 and kind == "g":
                            continue
                        if "nomq" in _p1 and kind in ("mq", "mk"):
                            continue
                        w, Bw = WF.next()
                        P.dma("sync", lambda e, w=w, bi=bi: e.dma_start(out=w[:], in_=wF[bi]), writes=[Bw])
                        p, Bp = pm.next()
                        for kc in range(KC):
                            P.op("tensor", lambda e, p=p, w=w, kc=kc, n=n: e.matmul(p[:, :n], lhsT=w[:, kc * 128:(kc + 1) * 128], rhs=hT[:, kc, :n],
                                                                                 start=(kc == 0), stop=(kc == KC - 1)), reads=[Bw, Bh], writes=[Bp])
                        o, Bo = ob.next()
                        if kind in ("mq", "mk"):
                            dst = (mqT if kind == "mq" else mkT)[idx * 128:(idx + 1) * 128, t0:t0 + n]
                            scl = 1.0 if kind == "mq" else 0.125
                            P.op("scalar", lambda e, o=o, p=p, n=n, scl=scl: e.activation(out=o[:, :n], in_=p[:, :n], func=AF.Identity, scale=scl), reads=[Bp], writes=[Bo])
                        elif kind == "g":
                            dst = gatesT[idx * 128:(idx + 1) * 128, t0:t0 + n]
                            P.op("scalar", lambda e, o=o, p=p, n=n: e.activation(out=o[:, :n], in_=p[:, :n], func=AF.Sigmoid), reads=[Bp], writes=[Bo])
                        else:
                            dst = {"wq": wqT, "wk": wkT, "fq": fqT, "fk": fkT}[kind][idx * 128:(idx + 1) * 128, t0:t0 + n]
                            ti = 0 if kind in ("wq", "wk") else 2
                            rot = ROW if kind in ("wq", "wk") else ROD
                            xb, Bxb = ob.next()
                            P.op("scalar", lambda e, xb=xb, p=p, n=n: e.activation(out=xb[:, :n], in_=p[:, :n], func=AF.Identity), reads=[Bp], writes=[Bxb])
                            _lvl = int(_os0.environ.get("K_ROPE", "3"))
                            if _lvl == 1:
                                P.dma(STQ, lambda e, xb=xb, dst=dst, n=n: e.dma_start(out=dst, in_=xb[:, :n]), reads=[Bxb])
                                continue
                            p2, Bp2 = prr.next()
                            P.op("tensor", lambda e, p2=p2, xb=xb, n=n, rot=rot: e.matmul(p2[:, :n], lhsT=cbf[:, rot], rhs=xb[:, :n], start=True, stop=True),
                                 reads=[Bxb, Bc], writes=[Bp2])
                            if _lvl == 2:
                                P.op("scalar", lambda e, o=o, p2=p2, n=n: e.activation(out=o[:, :n], in_=p2[:, :n], func=AF.Identity), reads=[Bp2], writes=[Bo])
                                P.dma(STQ, lambda e, o=o, dst=dst, n=n: e.dma_start(out=dst, in_=o[:, :n]), reads=[Bo])
                                continue
                            f1, Bf1 = of.next()
                            f2, Bf2 = of.next()
                            _sw = _os0.environ.get("K_SWAP", "")
                            if _sw == "swap":
                                P.op("vector", lambda e, f1=f1, p=p, n=n, ti=ti: e.tensor_tensor(out=f1[:, :n], in0=rp[:, ti, :n], in1=p[:, :n], op=ALU.mult),
                                     reads=[Bp, Brp], writes=[Bf1])
                            elif _sw == "sb":
                                P.op("vector", lambda e, f1=f1, p=p, n=n, ti=ti: e.tensor_tensor(out=f1[:, :n], in0=rstd[:, :n], in1=rp[:, ti, :n], op=ALU.mult),
                                     reads=[Bp, Brp], writes=[Bf1])
                            else:
                                P.op("vector", lambda e, f1=f1, p=p, n=n, ti=ti: e.tensor_tensor(out=f1[:, :n], in0=p[:, :n], in1=rp[:, ti, :n], op=ALU.mult),
                                     reads=[Bp, Brp, Bxb], writes=[Bf1])
                            if _lvl == 4:
                                P.op("scalar", lambda e, o=o, f1=f1, n=n: e.activation(out=o[:, :n], in_=f1[:, :n], func=AF.Identity), reads=[Bf1], writes=[Bo])
                                P.dma(STQ, lambda e, o=o, dst=dst, n=n: e.dma_start(out=dst, in_=o[:, :n]), reads=[Bo])
                                continue
                            P.op("vector", lambda e, f2=f2, p2=p2, n=n, ti=ti: e.tensor_tensor(out=f2[:, :n], in0=p2[:, :n], in1=rp[:, ti + 1, :n], op=ALU.mult),
                                 reads=[Bp2, Brp], writes=[Bf2])
                            P.op("vector" if "ropevec" in _p1 else "gpsimd", lambda e, o=o, f1=f1, f2=f2, n=n: e.tensor_tensor(out=o[:, :n], in0=f1[:, :n], in1=f2[:, :n], op=ALU.add),
                                 reads=[Bf1, Bf2], writes=[Bo])
                        P.dma(STQ, lambda e, o=o, dst=dst, n=n: e.dma_start(out=dst, in_=o[:, :n]), reads=[Bo])
                    nsub = n // 128
                    for blk in range(0 if "noT" not in _p1 else 8, 8):
                        ps_ = [pm.next() for _ in range(nsub)]
                        for kg in range(KC // KG):
                            w, Bw = WTt.next()
                            P.dma("sync", lambda e, w=w, blk=blk, kg=kg: e.dma_start(out=w[:], in_=wT[blk][:, kg * KG * 512:(kg + 1) * KG * 512]), writes=[Bw])
                            for s in range(nsub):
                                p, Bp = ps_[s]
                                for j in range(KG):
                                    kc = kg * KG + j
                                    P.op("tensor", lambda e, p=p, w=w, s=s, j=j, kc=kc: e.matmul(
                                        p[:, :], lhsT=hT[:, kc, s * 128:(s + 1) * 128], rhs=w[:, j * 512:(j + 1) * 512],
                                        start=(kc == 0), stop=(kc == KC - 1)), reads=[Bw, Bh], writes=[Bp])
                        for s in range(nsub):
                            p, Bp = ps_[s]
                            o, Bo = ob.next()
                            tt = t0 + s * 128
                            eng = "scalar" if s % 2 == 0 else "vector"
                            if blk == 5:
                                P.op("vector", lambda e, o=o, p=p: e.tensor_copy(out=o[:, 0:256], in_=p[:, 0:256]), reads=[Bp], writes=[Bo])
                                f1, Bf1 = of.next()
                                P.op("scalar", lambda e, f1=f1, p=p: e.activation(out=f1[:, 0:32], in_=p[:, 256:288], func=AF.Identity), reads=[Bp, Bo], writes=[Bf1])
                                P.dma(STQ, lambda e, o=o, tt=tt: e.dma_start(out=TM[tt:tt + 128, 2560:2816], in_=o[:, 0:256]), reads=[Bo])
                                P.dma(STQ, lambda e, f1=f1, tt=tt: e.dma_start(out=MG[tt:tt + 128, :], in_=f1[:, 0:32]), reads=[Bf1])
                                continue
                            scl = 0.125 if blk == 0 else 1.0
                            if eng == "scalar":
                                P.op("scalar", lambda e, o=o, p=p, scl=scl: e.activation(out=o[:, :], in_=p[:, :], func=AF.Identity, scale=scl), reads=[Bp], writes=[Bo])
                            else:
                                P.op("vector", lambda e, o=o, p=p, scl=scl: e.tensor_scalar_mul(out=o[:, :], in0=p[:, :], scalar1=scl), reads=[Bp], writes=[Bo])
                            c0 = blk * 512 if blk < 5 else 3072 + (blk - 6) * 512
                            P.dma(STQ, lambda e, o=o, tt=tt, c0=c0: e.dma_start(out=TM[tt:tt + 128, c0:c0 + 512], in_=o[:, :]), reads=[Bo])
                P.emit()

        def gate_math(st, mg, Bmg, gb, dirs, pg, Bpg, sm, Bsm):
            P.op("vector", lambda e: e.tensor_tensor(out=sm[:, 0, :], in0=mg[:, 0:16], in1=gb[0:64, 0:16], op=ALU.add), reads=[Bmg, Bc], writes=[Bsm])
            P.op("vector", lambda e: e.tensor_tensor(out=sm[:, 1, :], in0=mg[:, 16:32], in1=gb[0:64, 16:32], op=ALU.add), reads=[Bmg, Bc, Bsm], writes=[Bsm])
            P.op("scalar", lambda e: e.activation(out=sm[:, 1, :], in_=sm[:, 1, :], func=AF.Exp, scale=-1.0), reads=[Bsm], writes=[Bsm])
            P.op("scalar", lambda e: e.activation(out=sm[:, 1, :], in_=sm[:, 1, :], func=AF.Ln, bias=epsD[0:64, 2:3]), reads=[Bsm], writes=[Bsm])
            for d in dirs:
                tri = TRF if d == 0 else TRB
                P.op("tensor", lambda e, d=d, tri=tri: e.matmul(pg[:, d * 8:(d + 1) * 8], lhsT=cst[0:64, tri][:, 0:64], rhs=sm[:, 1, d * 8:(d + 1) * 8],
                                                              start=True, stop=True), reads=[Bsm, Bc], writes=[Bpg])
                P.op("tensor", lambda e, d=d: e.matmul(pg[:, 16 + d * 8:16 + (d + 1) * 8], lhsT=cst[0:64, ONE][:, 0:64], rhs=sm[:, 1, d * 8:(d + 1) * 8],
                                                     start=True, stop=True), reads=[Bsm, Bc], writes=[Bpg])
            lo, hi = min(dirs) * 8, (max(dirs) + 1) * 8
            P.op("vector", lambda e: e.tensor_tensor(out=sm[:, 2, lo:hi], in0=sm[:, 0, lo:hi], in1=pg[:, lo:hi], op=ALU.add), reads=[Bpg, Bsm], writes=[Bsm])
            P.op("vector", lambda e: e.tensor_tensor(out=sm[:, 4, lo:hi], in0=sm[:, 2, lo:hi], in1=pg[:, 16 + lo:16 + hi], op=ALU.subtract), reads=[Bpg, Bsm], writes=[Bsm])
            P.op("scalar", lambda e: e.activation(out=sm[:, 2, lo:hi], in_=sm[:, 2, lo:hi], func=AF.Exp), reads=[Bsm], writes=[Bsm])
            P.op("scalar", lambda e: e.activation(out=sm[:, 4, lo:hi], in_=sm[:, 4, lo:hi], func=AF.Exp), reads=[Bsm], writes=[Bsm])
            P.op("scalar", lambda e: e.activation(out=sm[:, 3, lo:hi], in_=pg[:, lo:hi], func=AF.Exp, scale=-1.0), reads=[Bpg, Bsm], writes=[Bsm])
            P.op("scalar", lambda e: e.activation(out=sm[:, 5, lo:hi], in_=pg[:, 16 + lo:16 + hi], func=AF.Exp, scale=-1.0), reads=[Bpg, Bsm], writes=[Bsm])

        def phase_mstate(l, d):
            ncc = CTX // 64
            order = list(range(NCH)) if d == 0 else (list(range(ncc - 1, -1, -1)) + list(range(NCH - 1, ncc - 1, -1)))
            with ExitStack() as st:
                gb = sb(st, [128, 32], F32)
                P.dma("sync", lambda e: e.dma_start(out=gb[:], in_=W[l]["gbias"][:, :]), writes=[Bc])
                Cs = sb(st, [64, 8, 129], F32)
                BCs = Buf()
                P.op("vector", lambda e: e.memset(Cs[:], 0.0), writes=[BCs])
                mgs = Rot([sb(st, [64, 32], F32) for _ in range(3)])
                ks = Rot([sb(st, [64, 8, 64], BF16) for _ in range(3)])
                v1 = Rot([sb(st, [64, 8, 129], BF16) for _ in range(3)])
                for t_, b_ in zip(v1.t, v1.b):
                    P.op("gpsimd", lambda e, t_=t_: e.memset(t_[:, :, 128:129], 1.0), writes=[b_])
                sms = Rot([sb(st, [64, 6, 16], F32) for _ in range(2)])
                kws = Rot([sb(st, [64, 8, 64], BF16) for _ in range(2)])
                cbs = Rot([sb(st, [64, 8, 129], BF16) for _ in range(2)])
                pgs = Rot([pst(st, [64, 32]) for _ in range(2)])
                pc = pst(st, [64, 4, 512])
                Bpc = Buf()
                for c in order:
                    t0 = c * 64
                    mg, Bmg = mgs.next()
                    k, Bk = ks.next()
                    v, Bv = v1.next()
                    P.dma("sync", lambda e, mg=mg, t0=t0: e.dma_start(out=mg[:], in_=MG[t0:t0 + 64, :]), writes=[Bmg])
                    P.dma("sync", lambda e, k=k, t0=t0: e.dma_start(out=k[:].rearrange("p a b -> p (a b)"), in_=TM[t0:t0 + 64, 0:512]), writes=[Bk])
                    P.dma("sync", lambda e, v=v, t0=t0: e.dma_start(out=v[:, :, 0:128], in_=TM[t0:t0 + 64, 512:1536].rearrange("p (a b) -> p a b", b=128)), writes=[Bv])
                    sm, Bsm = sms.next()
                    pg, Bpg = pgs.next()
                    gate_math(st, mg, Bmg, gb, [d], pg, Bpg, sm, Bsm)
                    kw, Bkw = kws.next()
                    P.op("vector", lambda e, kw=kw, k=k, sm=sm: e.tensor_tensor(out=kw[:], in0=k[:], in1=sm[:, 4, d * 8:(d + 1) * 8].unsqueeze(2).to_broadcast([64, 8, 64]),
                                                                              op=ALU.mult), reads=[Bk, Bsm], writes=[Bkw])
                    cb_, Bcb = cbs.next()
                    P.op("scalar", lambda e, cb_=cb_: e.activation(out=cb_[:], in_=Cs[:], func=AF.Identity), reads=[BCs], writes=[Bcb])
                    P.dma(STQ, lambda e, cb_=cb_, c=c: e.dma_start(out=CST[d, c], in_=cb_[:].rearrange("p a b -> p (a b)")), reads=[Bcb])
                    P.op("gpsimd", lambda e, sm=sm: e.tensor_tensor(out=Cs[:], in0=Cs[:], in1=sm[:, 5, d * 8:(d + 1) * 8].unsqueeze(2).to_broadcast([64, 8, 129]),
                                                                   op=ALU.mult), reads=[Bsm, BCs], writes=[BCs])
                    for r in range(2):
                        for hh in range(4):
                            h = r * 4 + hh
                            P.op("tensor", lambda e, kw=kw, v=v, h=h, hh=hh: e.matmul(pc[:, hh, 0:129], lhsT=kw[:, h, :], rhs=v[:, h, :], start=True, stop=True),
                                 reads=[Bkw, Bv], writes=[Bpc])
                        P.op("vector", lambda e, r=r: e.tensor_tensor(out=Cs[:, r * 4:(r + 1) * 4, :], in0=Cs[:, r * 4:(r + 1) * 4, :], in1=pc[:, :, 0:129], op=ALU.add),
                             reads=[Bpc, BCs], writes=[BCs, Bpc])
                P.emit()

        def phase_mout(l, with_ctx):
            ncc = CTX // 64
            chunks = list(range(0 if with_ctx else ncc, NCH))
            mq3 = mqT.rearrange("(h d) t -> d h t", d=64)
            mk3 = mkT.rearrange("(h d) t -> d h t", d=64)
            with ExitStack() as st:
                gb = sb(st, [128, 32], F32)
                mng = sb(st, [64, 1024], F32)
                P.dma("sync", lambda e: e.dma_start(out=gb[:], in_=W[l]["gbias"][:, :]), writes=[Bc])
                P.dma("sync", lambda e: e.dma_start(out=mng[:], in_=W[l]["mng"][0:64, :]), writes=[Bc])
                mgs = Rot([sb(st, [64, 32], F32) for _ in range(2)])
                qs = Rot([sb(st, [64, 8, 64], BF16) for _ in range(2)])
                ks = Rot([sb(st, [64, 8, 64], BF16) for _ in range(2)])
                v1 = Rot([sb(st, [64, 8, 129], BF16) for _ in range(2)])
                for t_, b_ in zip(v1.t, v1.b):
                    P.op("gpsimd", lambda e, t_=t_: e.memset(t_[:, :, 128:129], 1.0), writes=[b_])
                cf = Rot([sb(st, [64, 2, 8 * 129], BF16) for _ in range(2)])
                mos = Rot([sb(st, [64, 1024], BF16) for _ in range(2)])
                sms = Rot([sb(st, [64, 6, 16], F32) for _ in range(2)])
                ats = Rot([sb(st, [64, 2, 8, 64], BF16) for _ in range(2)])
                hacc = sb(st, [64, 8, 128], F32)
                htmp = sb(st, [64, 8, 128], F32)
                sig = sb(st, [64, 1024], F32)
                sml = sb(st, [64, 64], F32)
                yo = Rot([sb(st, [64, 1024], BF16) for _ in range(2)])
                Bh_, Bsl = Buf(), Buf()
                pgs = Rot([pst(st, [64, 32]) for _ in range(2)])
                pss = Rot([pst(st, [64, 8, 64]) for _ in range(1)])
                po = pst(st, [64, 4, 512])
                Bpo = Buf()
                for c in chunks:
                    t0 = c * 64
                    mg, Bmg = mgs.next()
                    q, Bq = qs.next()
                    k, Bk = ks.next()
                    v, Bv = v1.next()
                    cc_, Bcc = cf.next()
                    mo, Bmo = mos.next()
                    P.dma("sync", lambda e, mg=mg, t0=t0: e.dma_start(out=mg[:], in_=MG[t0:t0 + 64, :]), writes=[Bmg])
                    P.dma("sync", lambda e, q=q, t0=t0: e.dma_start(out=q[:], in_=mq3[:, :, t0:t0 + 64]), writes=[Bq])
                    P.dma("sync", lambda e, k=k, t0=t0: e.dma_start(out=k[:], in_=mk3[:, :, t0:t0 + 64]), writes=[Bk])
                    P.dma("sync", lambda e, v=v, t0=t0: e.dma_start(out=v[:, :, 0:128], in_=TM[t0:t0 + 64, 512:1536].rearrange("p (a b) -> p a b", b=128)), writes=[Bv])
                    P.dma("sync", lambda e, cc_=cc_, c=c: e.dma_start(out=cc_[:], in_=CST[:, c].rearrange("d p f -> p d f")), writes=[Bcc])
                    P.dma("sync", lambda e, mo=mo, t0=t0: e.dma_start(out=mo[:], in_=TM[t0:t0 + 64, 1536:2560]), writes=[Bmo])
                    sm, Bsm = sms.next()
                    pg, Bpg = pgs.next()
                    gate_math(st, mg, Bmg, gb, [0, 1], pg, Bpg, sm, Bsm)
                    pS, BpS = pss.next()
                    for h in range(8):
                        P.op("tensor", lambda e, pS=pS, k=k, q=q, h=h: e.matmul(pS[:, h, :], lhsT=k[:, h, :], rhs=q[:, h, :], start=True, stop=True),
                             reads=[Bk, Bq], writes=[BpS])
                    at, Bat = ats.next()
                    for d in range(2):
                        msk = TRF if d == 0 else TRB
                        P.op("vector", lambda e, at=at, pS=pS, sm=sm, d=d: e.tensor_tensor(
                            out=at[:, d], in0=pS[:], in1=sm[:, 2, d * 8:(d + 1) * 8].unsqueeze(2).to_broadcast([64, 8, 64]), op=ALU.mult),
                            reads=[BpS, Bsm], writes=[Bat])
                        P.op("gpsimd", lambda e, at=at, d=d, msk=msk: e.tensor_tensor(
                            out=at[:, d], in0=at[:, d], in1=cbf[0:64, msk][:, 0:64].unsqueeze(1).to_broadcast([64, 8, 64]), op=ALU.mult),
                            reads=[Bat, Bc], writes=[Bat])
                    for d in range(2):
                        for r in range(2):
                            for hh in range(4):
                                h = r * 4 + hh
                                P.op("tensor", lambda e, at=at, v=v, d=d, h=h, hh=hh: e.matmul(po[:, hh, 0:129], lhsT=at[:, d, h, :], rhs=v[:, h, :], start=True, stop=False),
                                     reads=[Bat, Bv], writes=[Bpo])
                                P.op("tensor", lambda e, q=q, cc_=cc_, d=d, h=h, hh=hh: e.matmul(po[:, hh, 0:129], lhsT=q[:, h, :], rhs=cc_[:, d, h * 129:(h + 1) * 129],
                                                                                               start=False, stop=True), reads=[Bq, Bcc], writes=[Bpo])
                            ebs = sm[:, 3, d * 8 + r * 4:d * 8 + r * 4 + 4]
                            P.op("vector", lambda e, ebs=ebs: e.tensor_tensor(out=sml[:, 0:4], in0=po[:, :, 128], in1=ebs, op=ALU.mult), reads=[Bpo, Bsm], writes=[Bsl])
                            P.op("vector", lambda e: e.tensor_scalar_mul(out=sml[:, 16:20], in0=sml[:, 0:4], scalar1=-1.0), reads=[Bsl], writes=[Bsl])
                            P.op("vector", lambda e: e.tensor_tensor(out=sml[:, 0:4], in0=sml[:, 0:4], in1=sml[:, 16:20], op=ALU.max), reads=[Bsl], writes=[Bsl])
                            P.op("vector", lambda e: e.tensor_scalar_max(out=sml[:, 0:4], in0=sml[:, 0:4], scalar1=1.0), reads=[Bsl], writes=[Bsl])
                            P.op("vector", lambda e: e.reciprocal(out=sml[:, 0:4], in_=sml[:, 0:4]), reads=[Bsl], writes=[Bsl])
                            P.op("vector", lambda e, ebs=ebs: e.tensor_tensor(out=sml[:, 4:8], in0=ebs, in1=sml[:, 0:4], op=ALU.mult), reads=[Bsl, Bsm], writes=[Bsl])
                            dst = hacc if d == 0 else htmp
                            P.op("vector", lambda e, dst=dst, r=r: e.tensor_tensor(out=dst[:, r * 4:(r + 1) * 4, :], in0=po[:, :, 0:128],
                                                                                  in1=sml[:, 4:8].unsqueeze(2).to_broadcast([64, 4, 128]), op=ALU.mult),
                                 reads=[Bpo, Bsl, Bh_], writes=[Bh_, Bpo])
                    P.op("gpsimd", lambda e: e.tensor_tensor(out=hacc[:], in0=hacc[:], in1=htmp[:], op=ALU.add), reads=[Bh_], writes=[Bh_])
                    P.op("gpsimd", lambda e: e.tensor_tensor(out=htmp[:], in0=hacc[:], in1=hacc[:], op=ALU.mult), reads=[Bh_], writes=[Bh_])
                    P.op("vector", lambda e: e.tensor_reduce(out=sml[:, 8:16], in_=htmp[:], axis=AX.X, op=ALU.add), reads=[Bh_, Bsl], writes=[Bsl])
                    P.op("scalar", lambda e: e.activation(out=sml[:, 8:16], in_=sml[:, 8:16], func=AF.Ln, scale=1.0 / 128, bias=epsD[0:64, 1:2]), reads=[Bsl, Bc], writes=[Bsl])
                    P.op("scalar", lambda e: e.activation(out=sml[:, 8:16], in_=sml[:, 8:16], func=AF.Exp, scale=-0.5), reads=[Bsl], writes=[Bsl])
                    P.op("vector", lambda e: e.tensor_tensor(out=hacc[:], in0=hacc[:], in1=sml[:, 8:16].unsqueeze(2).to_broadcast([64, 8, 128]), op=ALU.mult),
                         reads=[Bsl, Bh_], writes=[Bh_])
                    P.op("gpsimd", lambda e: e.tensor_tensor(out=hacc[:].rearrange("p a b -> p (a b)"), in0=hacc[:].rearrange("p a b -> p (a b)"), in1=mng[:], op=ALU.mult),
                         reads=[Bh_, Bc], writes=[Bh_])
                    P.op("scalar", lambda e, mo=mo: e.activation(out=sig[:], in_=mo[:], func=AF.Sigmoid), reads=[Bmo, Bh_], writes=[Bh_])
                    y, By = yo.next()
                    P.op("vector", lambda e, y=y: e.tensor_tensor(out=y[:], in0=hacc[:].rearrange("p a b -> p (a b)"), in1=sig[:], op=ALU.mult), reads=[Bh_], writes=[By, Bh_])
                    P.dma(STQ, lambda e, y=y, t0=t0: e.dma_start(out=Y[t0:t0 + 64, 0:1024], in_=y[:]), reads=[By])
                P.emit()

        def phase_win(l, with_ctx):
            nb = SEQ // 128
            ncb = CTX // 128
            scale = 128 ** -0.5
            wq4 = wqT.rearrange("(h d) t -> d h t", d=128)
            with ExitStack() as st:
                qs = Rot([sb(st, [128, 4, 128], BF16) for _ in range(2)])
                kts = Rot([sb(st, [128, 128], BF16) for _ in range(4)])
                vts = Rot([sb(st, [128, 129], BF16) for _ in range(4)])
                for t_, b_ in zip(vts.t, vts.b):
                    P.op("gpsimd", lambda e, t_=t_: e.memset(t_[:, 128:129], 1.0), writes=[b_])
                pts = Rot([sb(st, [128, 4, 128], BF16) for _ in range(3)])
                sml = sb(st, [128, 16], F32)
                Bsl = Buf()
                ys = Rot([sb(st, [128, 4, 128], BF16) for _ in range(2)])
                pS = Rot([pst(st, [128, 512]) for _ in range(2)])
                po = pst(st, [128, 4, 512])
                Bpo = Buf()
                jobs = []
                if with_ctx:
                    for qb in range(ncb):
                        jobs.append((qb * 128, [(kb * 128, None) for kb in range(ncb)]))
                for n_ in range(nb):
                    keys = [(kb * 128, None) for kb in range(ncb)]
                    for off, m in ((-1, WML), (0, None), (1, WMR)):
                        if 0 <= n_ + off < nb:
                            keys.append((CTX + (n_ + off) * 128, m))
                    jobs.append((CTX + n_ * 128, keys))
                for (tq, keys) in jobs:
                    for kh in range(2):
                        q, Bq = qs.next()
                        P.dma("sync", lambda e, q=q, tq=tq, kh=kh: e.dma_start(out=q[:], in_=wq4[:, kh * 4:(kh + 1) * 4, tq:tq + 128]), writes=[Bq])
                        for ki, (tk, m) in enumerate(keys):
                            kt, Bk = kts.next()
                            vt, Bv = vts.next()
                            P.dma("sync", lambda e, kt=kt, tk=tk, kh=kh: e.dma_start(out=kt[:], in_=wkT[kh * 128:(kh + 1) * 128, tk:tk + 128]), writes=[Bk])
                            P.dma("sync", lambda e, vt=vt, tk=tk, kh=kh: e.dma_start(out=vt[:, 0:128], in_=TM[tk:tk + 128, 2560 + kh * 128:2560 + (kh + 1) * 128]), writes=[Bv])
                            p, Bp = pS.next()
                            P.op("tensor", lambda e, p=p, kt=kt, q=q: e.matmul(p[:, :], lhsT=kt[:], rhs=q[:].rearrange("p a b -> p (a b)"), start=True, stop=True),
                                 reads=[Bk, Bq], writes=[Bp])
                            pt, Bpt = pts.next()
                            P.op("scalar", lambda e, pt=pt, p=p: e.activation(out=pt[:].rearrange("p a b -> p (a b)"), in_=p[:, :], func=AF.Exp, scale=scale), reads=[Bp], writes=[Bpt])
                            if m is not None:
                                P.op("vector", lambda e, pt=pt, m=m: e.tensor_tensor(out=pt[:], in0=pt[:], in1=cbf[:, m].unsqueeze(1).to_broadcast([128, 4, 128]), op=ALU.mult),
                                     reads=[Bpt, Bc], writes=[Bpt])
                            for g_ in range(4):
                                P.op("tensor", lambda e, pt=pt, vt=vt, g_=g_, ki=ki, nk=len(keys): e.matmul(po[:, g_, 0:129], lhsT=pt[:, g_, :], rhs=vt[:], start=(ki == 0), stop=(ki == nk - 1)),
                                     reads=[Bpt, Bv], writes=[Bpo])
                        P.op("vector", lambda e, kh=kh: e.tensor_tensor(out=sml[:, 0:4], in0=po[:, :, 128], in1=esink[l][:, kh * 4:(kh + 1) * 4], op=ALU.add), reads=[Bpo, Bmod], writes=[Bsl])
                        P.op("vector", lambda e: e.reciprocal(out=sml[:, 4:8], in_=sml[:, 0:4]), reads=[Bsl], writes=[Bsl])
                        y, By = ys.next()
                        P.op("vector", lambda e, y=y: e.tensor_tensor(out=y[:], in0=po[:, :, 0:128], in1=sml[:, 4:8].unsqueeze(2).to_broadcast([128, 4, 128]), op=ALU.mult),
                             reads=[Bpo, Bsl], writes=[By, Bpo])
                        P.dma(STQ, lambda e, y=y, tq=tq, kh=kh: e.dma_start(out=Y[tq:tq + 128, 1024 + kh * 512:1024 + (kh + 1) * 512], in_=y[:].rearrange("p a b -> p (a b)")), reads=[By])
                P.emit()

        def phase_diff(l, with_ctx):
            lam_init = 0.8 - 0.6 * math.exp(-0.3 * l)
            nkb_all = T // 128
            ncb = CTX // 128
            with ExitStack() as st:
                dng = sb(st, [128, 128], F32)
                P.dma("sync", lambda e: e.dma_start(out=dng[:], in_=W[l]["dng"][:, :]), writes=[Bc])
                P.op("vector", lambda e: e.tensor_scalar_mul(out=dng[:], in0=dng[:], scalar1=(1.0 - lam_init)), reads=[Bc], writes=[Bc])
                Kh = Rot([sb(st, [64, 2, T], BF16) for _ in range(2)])
                Vh = Rot([sb(st, [128, nkb_all, 129], BF16) for _ in range(2)])
                for t_, b_ in zip(Vh.t, Vh.b):
                    P.op("gpsimd", lambda e, t_=t_: e.memset(t_[:, :, 128:129], 1.0), writes=[b_])
                qs = Rot([sb(st, [64, 2, 256], BF16) for _ in range(2)])
                pts = Rot([sb(st, [128, 512], BF16) for _ in range(3)])
                sml = sb(st, [128, 16], F32)
                o1 = sb(st, [128, 128], F32)
                o2 = sb(st, [128, 128], F32)
                Bsl, Bo1 = Buf(), Buf()
                ys = Rot([sb(st, [128, 128], BF16) for _ in range(2)])
                pS = Rot([pst(st, [128, 512]) for _ in range(2)])
                po = pst(st, [128, 4, 512])
                Bpo = Buf()
                qtiles = []
                if with_ctx:
                    for t0 in range(0, CTX, 256):
                        qtiles.append((t0, min(256, CTX - t0), ncb))
                for t0 in range(CTX, T, 256):
                    qtiles.append((t0, 256, nkb_all))
                for h in range(8):
                    K_, BK = Kh.next()
                    V_, BV = Vh.next()
                    P.dma("sync", lambda e, K_=K_, h=h: e.dma_start(out=K_[:], in_=fkT[h * 128:(h + 1) * 128, :].rearrange("(j d) t -> d j t", d=64)), writes=[BK])
                    for kb in range(nkb_all):
                        P.dma("sync", lambda e, V_=V_, h=h, kb=kb: e.dma_start(out=V_[:, kb, 0:128], in_=TM[kb * 128:(kb + 1) * 128, 3072 + h * 128:3072 + (h + 1) * 128]), writes=[BV])
                    for (tq, nq, nkb) in qtiles:
                        q, Bq = qs.next()
                        nsub = nq // 128
                        P.dma("sync", lambda e, q=q, tq=tq, nq=nq, h=h: e.dma_start(out=q[:, :, :nq], in_=fqT[h * 128:(h + 1) * 128, tq:tq + nq].rearrange("(j d) t -> d j t", d=64)), writes=[Bq])
                        for kb in range(nkb):
                            p, Bp = pS.next()
                            for j in range(2):
                                P.op("tensor", lambda e, p=p, K_=K_, q=q, kb=kb, j=j, nq=nq: e.matmul(
                                    p[:, j * 256:j * 256 + nq], lhsT=K_[:, j, kb * 128:(kb + 1) * 128], rhs=q[:, j, :nq], start=True, stop=True),
                                    reads=[BK, Bq], writes=[Bp])
                            pt, Bpt = pts.next()
                            if nq == 256:
                                P.op("scalar", lambda e, pt=pt, p=p: e.activation(out=pt[:], in_=p[:, :], func=AF.Exp, scale=0.125), reads=[Bp], writes=[Bpt])
                            else:
                                for j in range(2):
                                    P.op("scalar", lambda e, pt=pt, p=p, j=j, nq=nq: e.activation(out=pt[:, j * 256:j * 256 + nq], in_=p[:, j * 256:j * 256 + nq], func=AF.Exp, scale=0.125),
                                         reads=[Bp], writes=[Bpt])
                            for j in range(2):
                                for s in range(nsub):
                                    P.op("tensor", lambda e, pt=pt, V_=V_, j=j, s=s, kb=kb, nkb=nkb: e.matmul(
                                        po[:, j * 2 + s, 0:129], lhsT=pt[:, j * 256 + s * 128:j * 256 + (s + 1) * 128], rhs=V_[:, kb, :], start=(kb == 0), stop=(kb == nkb - 1)),
                                        reads=[Bpt, BV], writes=[Bpo])
                        for s in range(nsub):
                            P.op("vector", lambda e, s=s: e.reciprocal(out=sml[:, 0:1], in_=po[:, s, 128:129]), reads=[Bpo], writes=[Bsl])
                            P.op("vector", lambda e, s=s: e.reciprocal(out=sml[:, 1:2], in_=po[:, 2 + s, 128:129]), reads=[Bpo, Bsl], writes=[Bsl])
                            P.op("vector", lambda e, h=h: e.tensor_tensor(out=sml[:, 1:2], in0=sml[:, 1:2], in1=lamt[l][:, 8 + h:9 + h], op=ALU.mult), reads=[Bsl, Bmod], writes=[Bsl])
                            P.op("vector", lambda e, s=s: e.tensor_scalar(out=o1[:], in0=po[:, s, 0:128], scalar1=sml[:, 0:1], scalar2=None, op0=ALU.mult), reads=[Bpo, Bsl], writes=[Bo1])
                            P.op("vector", lambda e, s=s: e.scalar_tensor_tensor(out=o1[:], in0=po[:, 2 + s, 0:128], scalar=sml[:, 1:2], in1=o1[:], op0=ALU.mult, op1=ALU.add),
                                 reads=[Bpo, Bsl, Bo1], writes=[Bo1, Bpo])
                            P.op("gpsimd", lambda e: e.tensor_tensor(out=o2[:], in0=o1[:], in1=o1[:], op=ALU.mult), reads=[Bo1], writes=[Bo1])
                            P.op("vector", lambda e: e.tensor_reduce(out=sml[:, 2:3], in_=o2[:], axis=AX.X, op=ALU.add), reads=[Bo1, Bsl], writes=[Bsl])
                            P.op("scalar", lambda e: e.activation(out=sml[:, 2:3], in_=sml[:, 2:3], func=AF.Ln, scale=1.0 / 128, bias=epsD[:, 1:2]), reads=[Bsl, Bc], writes=[Bsl])
                            P.op("scalar", lambda e: e.activation(out=sml[:, 2:3], in_=sml[:, 2:3], func=AF.Exp, scale=-0.5), reads=[Bsl], writes=[Bsl])
                            y, By = ys.next()
                            P.op("vector", lambda e, y=y: e.scalar_tensor_tensor(out=y[:], in0=o1[:], scalar=sml[:, 2:3], in1=dng[:], op0=ALU.mult, op1=ALU.mult),
                                 reads=[Bo1, Bsl, Bc], writes=[By, Bo1])
                            P.dma(STQ, lambda e, y=y, tq=tq, s=s, h=h: e.dma_start(out=Y[tq + s * 128:tq + (s + 1) * 128, 2048 + h * 128:2048 + (h + 1) * 128], in_=y[:]), reads=[By])
                P.emit()

        def phase_merge(l, with_ctx, src, dst):
            wB = G[l]["wbr"][2].rearrange("(n p) f -> n p f", p=128)
            wO = G[l]["wout"][2].rearrange("(n p) f -> n p f", p=128)
            with ExitStack() as st:
                yT = sb(st, [128, 24, 512], BF16)
                zT = sb(st, [128, KC, 512], BF16)
                ByT, BzT = Buf(), Buf()
                yrows = Rot([sb(st, [128, 3072], BF16) for _ in range(2)])
                WB = Rot([sb(st, [128, 1024], BF16) for _ in range(4)])
                WO = Rot([sb(st, [128, KC * 128], BF16) for _ in range(2)])
                gts = Rot([sb(st, [128, 512], BF16) for _ in range(4)])
                zf = Rot([sb(st, [128, 512], F32) for _ in range(3)])
                xs = Rot([sb(st, [128, 512], F32) for _ in range(2)])
                xo = Rot([sb(st, [128, 512], F32) for _ in range(2)])
                ptr = Rot([pst(st, [128, 4, 128], BF16) for _ in range(2)])
                pm = Rot([pst(st, [128, 512]) for _ in range(4)])
                for (t0, n, is_ctx) in token_groups(with_ctx):
                    md = modC[l] if is_ctx else modL[l]
                    for s in range(n // 128):
                        yr, Byr = yrows.next()
                        P.dma("sync", lambda e, yr=yr, t0=t0, s=s: e.dma_start(out=yr[:], in_=Y[t0 + s * 128:t0 + (s + 1) * 128, :]), writes=[Byr])
                        for c4 in range(6):
                            pt_, Bpt_ = ptr.next()
                            for j in range(4):
                                cbk = c4 * 4 + j
                                P.op("tensor", lambda e, pt_=pt_, yr=yr, cbk=cbk, j=j: e.transpose(pt_[:, j, :], yr[:, cbk * 128:(cbk + 1) * 128], cbf[:, IDF]),
                                     reads=[Byr, Bc], writes=[Bpt_])
                            eng = "vector" if c4 % 2 == 0 else "scalar"
                            if eng == "vector":
                                P.op("vector", lambda e, pt_=pt_, c4=c4, s=s: e.tensor_copy(out=yT[:, c4 * 4:(c4 + 1) * 4, s * 128:(s + 1) * 128], in_=pt_[:]), reads=[Bpt_], writes=[ByT])
                            else:
                                P.op("scalar", lambda e, pt_=pt_, c4=c4, s=s: e.activation(out=yT[:, c4 * 4:(c4 + 1) * 4, s * 128:(s + 1) * 128], in_=pt_[:], func=AF.Identity), reads=[Bpt_], writes=[ByT])
                    for cb in range(KC):
                        zs = []
                        for i in range(3):
                            w, Bw = WB.next()
                            gt, Bgt = gts.next()
                            P.dma("sync", lambda e, w=w, i=i, cb=cb: e.dma_start(out=w[:], in_=wB[i * KC + cb]), writes=[Bw])
                            P.dma("sync", lambda e, gt=gt, i=i, cb=cb, t0=t0, n=n: e.dma_start(out=gt[:, :n], in_=gatesT[(i * KC + cb) * 128:(i * KC + cb + 1) * 128, t0:t0 + n]), writes=[Bgt])
                            p, Bp = pm.next()
                            for kc in range(8):
                                P.op("tensor", lambda e, p=p, w=w, i=i, kc=kc, n=n: e.matmul(p[:, :n], lhsT=w[:, kc * 128:(kc + 1) * 128], rhs=yT[:, i * 8 + kc, :n], start=(kc == 0), stop=(kc == 7)),
                                     reads=[Bw, ByT], writes=[Bp])
                            z, Bz = zf.next()
                            P.op("vector", lambda e, z=z, p=p, gt=gt, n=n: e.tensor_tensor(out=z[:, :n], in0=p[:, :n], in1=gt[:, :n], op=ALU.mult), reads=[Bp, Bgt], writes=[Bz])
                            zs.append((z, Bz))
                        P.op("gpsimd", lambda e, zs=zs, n=n: e.tensor_tensor(out=zs[0][0][:, :n], in0=zs[0][0][:, :n], in1=zs[1][0][:, :n], op=ALU.add), reads=[zs[0][1], zs[1][1]], writes=[zs[0][1]])
                        P.op("gpsimd", lambda e, zs=zs, n=n, cb=cb: e.tensor_tensor(out=zT[:, cb, :n], in0=zs[0][0][:, :n], in1=zs[2][0][:, :n], op=ALU.add), reads=[zs[0][1], zs[2][1]], writes=[BzT])
                    for cb in range(KC):
                        w, Bw = WO.next()
                        x, Bx = xs.next()
                        P.dma("sync", lambda e, w=w, cb=cb: e.dma_start(out=w[:], in_=wO[cb]), writes=[Bw])
                        P.dma("sync", lambda e, x=x, cb=cb, t0=t0, n=n: e.dma_start(out=x[:, :n], in_=src[cb][:, t0:t0 + n]), writes=[Bx])
                        p, Bp = pm.next()
                        for kc in range(KC):
                            P.op("tensor", lambda e, p=p, w=w, kc=kc, n=n: e.matmul(p[:, :n], lhsT=w[:, kc * 128:(kc + 1) * 128], rhs=zT[:, kc, :n], start=(kc == 0), stop=(kc == KC - 1)),
                                 reads=[Bw, BzT], writes=[Bp])
                        o, Bo = xo.next()
                        P.op("vector", lambda e, o=o, p=p, x=x, cb=cb, n=n, md=md: e.scalar_tensor_tensor(out=o[:, :n], in0=p[:, :n], scalar=md[:, 2, cb:cb + 1], in1=x[:, :n], op0=ALU.mult, op1=ALU.add),
                             reads=[Bp, Bx, Bmod], writes=[Bo])
                        P.dma(STQ, lambda e, o=o, cb=cb, t0=t0, n=n: e.dma_start(out=dst[cb][:, t0:t0 + n], in_=o[:, :n]), reads=[Bo])
                P.emit()

        def phase_ffn_up(l, with_ctx, src):
            wU = G[l]["wup"][2].rearrange("(n p) f -> n p f", p=128)
            with ExitStack() as st:
                hT = sb(st, [128, KC, 512], BF16)
                Bh = Buf()
                xch = Rot([sb(st, [128, 512], F32) for _ in range(3)])
                sq = Rot([sb(st, [128, 512], BF16) for _ in range(2)])
                tmp = Rot([sb(st, [128, 512], F32) for _ in range(2)])
                rstd = sb(st, [128, 512], F32)
                cw = sb(st, [128, 3, FC], F32)
                cb_ = sb(st, [128, FC], F32)
                P.dma("sync", lambda e: e.dma_start(out=cw[:], in_=W[l]["convw"][:, :, :]), writes=[Bc])
                P.dma("sync", lambda e: e.dma_start(out=cb_[:], in_=W[l]["convb"][:, :]), writes=[Bc])
                WU = Rot([sb(st, [128, KC * 128], BF16) for _ in range(4)])
                gs = Rot([sb(st, [128, 512], F32) for _ in range(2)])
                sg = Rot([sb(st, [128, 512], F32) for _ in range(2)])
                ao = Rot([sb(st, [128, 512], BF16) for _ in range(3)])
                pm = Rot([pst(st, [128, 512]) for _ in range(6)])
                pn = pst(st, [128, 512])
                Bpn = Buf()
                seqs = ([(0, CTX, True)] if with_ctx else []) + [(CTX, SEQ, False)]
                for (s0, slen, is_ctx) in seqs:
                    ci = 3 if is_ctx else 2
                    md = modC[l] if is_ctx else modL[l]
                    for o0 in range(0, slen, 510):
                        no = min(510, slen - o0)
                        lo = max(o0 - 1, 0)
                        hi = min(o0 + no + 1, slen)
                        c0 = lo - (o0 - 1)
                        nin = no + 2
                        norm_prologue(st, src, [(c0, s0 + lo, hi - lo)], hT, Bh, gef[l][:, ci, :], md[:, 3, :], pn, Bpn, xch, sq, rstd, tmp)
                        if c0 == 1:
                            P.op("gpsimd", lambda e: e.memset(hT[:, :, 0:1], 0.0), writes=[Bh])
                        if hi - lo + c0 < nin:
                            P.op("gpsimd", lambda e, nin=nin: e.memset(hT[:, :, nin - 1:nin], 0.0), writes=[Bh])
                        for fc in range(FC):
                            wu, Bwu = WU.next()
                            wg, Bwg = WU.next()
                            P.dma("sync", lambda e, wu=wu, fc=fc: e.dma_start(out=wu[:], in_=wU[fc]), writes=[Bwu])
                            P.dma("sync", lambda e, wg=wg, fc=fc: e.dma_start(out=wg[:], in_=wU[FC + fc]), writes=[Bwg])
                            pu, Bpu = pm.next()
                            pg_, Bpg_ = pm.next()
                            for (p, Bp, w, Bw) in ((pu, Bpu, wu, Bwu), (pg_, Bpg_, wg, Bwg)):
                                for kc in range(KC):
                                    P.op("tensor", lambda e, p=p, w=w, kc=kc, nin=nin: e.matmul(p[:, :nin], lhsT=w[:, kc * 128:(kc + 1) * 128], rhs=hT[:, kc, :nin], start=(kc == 0), stop=(kc == KC - 1)),
                                         reads=[Bw, Bh], writes=[Bp])
                            g1_, Bg1 = gs.next()
                            P.op("scalar", lambda e, g1_=g1_, pg_=pg_, fc=fc, no=no: e.activation(out=g1_[:, :no], in_=pg_[:, 1:no + 1], func=AF.Identity, scale=cw[:, 1, fc:fc + 1], bias=cb_[:, fc:fc + 1]),
                                 reads=[Bpg_, Bc], writes=[Bg1])
                            P.op("vector", lambda e, g1_=g1_, pg_=pg_, fc=fc, no=no: e.scalar_tensor_tensor(out=g1_[:, :no], in0=pg_[:, 0:no], scalar=cw[:, 0, fc:fc + 1], in1=g1_[:, :no], op0=ALU.mult, op1=ALU.add),
                                 reads=[Bpg_, Bg1, Bc], writes=[Bg1])
                            P.op("vector", lambda e, g1_=g1_, pg_=pg_, fc=fc, no=no: e.scalar_tensor_tensor(out=g1_[:, :no], in0=pg_[:, 2:no + 2], scalar=cw[:, 2, fc:fc + 1], in1=g1_[:, :no], op0=ALU.mult, op1=ALU.add),
                                 reads=[Bpg_, Bg1, Bc], writes=[Bg1, Bpg_])
                            s_, Bs_ = sg.next()
                            P.op("scalar", lambda e, s_=s_, g1_=g1_, no=no: e.activation(out=s_[:, :no], in_=g1_[:, :no], func=AF.Silu), reads=[Bg1], writes=[Bs_])
                            a, Ba = ao.next()
                            P.op("vector", lambda e, a=a, s_=s_, pu=pu, no=no: e.tensor_tensor(out=a[:, :no], in0=pu[:, 1:no + 1], in1=s_[:, :no], op=ALU.mult), reads=[Bs_, Bpu], writes=[Ba, Bpu])
                            P.dma(STQ, lambda e, a=a, fc=fc, s0=s0, o0=o0, no=no: e.dma_start(out=actT[fc * 128:(fc + 1) * 128, s0 + o0:s0 + o0 + no], in_=a[:, :no]), reads=[Ba])
                P.emit()

        def phase_ffn_dn(l, with_ctx, src, dst):
            wD = G[l]["wdn"][2].rearrange("(n p) f -> n p f", p=128)
            a3 = actT.rearrange("(c p) t -> p c t", p=128)
            with ExitStack() as st:
                aT = sb(st, [128, FC, 512], BF16)
                Ba = Buf()
                WD = Rot([sb(st, [128, FC * 128], BF16) for _ in range(2)])
                xs = Rot([sb(st, [128, 512], F32) for _ in range(2)])
                xo = Rot([sb(st, [128, 512], F32) for _ in range(2)])
                pm = Rot([pst(st, [128, 512]) for _ in range(4)])
                for (t0, n, is_ctx) in token_groups(with_ctx):
                    md = modC[l] if is_ctx else modL[l]
                    step = 16
                    for f0 in range(0, FC, step):
                        f1 = min(FC, f0 + step)
                        P.dma("sync", lambda e, f0=f0, f1=f1, t0=t0, n=n: e.dma_start(out=aT[:, f0:f1, :n], in_=a3[:, f0:f1, t0:t0 + n]), writes=[Ba])
                    for cb in range(KC):
                        w, Bw = WD.next()
                        x, Bx = xs.next()
                        P.dma("sync", lambda e, w=w, cb=cb: e.dma_start(out=w[:], in_=wD[cb]), writes=[Bw])
                        P.dma("sync", lambda e, x=x, cb=cb, t0=t0, n=n: e.dma_start(out=x[:, :n], in_=src[cb][:, t0:t0 + n]), writes=[Bx])
                        p, Bp = pm.next()
                        for kc in range(FC):
                            P.op("tensor", lambda e, p=p, w=w, kc=kc, n=n: e.matmul(p[:, :n], lhsT=w[:, kc * 128:(kc + 1) * 128], rhs=aT[:, kc, :n], start=(kc == 0), stop=(kc == FC - 1)),
                                 reads=[Bw, Ba], writes=[Bp])
                        o, Bo = xo.next()
                        P.op("vector", lambda e, o=o, p=p, x=x, cb=cb, n=n, md=md: e.scalar_tensor_tensor(out=o[:, :n], in0=p[:, :n], scalar=md[:, 5, cb:cb + 1], in1=x[:, :n], op0=ALU.mult, op1=ALU.add),
                             reads=[Bp, Bx, Bmod], writes=[Bo])
                        P.dma(STQ, lambda e, o=o, cb=cb, t0=t0, n=n: e.dma_start(out=dst[cb][:, t0:t0 + n], in_=o[:, :n]), reads=[Bo])
                P.emit()

        def phase_final(src):
            with ExitStack() as st:
                xch = Rot([sb(st, [128, 512], F32) for _ in range(3)])
                sq = Rot([sb(st, [128, 512], BF16) for _ in range(2)])
                rstd = sb(st, [128, 512], F32)
                fg = sb(st, [128, KC], F32)
                os_ = Rot([sb(st, [128, 512], F32) for _ in range(3)])
                pn = pst(st, [128, 512])
                Bpn, Brs = Buf(), Buf()
                P.op("vector", lambda e: e.tensor_scalar_mul(out=fg[:], in0=fngt[:], scalar1=SQD), reads=[Bc], writes=[Bc])
                import os as _o3
                _skip = _o3.environ.get("K_SKIP", "") != ""
                for t0 in range(CTX, T, 512):
                    n = min(512, T - t0)
                    for kc in range(0 if not _skip else KC, KC):
                        x, Bx = xch.next()
                        s, Bs_ = sq.next()
                        P.dma("sync", lambda e, x=x, kc=kc, t0=t0, n=n: e.dma_start(out=x[:, :n], in_=src[kc][:, t0:t0 + n]), writes=[Bx])
                        P.op("scalar", lambda e, x=x, s=s, n=n: e.activation(out=s[:, :n], in_=x[:, :n], func=AF.Square), reads=[Bx], writes=[Bs_])
                        P.op("tensor", lambda e, s=s, n=n, kc=kc: e.matmul(pn[:, :n], lhsT=cbf[:, ONE], rhs=s[:, :n], start=(kc == 0), stop=(kc == KC - 1)), reads=[Bs_, Bc], writes=[Bpn])
                    if not _skip:
                        P.op("scalar", lambda e, n=n: e.activation(out=rstd[:, :n], in_=pn[:, :n], func=AF.Ln, bias=epsD[:, 0:1]), reads=[Bpn, Bc], writes=[Brs])
                        P.op("scalar", lambda e, n=n: e.activation(out=rstd[:, :n], in_=rstd[:, :n], func=AF.Exp, scale=-0.5), reads=[Brs], writes=[Brs])
                    for kc in range(KC):
                        x, Bx = xch.next()
                        o, Bo = os_.next()
                        P.dma("sync", lambda e, x=x, kc=kc, t0=t0, n=n: e.dma_start(out=x[:, :n], in_=src[kc][:, t0:t0 + n]), writes=[Bx])
                        import os as _o2
                        _dbg = _o2.environ.get("K_DBG", "")
                        if _dbg == "rstd":
                            P.op("vector", lambda e, o=o, n=n: e.tensor_copy(out=o[:, :n], in_=rstd[:, :n]), reads=[Bx, Brs, Bc], writes=[Bo])
                        elif _dbg == "pn":
                            P.op("vector", lambda e, o=o, n=n: e.tensor_copy(out=o[:, :n], in_=pn[:, :n]), reads=[Bx, Brs, Bc, Bpn], writes=[Bo])
                        elif _dbg == "x":
                            P.op("vector", lambda e, o=o, x=x, n=n: e.tensor_copy(out=o[:, :n], in_=x[:, :n]), reads=[Bx, Brs, Bc], writes=[Bo])
                        elif _dbg == "fg":
                            P.op("vector", lambda e, o=o, x=x, n=n, kc=kc: e.tensor_scalar(out=o[:, :n], in0=x[:, :n], scalar1=fg[:, kc:kc + 1], scalar2=None, op0=ALU.mult), reads=[Bx, Brs, Bc], writes=[Bo])
                        else:
                            P.op("vector", lambda e, x=x, o=o, kc=kc, n=n: e.scalar_tensor_tensor(out=o[:, :n], in0=x[:, :n], scalar=fg[:, kc:kc + 1], in1=rstd[:, :n], op0=ALU.mult, op1=ALU.mult),
                                 reads=[Bx, Brs, Bc], writes=[Bo])
                        P.dma(STQ, lambda e, o=o, kc=kc, t0=t0, n=n: e.dma_start(out=out[kc][:, t0 - CTX:t0 - CTX + n], in_=o[:, :n]), reads=[Bo])
                P.emit()

        import os as _os
        _stop = int(_os.environ.get("K_STOP", "999"))
        _cnt = [0]

        def _go(fn, *a):
            _cnt[0] += 1
            if _cnt[0] <= _stop:
                fn(*a)
        if _os.environ.get("K_ONEBLK", "") == "":
            P.emit()
        _go(phase_weights)
        cur = 0
        for l in range(C.NL):
            last = (l == C.NL - 1)
            wc = not last
            _go(phase_ada, l)
            _go(phase_p1, l, xT[cur])
            _go(phase_mstate, l, 0)
            _go(phase_mstate, l, 1)
            _go(phase_mout, l, wc)
            _go(phase_win, l, wc)
            _go(phase_diff, l, wc)
            mid = 1 if cur != 1 else 2
            _go(phase_merge, l, wc, xT[cur], xT[mid])
            _go(phase_ffn_up, l, wc, xT[mid])
            nxt = 2 if mid == 1 else 1
            _go(phase_ffn_dn, l, wc, xT[mid], xT[nxt])
            if _cnt[0] <= _stop:
                cur = nxt
        phase_final(xT[cur])
    return nc


IN_OFF = dict(mq=0, mk=512, mv=1024, mo=2048, mg=3072, wq=3104, wk=4128, wv=4384, fq=4640, fk=5664, fv=6688, g=7712)


def _tile_F(Wm, col_blocks, n_pad):
    K = Wm.shape[0]
    Kc = K // 128
    outt = np.zeros((n_pad, 128, Kc * 128), np.float32)
    for i, cols in enumerate(col_blocks):
        cols = np.asarray(cols)
        blk = np.zeros((K, 128), np.float32)
        ok = cols >= 0
        blk[:, ok] = Wm[:, cols[ok]]
        outt[i] = blk.reshape(Kc, 128, 128).transpose(1, 0, 2).reshape(128, Kc * 128)
    return outt


def _tile_T(Wm, col_blocks):
    K = Wm.shape[0]
    Kc = K // 128
    outt = np.zeros((len(col_blocks), 128, Kc * 512), np.float32)
    for i, cols in enumerate(col_blocks):
        cols = np.asarray(cols)
        blk = np.zeros((K, 512), np.float32)
        ok = cols >= 0
        blk[:, ok] = Wm[:, cols[ok]]
        outt[i] = blk.reshape(Kc, 128, 512).transpose(1, 0, 2).reshape(128, Kc * 512)
    return outt


def _fm(v):
    return np.ascontiguousarray(v.reshape(-1, 128).T.astype(np.float32))


def _rope_tables(C, hd, units):
    nf = hd // 4
    inv = (10000.0 ** (-np.arange(nf, dtype=np.float32) / nf)).astype(np.float32)
    pos = np.arange(C.SEQ)
    rows = (pos // 64).astype(np.float32)
    cols = (pos % 64).astype(np.float32)
    ang = np.stack([rows[:, None] * inv, cols[:, None] * inv], axis=1)
    cos = np.cos(ang).astype(np.float32)
    sin = np.sin(ang).astype(np.float32)
    ct = np.ones((128, C.T), np.float32)
    stt = np.zeros((128, C.T), np.float32)
    for u in range(units):
        for i in range(2):
            for j in range(2):
                r0 = u * hd + i * (hd // 2) + j * nf
                ct[r0:r0 + nf, C.CTX:] = cos[:, i, :].T
                stt[r0:r0 + nf, C.CTX:] = sin[:, i, :].T
    return ct, stt


def _rot_matrix(hd, units):
    nf = hd // 4
    R = np.zeros((128, 128), np.float32)
    for u in range(units):
        for i in range(2):
            for f in range(nf):
                a = u * hd + i * (hd // 2) + f
                b = a + nf
                R[b, a] = -1.0
                R[a, b] = 1.0
    return R


def prepare_inputs(C, inp):
    D, KC, FC = C.D, C.KC, C.FC
    f32 = np.float32
    consts = np.zeros((128, 8 * 128), f32)
    consts[:, 0:128] = np.eye(128, dtype=f32)
    j = np.arange(64)[:, None]
    l_ = np.arange(64)[None, :]
    consts[0:64, 128:192] = (j <= l_)
    consts[0:64, 256:320] = (j >= l_)
    kj = np.arange(128)[:, None]
    qi = np.arange(128)[None, :]
    consts[:, 384:512] = (kj >= qi)
    consts[:, 512:640] = (kj <= qi)
    consts[:, 640:768] = 1.0
    consts[:, 768:896] = _rot_matrix(128, 1)
    consts[:, 896:1024] = _rot_matrix(64, 2)
    cw, sw = _rope_tables(C, 128, 1)
    cd, sd = _rope_tables(C, 64, 2)
    ropes = np.stack([cw, sw, cd, sd], 0)
    cT = np.concatenate([np.asarray(inp["c"], f32), np.asarray(inp["c_ctx"], f32)[None]], 0)
    cT = np.ascontiguousarray(cT.reshape(5, KC, 128).transpose(2, 1, 0))
    common = dict(consts=consts, ropes=ropes, cT=cT, fng=_fm(np.asarray(inp["final_norm_g"])))
    per_layer = []
    for l in range(C.NL):
        w_in = np.asarray(inp["w_in"][l], f32)
        ar = np.arange(128)
        Fb = []
        for nm, cnt in (("mq", 4), ("mk", 4), ("wq", 8), ("wk", 2), ("fq", 8), ("fk", 8)):
            for i in range(cnt):
                Fb.append(IN_OFF[nm] + i * 128 + ar)
        Fb.append(np.where(ar < 32, IN_OFF["mg"] + ar, -1))
        for i in range(3 * KC):
            Fb.append(IN_OFF["g"] + i * 128 + ar)
        a5 = np.arange(512)
        Tb = [IN_OFF["mk"] + a5, IN_OFF["mv"] + a5, IN_OFF["mv"] + 512 + a5, IN_OFF["mo"] + a5, IN_OFF["mo"] + 512 + a5,
              np.where(a5 < 256, IN_OFF["wv"] + a5, np.where(a5 < 288, IN_OFF["mg"] + a5 - 256, -1)),
              IN_OFF["fv"] + a5, IN_OFF["fv"] + 512 + a5]
        wbr = np.asarray(inp["w_branch"][l], f32)
        d = dict(
            winF=_tile_F(w_in, Fb, C.nF), winT=_tile_T(w_in, Tb),
            wbr=_tile_F(wbr.reshape(3 * 1024, D)[:1024] * 0, [], C.nB),
            wout=_tile_F(np.asarray(inp["w_out"][l], f32), [i * 128 + ar for i in range(KC)], C.nO),
            wup=_tile_F(np.asarray(inp["ffn_w_up"][l], f32), [i * 128 + ar for i in range(2 * FC)], C.nU),
            wdn=_tile_F(np.asarray(inp["ffn_w_down"][l], f32), [i * 128 + ar for i in range(KC)], C.nO),
            ada=_tile_F(np.asarray(inp["ada_w"][l], f32), [i * 128 + ar for i in range(6 * KC)], C.nA),
        )
        wb = np.zeros((C.nB, 128, 1024), f32)
        for i in range(3):
            wb[i * KC:(i + 1) * KC] = _tile_F(wbr[i], [c * 128 + ar for c in range(KC)], KC)
        d["wbr"] = wb
        adab = np.zeros((C.nA * 128,), f32)
        adab[:6 * D] = np.asarray(inp["ada_b"][l], f32)
        d["adab_full"] = adab.reshape(C.nA, 128)
        gb = np.concatenate([np.asarray(inp["mlstm_i_bias"][l], f32).reshape(16), np.asarray(inp["mlstm_f_bias"][l], f32).reshape(16)])
        d["small"] = dict(
            n1g=_fm(np.asarray(inp["norm1_g"][l])), n2g=_fm(np.asarray(inp["norm2_g"][l])),
            convw=np.ascontiguousarray(np.asarray(inp["ffn_conv_w"][l], f32).reshape(3, FC, 128).transpose(2, 0, 1)),
            convb=_fm(np.asarray(inp["ffn_conv_b"][l])),
            gbias=np.ascontiguousarray(np.broadcast_to(gb[None], (128, 32))),
            mng=np.ascontiguousarray(np.broadcast_to(np.asarray(inp["mlstm_norm_g"][l], f32)[None], (128, 1024))),
            sink=np.ascontiguousarray(np.broadcast_to(np.asarray(inp["swa_sink"][l], f32)[None], (128, 8))),
            lam=np.ascontiguousarray(np.asarray(inp["diff_lambda"][l], f32).transpose(2, 0, 1)),
            dng=np.ascontiguousarray(np.broadcast_to(np.asarray(inp["diff_norm_g"][l], f32)[None], (128, 128))),
        )
        per_layer.append(d)
    in_maps = []
    for core in range(8):
        b = core // 2
        xi = np.concatenate([np.asarray(inp["ctx"][b], f32), np.asarray(inp["x"][b], f32)], 0)
        m = dict(common)
        m["xin"] = np.ascontiguousarray(xi.T.reshape(KC, 128, C.T))
        ohm = np.zeros((128, 4), f32)
        ohm[:, b] = 1.0
        m["onehot"] = ohm
        for l in range(C.NL):
            d = per_layer[l]
            for nm in ("winF", "winT", "wbr", "wout", "wup", "wdn"):
                full = d[nm]
                n = full.shape[0] // 8
                L = full.shape[2]
                PR = piece_rows(n * 128, L)
                m[f"{nm}{l}"] = np.ascontiguousarray(full.reshape(-1, 8, PR, L)[:, core]).reshape(n, 128, L)
            n = d["ada"].shape[0] // 8
            m[f"ada{l}"] = np.ascontiguousarray(d["ada"][core * n:(core + 1) * n])
            n = C.nA // 8
            m[f"adab{l}"] = np.ascontiguousarray(d["adab_full"][core * n:(core + 1) * n].T)
            for k, v in d["small"].items():
                m[f"{k}{l}"] = v
        in_maps.append(m)
    return in_maps


_CACHE = {}


def run(C, inp):
    key = (C.D, C.SEQ, C.CTX, C.DFF, C.NL)
    if key not in _CACHE:
        _CACHE[key] = build_program(C)
    nc = _CACHE[key]
    in_maps = prepare_inputs(C, inp)
    res = run_bass_kernel_spmd(nc, in_maps, core_ids=list(range(8)))
    outs = []
    for b in range(4):
        o = res.results[2 * b]["out"]
        outs.append(np.ascontiguousarray(o.reshape(C.D, C.SEQ).T))
    return np.stack(outs, 0).astype(np.float32)


def kernel(**inputs):
    return run(Cfg(), inputs)
```
